# Optimizing a Trainium2 kernel written in Bass

```python
import math
import jax, jax.numpy as jnp
from jax import lax
import numpy as np

D_MODEL = 1024
BATCH = 8
SEQ = 4096
DEPTH = 1

GRID_W = 64
CTX_LEN = 256
NA_HEADS = 8
NA_HEAD_DIM = 64
NA_WIN_ROWS = 8
NA_WIN_COLS = 16
GDN_HEADS = 4
GDN_HEAD_DIM = 128
GDN_CONV = 5
GDN_CHUNK = 64
D_FF = 2816
N_MOD = 9
ROPE_THETA = 10000.0
LN_EPS = 1e-6
NORM_EPS = 1e-6
DEEPNORM_ALPHA = (2 * DEPTH) ** 0.25
DEEPNORM_BETA = (8 * DEPTH) ** -0.25
NA_W = NA_HEADS * NA_HEAD_DIM
GDN_W = GDN_HEADS * GDN_HEAD_DIM
IN_SIZES = (NA_W, NA_W, NA_W, 3 * GDN_W, GDN_W, 2 * GDN_HEADS, 2 * GDN_HEADS, D_MODEL, D_MODEL)

kernel_name = 'hybrid_natten_gdn_macaron_dit_layer'


def _ln(h):
    hf = h.astype(jnp.float32)
    mu = jnp.mean(hf, axis=-1, keepdims=True)
    var = jnp.mean(jnp.square(hf - mu), axis=-1, keepdims=True)
    return (hf - mu) * lax.rsqrt(var + LN_EPS)


def post_ln(h, g, b):
    return (_ln(h) * g + b).astype(h.dtype)


def modulate(h, shift, scale):
    return (_ln(h) * (1.0 + scale) + shift).astype(h.dtype)


def swiglu(u, w_up, w_down):
    a, b = jnp.split(u @ w_up, 2, axis=-1)
    return (jax.nn.silu(a) * b) @ w_down


def to_heads(t, n_heads):
    b, n, w = t.shape
    return t.reshape(b, n, n_heads, w // n_heads).transpose(0, 2, 1, 3)


def from_heads(t):
    b, h, n, d = t.shape
    return t.transpose(0, 2, 1, 3).reshape(b, n, h * d)


def split_proj(p):
    return jnp.split(p, np.cumsum(IN_SIZES)[:-1].tolist(), axis=-1)


def l2norm(t):
    return t * lax.rsqrt(jnp.sum(jnp.square(t), axis=-1, keepdims=True) + NORM_EPS)


def axial_rope(n_tok, head_dim):
    n_freq = head_dim // 4
    freqs = ROPE_THETA ** (-jnp.arange(n_freq, dtype=jnp.float32) / n_freq)
    t = jnp.arange(n_tok)
    pos = jnp.stack([t // GRID_W, t % GRID_W], axis=-1).astype(jnp.float32)
    ang = pos[:, :, None] * freqs
    return jnp.cos(ang), jnp.sin(ang)


def apply_rope(x, cos, sin):
    shp = x.shape
    x = x.reshape(shp[:-1] + (2, 2, shp[-1] // 4))
    x1, x2 = x[..., 0, :], x[..., 1, :]
    return jnp.stack([x1 * cos - x2 * sin, x2 * cos + x1 * sin], axis=-2).reshape(shp)


def short_conv(h, w):
    ch = h.shape[-1]
    return lax.conv_general_dilated(h, w[:, None, :].astype(h.dtype), window_strides=(1,),
                                    padding=[(GDN_CONV // 2, GDN_CONV // 2)],
                                    dimension_numbers=('NWC', 'WIO', 'NWC'), feature_group_count=ch)


def na_latent(q, k, v, k_ctx, v_ctx, rpb):
    bsz, n_heads, rows, width, hd = q.shape
    kr = min(NA_WIN_ROWS, rows)
    n_loc = kr * NA_WIN_COLS
    col = np.arange(width)
    col_idx = np.clip(col - NA_WIN_COLS // 2, 0, width - NA_WIN_COLS)[:, None] + np.arange(NA_WIN_COLS)[None, :]
    bias_cols = rpb[:, :, col_idx - col[:, None] + NA_WIN_COLS - 1]
    scale = hd ** -0.5

    def one_row(r):
        r0 = jnp.clip(r - kr // 2, 0, rows - kr)
        k_win = lax.dynamic_slice_in_dim(k, r0, kr, axis=2)[:, :, :, col_idx]
        v_win = lax.dynamic_slice_in_dim(v, r0, kr, axis=2)[:, :, :, col_idx]
        q_r = lax.dynamic_index_in_dim(q, r, axis=2, keepdims=False) * scale
        bias = jnp.take(bias_cols, r0 + jnp.arange(kr) - r + NA_WIN_ROWS - 1, axis=1)
        s_loc = jnp.einsum('bhwd,bhiwjd->bhwij', q_r, k_win) + bias.transpose(0, 2, 1, 3)
        s_ctx = jnp.einsum('bhwd,bhnd->bhwn', q_r, k_ctx)
        s = jnp.concatenate([s_loc.reshape(bsz, n_heads, width, n_loc), s_ctx], axis=-1).astype(jnp.float32)
        p = jax.nn.softmax(s, axis=-1).astype(v.dtype)
        p_loc = p[..., :n_loc].reshape(bsz, n_heads, width, kr, NA_WIN_COLS)
        return (jnp.einsum('bhwij,bhiwjd->bhwd', p_loc, v_win)
                + jnp.einsum('bhwn,bhnd->bhwd', p[..., n_loc:], v_ctx))

    o = lax.map(one_row, jnp.arange(rows))
    return o.transpose(1, 0, 3, 2, 4).reshape(bsz, rows * width, n_heads * hd)


def na_context(q, k, v):
    s = jnp.einsum('bhnd,bhmd->bhnm', q * q.shape[-1] ** -0.5, k).astype(jnp.float32)
    return from_heads(jnp.einsum('bhnm,bhmd->bhnd', jax.nn.softmax(s, axis=-1).astype(v.dtype), v))


def gdn_prepare(qkv, a, b, conv_w, a_log, dt_bias, rope):
    bsz, n, _ = qkv.shape
    qkv = jax.nn.silu(short_conv(qkv, conv_w)).astype(jnp.float32)
    q, k, v = qkv.reshape(bsz, n, 3, GDN_HEADS, GDN_HEAD_DIM).transpose(2, 0, 3, 1, 4)
    q, k = l2norm(q), l2norm(k)
    if rope is not None:
        q, k = apply_rope(q, *rope), apply_rope(k, *rope)
    q = q * GDN_HEAD_DIM ** -0.5
    a = a.astype(jnp.float32).reshape(bsz, n, 2, GDN_HEADS)
    b = b.astype(jnp.float32).reshape(bsz, n, 2, GDN_HEADS)
    g = (-jnp.exp(a_log.astype(jnp.float32)) * jax.nn.softplus(a + dt_bias.astype(jnp.float32))).transpose(2, 0, 3, 1)
    beta = jax.nn.sigmoid(b).transpose(2, 0, 3, 1)

    def both(t):
        return jnp.stack([t, jnp.flip(t, axis=2)])

    return (both(q), both(k), both(v),
            jnp.stack([g[0], jnp.flip(g[1], axis=-1)]),
            jnp.stack([beta[0], jnp.flip(beta[1], axis=-1)]))


def gated_delta_chunks(q, k, v, g, beta, s0, with_output):
    nd = k.ndim - 2
    n_tok, dv = v.shape[-2], v.shape[-1]
    n_chunk = n_tok // GDN_CHUNK

    def chunk(t):
        return t.reshape(t.shape[:nd] + (n_chunk, GDN_CHUNK) + t.shape[nd + 1:])

    k, v, g, beta = chunk(k), chunk(v), jnp.cumsum(chunk(g), axis=-1), chunk(beta)
    lower = np.tril(np.ones((GDN_CHUNK, GDN_CHUNK), dtype=bool))
    strict = np.tril(np.ones((GDN_CHUNK, GDN_CHUNK), dtype=bool), -1)
    decay = jnp.exp(jnp.where(lower, g[..., :, None] - g[..., None, :], -jnp.inf))
    k_beta = k * beta[..., None]
    m = jnp.where(strict, jnp.einsum('...ik,...jk->...ij', k_beta, k) * decay, 0.0)
    rhs = jnp.concatenate([v * beta[..., None], k_beta * jnp.exp(g)[..., None]], axis=-1)
    uw = lax.linalg.triangular_solve(jnp.eye(GDN_CHUNK, dtype=m.dtype) + m, rhs,
                                     left_side=True, lower=True, unit_diagonal=True)
    u, w = uw[..., :dv], uw[..., dv:]
    k_tail = k * jnp.exp(g[..., -1:] - g)[..., None]
    g_tot = jnp.exp(g[..., -1])
    xs = [u, w, k_tail, g_tot]
    if with_output:
        q = chunk(q)
        xs += [q * jnp.exp(g)[..., None], jnp.einsum('...ik,...jk->...ij', q, k) * decay]
    xs = [jnp.moveaxis(t, nd, 0) for t in xs]

    def step(s, xc):
        v_new = xc[0] - jnp.einsum('...ck,...kv->...cv', xc[1], s)
        s_next = s * xc[3][..., None, None] + jnp.einsum('...ck,...cv->...kv', xc[2], v_new)
        if not with_output:
            return s_next, None
        return s_next, (jnp.einsum('...ck,...kv->...cv', xc[4], s)
                        + jnp.einsum('...cj,...jv->...cv', xc[5], v_new))

    s_fin, o = lax.scan(step, s0, xs)
    if not with_output:
        return None, s_fin
    return jnp.moveaxis(o, 0, nd).reshape(q.shape[:nd] + (n_tok, dv)), s_fin


def gdn_output(o, z, norm_w):
    bsz, n_heads, n_tok, dv = o.shape
    o = o.transpose(0, 2, 1, 3)
    o = o * lax.rsqrt(jnp.mean(jnp.square(o), axis=-1, keepdims=True) + NORM_EPS) * norm_w.astype(jnp.float32)
    o = o * jax.nn.silu(z.astype(jnp.float32)).reshape(bsz, n_tok, n_heads, dv)
    return o.reshape(bsz, n_tok, n_heads * dv).astype(z.dtype)


def branch_merge(o_a, o_b, gate_a, gate_b, w_pa, w_pb, w_o):
    y = jax.nn.sigmoid(gate_a) * (o_a @ w_pa) + jax.nn.sigmoid(gate_b) * (o_b @ w_pb)
    return y @ w_o


def hybrid_layer(x, ctx, c, c_ctx, w_ada, b_ada, ln_g, ln_b, ffn1_w_in, ffn1_w_out, w_in, na_rpb,
                 gdn_conv_w, gdn_a_log, gdn_dt_bias, gdn_norm_w, w_pa, w_pb, w_o, ffn2_w_in, ffn2_w_out,
                 rope, update_ctx):
    bsz, n_lat, d = x.shape
    rows = n_lat // GRID_W
    mod_x = (jax.nn.silu(c) @ w_ada + b_ada).reshape(bsz, N_MOD, 1, d)
    mod_c = (jax.nn.silu(c_ctx) @ w_ada + b_ada).reshape(N_MOD, 1, d)

    def sub_mod(m, i):
        return m[..., 3 * i, :, :], m[..., 3 * i + 1, :, :], m[..., 3 * i + 2, :, :]

    def ffn_sublayer(h, m, i, w_up, w_down):
        shift, scale, gate = sub_mod(m, i)
        y = swiglu(modulate(h, shift, scale), w_up, w_down)
        return post_ln(DEEPNORM_ALPHA * h + 0.5 * gate * y, ln_g[i], ln_b[i])

    x = ffn_sublayer(x, mod_x, 0, ffn1_w_in, ffn1_w_out)
    ctx = ffn_sublayer(ctx, mod_c, 0, ffn1_w_in, ffn1_w_out)

    shift_x, scale_x, gate_x = sub_mod(mod_x, 1)
    shift_c, scale_c, gate_c = sub_mod(mod_c, 1)
    qa, ka, va, qkv_b, z_b, a_b, b_b, ga, gb = split_proj(modulate(x, shift_x, scale_x) @ w_in)
    qa_c, ka_c, va_c, qkv_c, z_c, a_c, b_c, ga_c, gb_c = split_proj(modulate(ctx, shift_c, scale_c) @ w_in)

    def grid(t):
        return t.reshape(bsz, rows, GRID_W, NA_HEADS, NA_HEAD_DIM).transpose(0, 3, 1, 2, 4)

    kc_h, vc_h = to_heads(ka_c, NA_HEADS), to_heads(va_c, NA_HEADS)
    o_a = na_latent(grid(qa), grid(ka), grid(va), kc_h, vc_h, na_rpb)

    s0 = jnp.zeros((2, bsz, GDN_HEADS, GDN_HEAD_DIM, GDN_HEAD_DIM), jnp.float32)
    o_ctx2, s_ctx = gated_delta_chunks(*gdn_prepare(qkv_c, a_c, b_c, gdn_conv_w, gdn_a_log, gdn_dt_bias, None),
                                       s0, update_ctx)
    o_lat2, _ = gated_delta_chunks(*gdn_prepare(qkv_b, a_b, b_b, gdn_conv_w, gdn_a_log, gdn_dt_bias, rope),
                                   s_ctx, True)
    o_b = gdn_output(o_lat2[0] + jnp.flip(o_lat2[1], axis=2), z_b, gdn_norm_w)

    x = post_ln(DEEPNORM_ALPHA * x + gate_x * branch_merge(o_a, o_b, ga, gb, w_pa, w_pb, w_o), ln_g[1], ln_b[1])
    if update_ctx:
        o_a_c = na_context(to_heads(qa_c, NA_HEADS), kc_h, vc_h)
        o_b_c = gdn_output(o_ctx2[0] + jnp.flip(o_ctx2[1], axis=2), z_c, gdn_norm_w)
        ctx = post_ln(DEEPNORM_ALPHA * ctx + gate_c * branch_merge(o_a_c, o_b_c, ga_c, gb_c, w_pa, w_pb, w_o),
                      ln_g[1], ln_b[1])
        ctx = ffn_sublayer(ctx, mod_c, 2, ffn2_w_in, ffn2_w_out)
    else:
        ctx = None

    x = ffn_sublayer(x, mod_x, 2, ffn2_w_in, ffn2_w_out)
    return x, ctx


def setup_inputs(seed: int = 0) -> dict:
    key = jax.random.key(seed)
    ks = jax.random.split(key, 22)
    f32 = jnp.float32

    def nrm(k, shape, scale=1.0):
        return jax.random.normal(k, shape, f32) * scale

    n_in = sum(IN_SIZES)
    dt = jnp.exp(jax.random.uniform(ks[14], (DEPTH, 2, GDN_HEADS), f32, math.log(1e-3), math.log(1e-1)))
    return {
        'x': nrm(ks[0], (BATCH, SEQ, D_MODEL)),
        'c': nrm(ks[1], (BATCH, D_MODEL)),
        'ctx': nrm(ks[2], (BATCH, CTX_LEN, D_MODEL)),
        'c_ctx': nrm(ks[3], (D_MODEL,)),
        'w_ada': nrm(ks[4], (DEPTH, D_MODEL, N_MOD * D_MODEL), 0.5 * D_MODEL ** -0.5),
        'b_ada': nrm(ks[5], (DEPTH, N_MOD * D_MODEL), 0.02),
        'ln_g': 1.0 + nrm(ks[6], (DEPTH, 3, D_MODEL), 0.02),
        'ln_b': nrm(ks[7], (DEPTH, 3, D_MODEL), 0.02),
        'ffn1_w_in': nrm(ks[8], (DEPTH, D_MODEL, 2 * D_FF), D_MODEL ** -0.5),
        'ffn1_w_out': nrm(ks[9], (DEPTH, D_FF, D_MODEL), DEEPNORM_BETA * D_FF ** -0.5),
        'w_in': nrm(ks[10], (DEPTH, D_MODEL, n_in), D_MODEL ** -0.5),
        'na_rpb': nrm(ks[11], (DEPTH, NA_HEADS, 2 * NA_WIN_ROWS - 1, 2 * NA_WIN_COLS - 1), 0.1),
        'gdn_conv_w': nrm(ks[12], (DEPTH, GDN_CONV, 3 * GDN_W), GDN_CONV ** -0.5),
        'gdn_a_log': jnp.log(jax.random.uniform(ks[13], (DEPTH, 2, GDN_HEADS), f32, 1.0, 16.0)),
        'gdn_dt_bias': dt + jnp.log(-jnp.expm1(-dt)),
        'gdn_norm_w': 1.0 + nrm(ks[15], (DEPTH, GDN_HEAD_DIM), 0.02),
        'w_pa': nrm(ks[16], (DEPTH, NA_W, D_MODEL), NA_W ** -0.5),
        'w_pb': nrm(ks[17], (DEPTH, GDN_W, D_MODEL), GDN_W ** -0.5),
        'w_o': nrm(ks[18], (DEPTH, D_MODEL, D_MODEL), DEEPNORM_BETA * D_MODEL ** -0.5),
        'ffn2_w_in': nrm(ks[19], (DEPTH, D_MODEL, 2 * D_FF), D_MODEL ** -0.5),
        'ffn2_w_out': nrm(ks[20], (DEPTH, D_FF, D_MODEL), DEEPNORM_BETA * D_FF ** -0.5),
    }


def reference(x, c, ctx, c_ctx, w_ada, b_ada, ln_g, ln_b, ffn1_w_in, ffn1_w_out, w_in, na_rpb,
              gdn_conv_w, gdn_a_log, gdn_dt_bias, gdn_norm_w, w_pa, w_pb, w_o, ffn2_w_in, ffn2_w_out):
    rope = axial_rope(x.shape[1], GDN_HEAD_DIM)
    for layer in range(DEPTH):
        x, ctx = hybrid_layer(x, ctx, c, c_ctx, w_ada[layer], b_ada[layer], ln_g[layer], ln_b[layer],
                              ffn1_w_in[layer], ffn1_w_out[layer], w_in[layer], na_rpb[layer],
                              gdn_conv_w[layer], gdn_a_log[layer], gdn_dt_bias[layer], gdn_norm_w[layer],
                              w_pa[layer], w_pb[layer], w_o[layer], ffn2_w_in[layer], ffn2_w_out[layer],
                              rope, update_ctx=layer < DEPTH - 1)
    return x
```

```python
import numpy as np
from contextlib import ExitStack
import concourse.bass as bass
import concourse.mybir as mybir
from concourse.bass_utils import run_bass_kernel_spmd

F32 = mybir.dt.float32
F32R = mybir.dt.float32r
AF = mybir.ActivationFunctionType
ALU = mybir.AluOpType

ENGS = ("pe", "act", "dve", "pool", "sp")


class Res:
    __slots__ = ("name", "w", "r")

    def __init__(self, name=""):
        self.name = name
        self.w = None
        self.r = {}


class DSem:
    __slots__ = ("name", "total", "handle")

    def __init__(self, name):
        self.name = name
        self.total = 0
        self.handle = None


class OpRec:
    __slots__ = ("eng", "fn", "waits", "signal", "sigidx", "dsem", "dval", "cwaits", "retired")

    def __init__(self, eng, fn):
        self.eng = eng
        self.fn = fn
        self.waits = []
        self.cwaits = []
        self.signal = False
        self.sigidx = 0
        self.dsem = None
        self.dval = 0
        self.retired = False


class Prog:
    def __init__(self, nc, stack):
        self.nc = nc
        self.stack = stack
        self.ops = {e: [] for e in ENGS}
        self.dsems = []
        self.nops = 0
        self.esem = {e: stack.enter_context(nc.semaphore("s_" + e)) for e in ENGS}
        self.sigcount = {e: 0 for e in ENGS}
        self.known = {e: {} for e in ENGS}

    def dsem(self, name):
        d = DSem(name)
        self.dsems.append(d)
        return d

    def _deps(self, rec, reads, writes):
        eng = rec.eng
        is_dma = rec.dsem is not None
        deps = []
        for r in reads:
            if r.w is not None:
                deps.append((r.w, True))
        for w in writes:
            if w.w is not None:
                deps.append((w.w, False))
            for o in w.r.values():
                deps.append((o, False))
        for o, raw in deps:
            if o.retired:
                continue
            if o.dsem is not None:
                rec.waits.append((o.dsem, max(o.dsem.total, o.dval)))
                continue
            if o.eng == eng and not is_dma:
                if eng == "pe":
                    continue
                if not raw:
                    continue
            o.signal = True
            rec.cwaits.append(o)
        rkey = ("dma", id(rec.dsem)) if is_dma else eng
        for r in reads:
            r.r[rkey] = rec
        for w in writes:
            w.w = rec
            w.r = {}

    def op(self, eng, fn, reads=(), writes=()):
        rec = OpRec(eng, fn)
        self._deps(rec, reads, writes)
        self.ops[eng].append(rec)
        self.nops += 1
        return rec

    def dma(self, queue, dsem, out, in_, reads=(), writes=(), **kw):
        rec = OpRec(queue, lambda e: e.dma_start(out=out, in_=in_, **kw))
        rec.dsem = dsem
        self._deps(rec, reads, writes)
        dsem.total += 16
        rec.dval = dsem.total
        self.ops[queue].append(rec)
        self.nops += 1
        return rec

    def barrier(self):
        last = {}
        for e in ENGS:
            for rec in reversed(self.ops[e]):
                if rec.dsem is None and rec.fn is not None:
                    rec.signal = True
                    last[e] = rec
                    break
        dw = [(d, d.total) for d in self.dsems if d.total > 0]
        for e in ENGS:
            rec = OpRec(e, None)
            rec.waits = list(dw)
            rec.cwaits = [o for k, o in last.items() if k != e]
            self.ops[e].append(rec)

    def flush(self):
        self.barrier()
        nc = self.nc
        esem = self.esem
        for d in self.dsems:
            if d.handle is None:
                d.handle = self.stack.enter_context(nc.semaphore("d_" + d.name))
        for e in ENGS:
            for rec in self.ops[e]:
                if rec.signal and rec.dsem is None and rec.fn is not None:
                    self.sigcount[e] += 1
                    rec.sigidx = self.sigcount[e]

        def run(e, engobj):
            known = self.known[e]
            for rec in self.ops[e]:
                ws = []
                for (d, v) in rec.waits:
                    ws.append((d.handle, id(d), v))
                for o in rec.cwaits:
                    ws.append((esem[o.eng], o.eng, o.sigidx))
                for (h, key, v) in ws:
                    if known.get(key, 0) >= v:
                        continue
                    known[key] = v
                    engobj.wait_ge(h, v)
                if rec.fn is None:
                    continue
                ins = rec.fn(engobj)
                if rec.dsem is not None:
                    ins.then_inc(rec.dsem.handle, 16)
                elif rec.signal:
                    ins.then_inc(esem[e], 1)

        with nc.Block() as block:
            @block.tensor
            def _(pe):
                run("pe", pe)

            @block.scalar
            def _(act):
                run("act", act)

            @block.vector
            def _(dve):
                run("dve", dve)

            @block.gpsimd
            def _(pool):
                run("pool", pool)

            @block.sync
            def _(sp):
                run("sp", sp)
        for e in ENGS:
            for rec in self.ops[e]:
                rec.retired = True
            self.ops[e] = []


class Ring:
    def __init__(self, P, name, aps, with_dsem=True):
        self.slots = []
        for i, ap in enumerate(aps):
            self.slots.append((ap, Res(f"{name}{i}"), P.dsem(f"{name}{i}") if with_dsem else None))
        self.i = 0

    def next(self):
        s = self.slots[self.i % len(self.slots)]
        self.i += 1
        return s


D = 1024
DFF = 2816
NJ = DFF // 128
SEQ = 4096
CTX = 256
NTOK = SEQ + CTX
GRID_W = 64
LN_EPS = 1e-6
NORM_EPS = 1e-6
ALPHA = 2.0 ** 0.25
NEG = -30000.0
N_WIN_BLK = 45


def to_blocks(W):
    K, N = W.shape
    KC, NCB = K // 128, N // 128
    return np.ascontiguousarray(
        W.reshape(KC, 128, NCB, 128).transpose(2, 1, 0, 3).reshape(NCB, 128, KC * 128))


def na_tables():
    def r0(r):
        return min(max(r - 4, 0), 56)

    def c0(c):
        return min(max(c - 8, 0), 48)
    specs = [(2, j) for j in range(0, 5)] + [(0, j) for j in range(4)] + [(1, j) for j in range(4)] \
        + [(30, j) for j in range(28, 32)] + [(31, j) for j in range(28, 32)]
    ridx = np.zeros((21, 128, 128), np.int64)
    cidx = np.zeros((21, 128, 128), np.int64)
    mask = np.full((21, 128, 128), NEG, np.float32)
    for ti, (b, j) in enumerate(specs):
        for k in range(128):
            krow, kcol = 2 * j + k // 64, k % 64
            for q in range(128):
                qrow, qcol = 2 * b + q // 64, q % 64
                ok = (r0(qrow) <= krow < r0(qrow) + 8) and (c0(qcol) <= kcol < c0(qcol) + 16)
                if ok:
                    ridx[ti, k, q] = krow - qrow + 7
                    cidx[ti, k, q] = kcol - qcol + 15
                    mask[ti, k, q] = 0.0
    return ridx, cidx, mask


def na_chunks(b):
    if b == 0:
        return [(j, 5 + j) for j in range(4)]
    if b == 1:
        return [(j, 9 + j) for j in range(4)]
    if b == 30:
        return [(j, 13 + j - 28) for j in range(28, 32)]
    if b == 31:
        return [(j, 17 + j - 28) for j in range(28, 32)]
    return [(b - 2 + d, d) for d in range(5)]


_NA_CACHE = {}


def build_program(dbg=None, groups=tuple(range(9)), phases=(1, 2, 3, 4), na_hps=(0, 1, 2, 3), m_groups=tuple(range(8))):
    nc = bass.Bass("TRN2", target_bir_lowering=False)
    dbg = dbg or set()
    uid = [0]

    def din(name, shape, dt=F32):
        return nc.dram_tensor(name, list(shape), dt, kind="ExternalInput").ap()

    def dscr(name, shape, dt=F32):
        kind = "ExternalOutput" if name in dbg else "Internal"
        return nc.dram_tensor(name, list(shape), dt, kind=kind).ap()

    x_d = din("x", [SEQ, D])
    ctx_d = din("ctx", [CTX, D])
    cT_d = din("cT", [128, 8, 2])
    wada_d = din("w_ada", [D, 9 * D])
    badaT_d = din("b_adaT", [128, 72])
    lngb_d = din("lngb", [6, D])
    ident_d = din("ident", [128, 128])
    wu1_d = din("wu1", [44, 128, 1024], F32R)
    wd1_d = din("wd1", [8, 128, 2816], F32R)
    win_d = din("win", [N_WIN_BLK, 128, 1024], F32R)
    wpa_d = din("wpa", [8, 128, 512], F32R)
    wpb_d = din("wpb", [8, 128, 512], F32R)
    wo_d = din("wo", [8, 128, 1024], F32R)
    wu2_d = din("wu2", [44, 128, 1024], F32R)
    wd2_d = din("wd2", [8, 128, 2816], F32R)
    natab_d = din("na_tab", [8, 128, 21, 128])
    namask_d = din("na_mask", [128, 21, 128])
    convw_d = din("g_convw", [128, 12, 5])
    gmask_d = din("g_mask", [128, 8, 128])
    rope_d = din("g_rope", [128, 32, 128])
    adt_d = din("g_adt", [128, 16])
    normw_d = din("g_normw", [128, 128])
    out_d = nc.dram_tensor("out", [SEQ, D], F32, kind="ExternalOutput").ap()

    X1 = dscr("X1", [SEQ, D])
    QAT = dscr("QAT", [512, SEQ])
    KAT = dscr("KAT", [512, NTOK])
    VA = dscr("VA", [NTOK, 512])
    QKVT = dscr("QKVT", [1536, NTOK])
    AB = dscr("AB", [NTOK, 16])
    OA = dscr("OA", [SEQ, 512])
    OB = dscr("OB", [SEQ, 512])
    QKVN = dscr("QKVN", [NTOK, 1536])
    GB = dscr("GB", [NTOK, 16])
    DBG1 = dscr("DBG1", [128, 144])

    st = ExitStack()
    P = Prog(nc, st)
    cur = [st]

    def sb(name, shape, dt=F32):
        uid[0] += 1
        return cur[0].enter_context(nc.sbuf_tensor(f"sb{uid[0]}_{name}", list(shape), dt))

    ident = sb("ident", [128, 128])
    modT = sb("modT", [128, 72, 2])
    mod1p = sb("mod1p", [128, 72, 2])
    modh = sb("modh", [128, 72, 2])
    scT = sb("scT", [128, 8, 2])
    badaT = sb("badaT", [128, 72])
    stats = sb("stats", [128, 8, 16])
    epsc = sb("epsc", [128, 2])
    R_ident, R_mod = Res("ident"), Res("mod")
    P.op("dve", lambda e: e.memset(epsc[:, 0:1], LN_EPS), writes=[R_ident])
    P.op("dve", lambda e: e.memset(epsc[:, 1:2], NORM_EPS), writes=[R_ident])
    ds_misc = P.dsem("misc")

    psum = [st.enter_context(nc.psum_tensor(f"ps{i}", [128, 512], F32)) for i in range(8)]
    R_ps = [Res(f"ps{i}") for i in range(8)]

    ph0 = ExitStack()
    cur[0] = ph0
    wa_all = sb("wa_all", [128, 2, 6144])
    P.dma("sp", ds_misc, ident[:], ident_d[:, :], writes=[R_ident])
    P.dma("sp", ds_misc, scT[:], cT_d[:, :, :], writes=[R_mod])
    P.dma("sp", ds_misc, badaT[:], badaT_d[:, :], writes=[R_mod])
    P.op("act", lambda e: e.activation(out=scT[:], in_=scT[:], func=AF.Silu), reads=[R_mod], writes=[R_mod])
    wa_bufs = [wa_all[:, i, :] for i in range(2)]
    wa_ring = Ring(P, "wa", wa_bufs)
    for pn in range(12):
        ap, res, ds = wa_ring.next()
        apv = ap.rearrange("p (k n) -> p k n", k=8)
        P.dma("sp", ds, apv, wada_d[:, pn * 768:(pn + 1) * 768].rearrange("(k p) n -> p k n", p=128), writes=[res])
        for cc in range(6):
            ch = pn * 6 + cc
            for kc in range(8):
                P.op("pe", lambda e, apv=apv, cc=cc, ch=ch, kc=kc: e.matmul(
                    psum[7][:, ch * 2:ch * 2 + 2], lhsT=apv[:, kc, cc * 128:(cc + 1) * 128], rhs=scT[:, kc, :],
                    start=(kc == 0), stop=(kc == 7)), reads=[res, R_mod], writes=[R_ps[7]])
    ps7v = psum[7][:, 0:144].rearrange("p (c t) -> p c t", t=2)
    for t in range(2):
        P.op("dve", lambda e, t=t: e.tensor_tensor(out=modT[:, :, t], in0=ps7v[:, :, t], in1=badaT[:], op=ALU.add),
             reads=[R_ps[7], R_mod], writes=[R_mod])
    P.op("dve", lambda e: e.tensor_scalar_add(out=mod1p[:], in0=modT[:], scalar1=1.0), reads=[R_mod], writes=[R_mod])
    P.op("dve", lambda e: e.tensor_scalar_mul(out=modh[:], in0=modT[:], scalar1=0.5), reads=[R_mod], writes=[R_mod])
    if "DBG1" in dbg:
        P.dma("sp", ds_misc, DBG1[:, :], modT[:].rearrange("p c t -> p (c t)"), reads=[R_mod])
    P.flush()
    ph0.close()

    stat_i = [0]

    def ln_normalize(src, rsrc, dst, rdst):
        k = stat_i[0] % 8
        stat_i[0] += 1
        s = stats[:, k, :]
        rs = Res("st")
        P.op("dve", lambda e: e.bn_stats(out=s[:, 0:6], in_=src[:, 0:512]), reads=[rsrc], writes=[rs])
        P.op("dve", lambda e: e.bn_stats(out=s[:, 6:12], in_=src[:, 512:1024]), reads=[rsrc], writes=[rs])
        P.op("dve", lambda e: e.bn_aggr(out=s[:, 12:14], in_=s[:, 0:12]), reads=[rs], writes=[rs])
        P.op("act", lambda e: e.activation(out=s[:, 14:15], in_=s[:, 13:14], func=AF.Sqrt, bias=epsc[:, 0:1]),
             reads=[rs, R_ident], writes=[rs])
        P.op("dve", lambda e: e.reciprocal(out=s[:, 14:15], in_=s[:, 14:15]), reads=[rs], writes=[rs])
        P.op("dve", lambda e: e.scalar_tensor_tensor(out=s[:, 15:16], in0=s[:, 12:13], scalar=-1.0, in1=s[:, 14:15],
                                                     op0=ALU.mult, op1=ALU.mult), reads=[rs], writes=[rs])
        P.op("act", lambda e: e.activation(out=dst, in_=src, func=AF.Identity, scale=s[:, 14:15], bias=s[:, 15:16]),
             reads=[rsrc, rs], writes=[rdst])

    class Env:
        pass

    def make_env(rows, merge=False):
        E = Env()
        nr = len(rows)
        lngb = sb("lngb", [128, 2 * nr, D])
        R_lngb = Res("lngb")
        for i, r in enumerate(rows):
            P.dma("sp", ds_misc, lngb[:, i, :], lngb_d[r].partition_broadcast(128), writes=[R_lngb])
            P.dma("sp", ds_misc, lngb[:, nr + i, :], lngb_d[3 + r].partition_broadcast(128), writes=[R_lngb])
        lnslot = {r: i for i, r in enumerate(rows)}
        hx = sb("hx", [128, 4, D])
        zt = sb("zt", [128, 4, D])
        xn_all = sb("xn", [128, 2, D])
        xn = [xn_all[:, i, :] for i in range(2)]
        uTr = sb("uT", [128, 8, 512], F32R)
        Gr = sb("G", [128, NJ, 512], F32R)
        wb_all = sb("wb", [128, 3, 2816], F32R)
        tmpA_all = sb("tmpA", [128, 3, 512])
        tmpB_all = sb("tmpB", [128, 3, 512])
        R_hx = [Res(f"hx{t}") for t in range(4)]
        R_zt = [Res(f"zt{t}") for t in range(4)]
        R_xn = [Res("xn0"), Res("xn1")]
        R_uT = [Res(f"uT{t}") for t in range(4)]
        R_G = [Res(f"G{j}") for j in range(NJ)]
        tag = "m" if merge else "f"
        ds_hx = [P.dsem(f"{tag}hx{t}") for t in range(4)]
        ds_zt = [P.dsem(f"{tag}zt{t}") for t in range(4)]
        wring = Ring(P, tag + "wb", [wb_all[:, i, :] for i in range(3)])
        tmpA_ring = Ring(P, tag + "tA", [tmpA_all[:, i, :] for i in range(3)])
        tmpB_ring = Ring(P, tag + "tB", [tmpB_all[:, i, :] for i in range(3)])
        xn_i = [0]
        ps_gemm_i = [0]
        ps_tr_i = [0]

        def gemm_ps():
            k = ps_gemm_i[0] % 4
            ps_gemm_i[0] += 1
            return psum[k], R_ps[k]

        def tr_ps():
            k = 4 + ps_tr_i[0] % 3
            ps_tr_i[0] += 1
            return psum[k], R_ps[k]

        class WStream:
            def __init__(self, plan):
                self.plan = plan
                self.loaded = []
                self.i = 0
                self.pre(2)

            def pre(self, n):
                while len(self.loaded) < n and self.i < len(self.plan):
                    src, ncol = self.plan[self.i]
                    self.i += 1
                    ap, res, ds = wring.next()
                    P.dma("pool", ds, ap[:, 0:ncol], src, writes=[res])
                    self.loaded.append((ap, res))

            def get(self):
                self.pre(1)
                ap, res = self.loaded.pop(0)
                self.pre(2)
                return ap, res

        def modulate_T(src, rsrc, t, base, sel):
            k = xn_i[0] % 2
            xn_i[0] += 1
            ln_normalize(src, rsrc, xn[k], R_xn[k])
            for half in range(2):
                ps, rps = tr_ps()
                for q in range(4):
                    kc = half * 4 + q
                    P.op("pe", lambda e, ps=ps, q=q, kc=kc, k=k: e.transpose(
                        out=ps[:, q * 128:(q + 1) * 128], in_=xn[k][:, kc * 128:(kc + 1) * 128], identity=ident[:]),
                        reads=[R_xn[k], R_ident], writes=[rps])
                for q in range(4):
                    kc = half * 4 + q
                    ci_shift = base * 8 + kc
                    ci_scale = (base + 1) * 8 + kc
                    dst = uTr[:, kc, t * 128:(t + 1) * 128]
                    if q % 2 == 0:
                        P.op("act", lambda e, ps=ps, q=q, dst=dst, a=ci_scale, b=ci_shift: e.activation(
                            out=dst, in_=ps[:, q * 128:(q + 1) * 128], func=AF.Identity,
                            scale=mod1p[:, a, sel:sel + 1], bias=modT[:, b, sel:sel + 1]),
                            reads=[rps, R_mod], writes=[R_uT[t]])
                    else:
                        P.op("dve", lambda e, ps=ps, q=q, dst=dst, a=ci_scale, b=ci_shift: e.tensor_scalar(
                            out=dst, in0=ps[:, q * 128:(q + 1) * 128], scalar1=mod1p[:, a, sel:sel + 1],
                            scalar2=modT[:, b, sel:sel + 1], op0=ALU.mult, op1=ALU.add),
                            reads=[rps, R_mod], writes=[R_uT[t]])

        def gemm_block(ws, KC, rhs_fn, rhs_res, ntok):
            wap, wres = ws.get()
            ps, rps = gemm_ps()
            for kc in range(KC):
                P.op("pe", lambda e, ps=ps, wap=wap, kc=kc: e.matmul(
                    ps[:, 0:ntok], lhsT=wap[:, kc * 128:(kc + 1) * 128], rhs=rhs_fn(kc),
                    start=(kc == 0), stop=(kc == KC - 1)), reads=[wres] + list(rhs_res), writes=[rps])
            return ps, rps

        def post_ln(buf, rbuf, lnrow):
            i = lnslot[lnrow]
            ln_normalize(buf, rbuf, buf, rbuf)
            P.op("dve", lambda e: e.tensor_tensor(out=buf, in0=buf, in1=lngb[:, i, :], op=ALU.mult),
                 reads=[rbuf, R_lngb], writes=[rbuf])
            P.op("pool", lambda e: e.tensor_tensor(out=buf, in0=buf, in1=lngb[:, nr + i, :], op=ALU.add),
                 reads=[rbuf, R_lngb], writes=[rbuf])

        def back_to_tokens(ps, rps, scale_ap, ntile, src, rsrc, dst, rdst, c):
            ntok = ntile * 128
            tb, rtb, _ = tmpB_ring.next()
            P.op("act", lambda e: e.activation(out=tb[:, 0:ntok], in_=ps[:, 0:ntok], func=AF.Identity, scale=scale_ap),
                 reads=[rps, R_mod], writes=[rtb])
            pt, rpt = tr_ps()
            for t in range(ntile):
                P.op("pe", lambda e, t=t: e.transpose(
                    out=pt[:, t * 128:(t + 1) * 128], in_=tb[:, t * 128:(t + 1) * 128], identity=ident[:]),
                    reads=[rtb, R_ident], writes=[rpt])
            P.op("dve", lambda e: e.scalar_tensor_tensor(
                out=dst[:, 0:ntile, c * 128:(c + 1) * 128], in0=src[:, 0:ntile, c * 128:(c + 1) * 128], scalar=ALPHA,
                in1=pt[:, 0:ntok].rearrange("p (t d) -> p t d", t=ntile), op0=ALU.mult, op1=ALU.add),
                reads=[rpt] + rsrc[:ntile], writes=rdst[:ntile])

        def ffn_sublayer(ntile, sub, sel, wu_d, wd_d, lnrow, src, rsrc, dst, rdst):
            ntok = ntile * 128
            base = 3 * sub
            for t in range(ntile):
                modulate_T(src[:, t, :], rsrc[t], t, base, sel)
            plan = []
            for j in range(NJ):
                plan.append((wu_d[j], 1024))
                plan.append((wu_d[NJ + j], 1024))
            for c in range(8):
                plan.append((wd_d[c], 2816))
            ws = WStream(plan)
            ures = R_uT[:ntile]
            for j in range(NJ):
                psa, rpa = gemm_block(ws, 8, lambda kc: uTr[:, kc, 0:ntok], ures, ntok)
                psb, rpb = gemm_block(ws, 8, lambda kc: uTr[:, kc, 0:ntok], ures, ntok)
                ta, rta, _ = tmpA_ring.next()
                P.op("act", lambda e, ta=ta, psa=psa: e.activation(out=ta[:, 0:ntok], in_=psa[:, 0:ntok], func=AF.Silu),
                     reads=[rpa], writes=[rta])
                P.op("dve", lambda e, ta=ta, psb=psb, j=j: e.tensor_tensor(
                    out=Gr[:, j, 0:ntok], in0=ta[:, 0:ntok], in1=psb[:, 0:ntok], op=ALU.mult),
                    reads=[rta, rpb], writes=[R_G[j]])
            for c in range(8):
                ps, rps = gemm_block(ws, NJ, lambda kc: Gr[:, kc, 0:ntok], R_G, ntok)
                gi = (base + 2) * 8 + c
                back_to_tokens(ps, rps, modh[:, gi, sel:sel + 1], ntile, src, rsrc, dst, rdst, c)
            for t in range(ntile):
                post_ln(dst[:, t, :], rdst[t], lnrow)

        E.__dict__.update(locals())
        return E

    def phase1_group(E, g):
        is_ctx = (g == 8)
        ntile = 2 if is_ctx else 4
        ntok = ntile * 128
        sel = 1 if is_ctx else 0
        tok0 = SEQ if is_ctx else g * 512
        hx, zt, uTr = E.hx, E.zt, E.uTr
        for t in range(ntile):
            src = ctx_d[t * 128:(t + 1) * 128, :] if is_ctx else x_d[g * 512 + t * 128:g * 512 + (t + 1) * 128, :]
            P.dma("sp", E.ds_hx[t], hx[:, t, :], src, writes=[E.R_hx[t]])
        E.ffn_sublayer(ntile, 0, sel, wu1_d, wd1_d, 0, hx, E.R_hx, zt, E.R_zt)
        if not is_ctx:
            for t in range(ntile):
                P.dma("sp", E.ds_zt[t], X1[tok0 + t * 128:tok0 + (t + 1) * 128, :], zt[:, t, :], reads=[E.R_zt[t]])
        for t in range(ntile):
            E.modulate_T(zt[:, t, :], E.R_zt[t], t, 3, sel)
        blocks = list(range(0 if not is_ctx else 4, 24)) + [28]
        ws = E.WStream([(win_d[cb], 1024) for cb in blocks])
        ures = E.R_uT[:ntile]
        for cb in blocks:
            ps, rps = E.gemm_block(ws, 8, lambda kc: uTr[:, kc, 0:ntok], ures, ntok)
            tb, rtb, dsb = E.tmpB_ring.next()
            if cb < 4:
                P.op("act", lambda e, tb=tb, ps=ps: e.activation(out=tb[:, 0:ntok], in_=ps[:, 0:ntok], func=AF.Copy,
                                                                 scale=0.125), reads=[rps], writes=[rtb])
                P.dma("sp", dsb, QAT[cb * 128:(cb + 1) * 128, tok0:tok0 + ntok], tb[:, 0:ntok], reads=[rtb])
            elif cb < 8 or (12 <= cb < 24):
                P.op("dve", lambda e, tb=tb, ps=ps: e.tensor_copy(out=tb[:, 0:ntok], in_=ps[:, 0:ntok]),
                     reads=[rps], writes=[rtb])
                if cb < 8:
                    dst = KAT[(cb - 4) * 128:(cb - 3) * 128, tok0:tok0 + ntok]
                else:
                    dst = QKVT[(cb - 12) * 128:(cb - 11) * 128, tok0:tok0 + ntok]
                P.dma("sp", dsb, dst, tb[:, 0:ntok], reads=[rtb])
            else:
                P.op("act", lambda e, tb=tb, ps=ps: e.activation(out=tb[:, 0:ntok], in_=ps[:, 0:ntok], func=AF.Copy),
                     reads=[rps], writes=[rtb])
                pt, rpt = E.tr_ps()
                for t in range(ntile):
                    P.op("pe", lambda e, pt=pt, tb=tb, t=t: e.transpose(
                        out=pt[:, t * 128:(t + 1) * 128], in_=tb[:, t * 128:(t + 1) * 128], identity=ident[:]),
                        reads=[rtb, R_ident], writes=[rpt])
                ta, rta, dsa = E.tmpA_ring.next()
                P.op("dve", lambda e, ta=ta, pt=pt: e.tensor_copy(out=ta[:, 0:ntok], in_=pt[:, 0:ntok]),
                     reads=[rpt], writes=[rta])
                tav = ta[:, 0:ntok].rearrange("p (t c) -> p t c", t=ntile)
                if cb == 28:
                    dst = AB[tok0:tok0 + ntok, :].rearrange("(t p) c -> p t c", p=128)
                    P.dma("sp", dsa, dst, tav[:, :, 0:16], reads=[rta])
                else:
                    dst = VA[tok0:tok0 + ntok, (cb - 8) * 128:(cb - 7) * 128].rearrange("(t p) c -> p t c", p=128)
                    P.dma("sp", dsa, dst, tav, reads=[rta])

    if 1 in phases:
        ph1 = ExitStack()
        cur[0] = ph1
        E1 = make_env([0])
        for g in groups:
            phase1_group(E1, g)
        P.flush()
        ph1.close()

    if 2 in phases:
        ph2 = ExitStack()
        cur[0] = ph2
        KT = sb("naK", [128, NTOK])
        QT = sb("naQ", [128, SEQ])
        vv = sb("naV", [128, 34, 2, 65])
        tabs = sb("naT", [128, 2, 21, 128])
        msk = sb("naM", [128, 21, 128])
        pT_all = sb("naP", [128, 4, 128])
        ob_all = sb("naO", [128, 2, 128])
        rec_all = sb("naR", [128, 4])
        R_KT, R_QT, R_vv, R_msk = Res("KT"), Res("QT"), Res("vv"), Res("msk")
        R_tab = [Res("tab0"), Res("tab1")]
        ds_na = P.dsem("na")
        pT_ring = Ring(P, "naP", [pT_all[:, i, :] for i in range(4)], with_dsem=False)
        ob_ring = Ring(P, "naO", [ob_all[:, i, :] for i in range(2)])
        rec_ring = Ring(P, "naR", [rec_all[:, i:i + 1] for i in range(4)], with_dsem=False)
        P.op("dve", lambda e: e.memset(vv[:, :, :, 64:65], 1.0), writes=[R_vv])
        P.dma("sp", ds_na, msk[:], namask_d[:, :, :], writes=[R_msk])
        cnt_s = [0]
        cnt_o = [0]
        for hp in na_hps:
            P.dma("sp", ds_na, KT[:], KAT[hp * 128:(hp + 1) * 128, :], writes=[R_KT])
            P.dma("sp", ds_na, QT[:], QAT[hp * 128:(hp + 1) * 128, :], writes=[R_QT])
            for h2 in range(2):
                c0 = hp * 128 + h2 * 64
                P.dma("sp", ds_na, vv[:, :, h2, 0:64], VA[:, c0:c0 + 64].rearrange("(t p) c -> p t c", p=128),
                      writes=[R_vv])
                P.dma("sp", ds_na, tabs[:, h2], natab_d[hp * 2 + h2], writes=[R_tab[h2]])
                P.op("dve", lambda e, h2=h2: e.tensor_tensor(out=tabs[:, h2], in0=tabs[:, h2], in1=msk[:], op=ALU.add),
                     reads=[R_tab[h2], R_msk], writes=[R_tab[h2]])
            for b in range(32):
                ob, rob, dsob = ob_ring.next()
                for h2 in range(2):
                    lo, hi = h2 * 64, (h2 + 1) * 64
                    chunks = na_chunks(b) + [(32, None), (33, None)]
                    po, rpo = psum[4 + cnt_o[0] % 2], R_ps[4 + cnt_o[0] % 2]
                    cnt_o[0] += 1
                    nch = len(chunks)

                    def pv(pT, rpT, j, ci, po=po, rpo=rpo, h2=h2, nch=nch):
                        P.op("pe", lambda e: e.matmul(po[:, 0:65], lhsT=pT, rhs=vv[:, j, h2, :],
                                                      start=(ci == 0), stop=(ci == nch - 1)),
                             reads=[rpT, R_vv], writes=[rpo])
                    pending = None
                    for ci, (j, ti) in enumerate(chunks):
                        k = cnt_s[0] % 4
                        cnt_s[0] += 1
                        ps_s, rps_s = psum[k], R_ps[k]
                        P.op("pe", lambda e, ps_s=ps_s, j=j, ti=ti, b=b, lo=lo, hi=hi: e.matmul(
                            ps_s[:, 0:128], lhsT=KT[lo:hi, j * 128:(j + 1) * 128], rhs=QT[lo:hi, b * 128:(b + 1) * 128],
                            start=True, stop=(ti is None)), reads=[R_KT, R_QT], writes=[rps_s])
                        if ti is not None:
                            P.op("pe", lambda e, ps_s=ps_s, ti=ti, h2=h2: e.matmul(
                                ps_s[:, 0:128], lhsT=ident[:], rhs=tabs[:, h2, ti, :], start=False, stop=True),
                                reads=[R_ident, R_tab[h2]], writes=[rps_s])
                        if pending is not None:
                            pv(*pending)
                        pT, rpT, _ = pT_ring.next()
                        P.op("act", lambda e, pT=pT, ps_s=ps_s: e.activation(out=pT, in_=ps_s[:, 0:128], func=AF.Exp),
                             reads=[rps_s], writes=[rpT])
                        pending = (pT, rpT, j, ci)
                    pv(*pending)
                    rc, rrc, _ = rec_ring.next()
                    P.op("dve", lambda e, rc=rc, po=po: e.reciprocal(out=rc, in_=po[:, 64:65]), reads=[rpo], writes=[rrc])
                    P.op("act", lambda e, rc=rc, po=po, ob=ob, lo=lo, hi=hi: e.activation(
                        out=ob[:, lo:hi], in_=po[:, 0:64], func=AF.Identity, scale=rc), reads=[rpo, rrc], writes=[rob])
                P.dma("sp", dsob, OA[b * 128:(b + 1) * 128, hp * 128:(hp + 1) * 128], ob, reads=[rob])
        P.flush()
        ph2.close()

    if 3 in phases:
        ph3 = ExitStack()
        cur[0] = ph3
        convw = sb("g_convw", [128, 12, 5])
        msk4 = sb("g_msk", [128, 8, 128])
        ropet = sb("g_rope", [128, 32, 128])
        adt = sb("g_adt", [128, 16])
        nea = sb("g_nea", [128, 8])
        onec = sb("g_onec", [128, 1])
        ones3 = sb("g_ones3", [128, 4, 128])
        R_gc = Res("gconst")
        ds_g = P.dsem("gconst")
        P.dma("sp", ds_g, convw[:], convw_d[:, :, :], writes=[R_gc])
        P.dma("sp", ds_g, msk4[:], gmask_d[:, :, :], writes=[R_gc])
        P.dma("sp", ds_g, ropet[:], rope_d[:, :, :], writes=[R_gc])
        P.dma("sp", ds_g, adt[:], adt_d[:, :], writes=[R_gc])
        P.op("dve", lambda e: e.memset(onec[:], 1.0), writes=[R_gc])
        P.op("dve", lambda e: e.memset(ones3[:], 1.0), writes=[R_gc])
        P.op("act", lambda e: e.activation(out=nea[:], in_=adt[:, 0:8], func=AF.Exp), reads=[R_gc], writes=[R_gc])
        P.op("dve", lambda e: e.tensor_scalar_mul(out=nea[:], in0=nea[:], scalar1=-1.0), reads=[R_gc], writes=[R_gc])
        gbank_i = [0]

        def gbank():
            k = gbank_i[0] % 8
            gbank_i[0] += 1
            return psum[k], R_ps[k]

        def v3(ap, h=4):
            return ap.rearrange("p (h n) -> p h n", h=h)

        def bc_last(ap2, h=4, n=128):
            return ap2.unsqueeze(2).to_broadcast([128, h, n])

        def bc_mid(ap2, h=4, n=128):
            return ap2.unsqueeze(1).to_broadcast([128, h, n])

        def tile_rows(tau):
            return tau * 128 if tau < 32 else SEQ + (tau - 32) * 128

        R_QKVN = [Res(f"qkvn{t}") for t in range(34)]
        R_GB = [Res(f"gb{t}") for t in range(34)]
        R_OB = [Res(f"ob{t}") for t in range(32)]

        cw = sb("g_cw", [128, 12, 132])
        acc = sb("g_acc", [128, 12, 128])
        tmpc = sb("g_tmpc", [128, 12, 128])
        tm = sb("g_tm", [128, 1536])
        sq = sb("g_sq", [128, 1024])
        rp = sb("g_rp", [128, 1024])
        qkn = sb("g_qkn", [128, 1024])
        tA = sb("g_tA", [128, 512])
        tB = sb("g_tB", [128, 512])
        smp = sb("g_smp", [128, 16])
        abt = sb("g_abt", [128, 16])
        gbt = sb("g_gbt", [128, 16])
        R_cw, R_acc, R_tmpc, R_tm, R_sq, R_rp, R_qkn = (Res(n) for n in ("cw", "acc", "tmpc", "tm", "sq", "rp", "qkn"))
        R_tA, R_tB, R_smp, R_abt, R_gbt = (Res(n) for n in ("tA", "tB", "smp", "abt", "gbt"))
        ds_cw, ds_tm, ds_qkn, ds_abt, ds_gbt = (P.dsem(n) for n in ("g_cw", "g_tm", "g_qkn", "g_abt", "g_gbt"))
        for tau in range(34):
            t0 = tile_rows(tau)
            seg_lo, seg_hi = (0, SEQ) if tau < 32 else (SEQ, NTOK)
            lo, hi = max(t0 - 2, seg_lo), min(t0 + 130, seg_hi)
            off = lo - (t0 - 2)
            if off > 0:
                P.op("pool", lambda e: e.memset(cw[:, :, 0:2], 0.0), writes=[R_cw])
            if hi < t0 + 130:
                P.op("pool", lambda e: e.memset(cw[:, :, 130:132], 0.0), writes=[R_cw])
            P.dma("sp", ds_cw, cw[:, :, off:off + hi - lo], QKVT[:, lo:hi].rearrange("(c p) n -> p c n", p=128),
                  writes=[R_cw])
            P.op("dve", lambda e: e.tensor_tensor(out=acc[:], in0=cw[:, :, 0:128],
                                                  in1=convw[:, :, 0:1].to_broadcast([128, 12, 128]), op=ALU.mult),
                 reads=[R_cw, R_gc], writes=[R_acc])
            for k in range(1, 5):
                P.op("pool", lambda e, k=k: e.tensor_tensor(out=tmpc[:], in0=cw[:, :, k:k + 128],
                                                            in1=convw[:, :, k:k + 1].to_broadcast([128, 12, 128]),
                                                            op=ALU.mult), reads=[R_cw, R_gc], writes=[R_tmpc])
                P.op("dve", lambda e: e.tensor_tensor(out=acc[:], in0=acc[:], in1=tmpc[:], op=ALU.add),
                     reads=[R_acc, R_tmpc], writes=[R_acc])
            P.op("act", lambda e: e.activation(out=acc[:], in_=acc[:], func=AF.Silu), reads=[R_acc], writes=[R_acc])
            for c4 in range(3):
                ps, rps = gbank()
                for q in range(4):
                    P.op("pe", lambda e, ps=ps, q=q, c4=c4: e.transpose(
                        out=ps[:, q * 128:(q + 1) * 128], in_=acc[:, c4 * 4 + q, :], identity=ident[:]),
                        reads=[R_acc, R_ident], writes=[rps])
                if c4 % 2 == 0:
                    P.op("act", lambda e, ps=ps, c4=c4: e.activation(out=tm[:, c4 * 512:(c4 + 1) * 512], in_=ps[:, :],
                                                                     func=AF.Copy), reads=[rps], writes=[R_tm])
                else:
                    P.op("dve", lambda e, ps=ps, c4=c4: e.tensor_copy(out=tm[:, c4 * 512:(c4 + 1) * 512], in_=ps[:, :]),
                         reads=[rps], writes=[R_tm])
            P.op("dve", lambda e: e.tensor_tensor(out=sq[:], in0=tm[:, 0:1024], in1=tm[:, 0:1024], op=ALU.mult),
                 reads=[R_tm], writes=[R_sq])
            P.op("dve", lambda e: e.reduce_sum(out=smp[:, 0:8], in_=v3(sq[:], 8), axis=mybir.AxisListType.X),
                 reads=[R_sq], writes=[R_smp])
            P.op("act", lambda e: e.activation(out=smp[:, 8:16], in_=smp[:, 0:8], func=AF.Sqrt, bias=epsc[:, 1:2]),
                 reads=[R_smp, R_ident], writes=[R_smp])
            P.op("dve", lambda e: e.reciprocal(out=smp[:, 8:16], in_=smp[:, 8:16]), reads=[R_smp], writes=[R_smp])
            P.op("dve", lambda e: e.tensor_scalar_mul(out=smp[:, 8:12], in0=smp[:, 8:12], scalar1=128.0 ** -0.5),
                 reads=[R_smp], writes=[R_smp])
            if tau < 32:
                x5 = tm[:, 0:1024].rearrange("p (i a h f) -> p i a h f", i=8, a=2, h=2)
                r5 = rp[:].rearrange("p (i a h f) -> p i a h f", i=8, a=2, h=2)
                cs = ropet[:, tau, :].rearrange("p (s a f) -> p s a f", s=2, a=2)
                tA4 = tA[:].rearrange("p (i a f) -> p i a f", i=8, a=2)
                tB4 = tB[:].rearrange("p (i a f) -> p i a f", i=8, a=2)
                for (xa, xb, half, op) in ((0, 1, 0, ALU.subtract), (1, 0, 1, ALU.add)):
                    for ax in range(2):
                        cosb = cs[:, 0, ax, :].unsqueeze(1).to_broadcast([128, 8, 32])
                        sinb = cs[:, 1, ax, :].unsqueeze(1).to_broadcast([128, 8, 32])
                        P.op("pool", lambda e, xa=xa, ax=ax, cosb=cosb: e.tensor_tensor(
                            out=tA4[:, :, ax, :], in0=x5[:, :, ax, xa, :], in1=cosb, op=ALU.mult),
                            reads=[R_tm, R_gc], writes=[R_tA])
                        P.op("dve", lambda e, xb=xb, ax=ax, sinb=sinb: e.tensor_tensor(
                            out=tB4[:, :, ax, :], in0=x5[:, :, ax, xb, :], in1=sinb, op=ALU.mult),
                            reads=[R_tm, R_gc], writes=[R_tB])
                        P.op("dve", lambda e, half=half, op=op, ax=ax: e.tensor_tensor(
                            out=r5[:, :, ax, half, :], in0=tA4[:, :, ax, :], in1=tB4[:, :, ax, :], op=op),
                            reads=[R_tA, R_tB], writes=[R_rp])
                srcqk, rsrc = rp, R_rp
            else:
                srcqk, rsrc = tm, R_tm
            P.op("dve", lambda e, srcqk=srcqk: e.tensor_tensor(out=v3(qkn[:], 8), in0=v3(srcqk[:, 0:1024], 8),
                                                               in1=bc_last(smp[:, 8:16], 8), op=ALU.mult),
                 reads=[rsrc, R_smp], writes=[R_qkn])
            P.dma("sp", ds_qkn, QKVN[t0:t0 + 128, 0:1024], qkn[:], reads=[R_qkn], writes=[R_QKVN[tau]])
            P.dma("sp", ds_tm, QKVN[t0:t0 + 128, 1024:1536], tm[:, 1024:1536], reads=[R_tm], writes=[R_QKVN[tau]])
            P.dma("sp", ds_abt, abt[:], AB[t0:t0 + 128, :], writes=[R_abt])
            P.op("dve", lambda e: e.tensor_tensor(out=gbt[:, 0:8], in0=abt[:, 0:8], in1=adt[:, 8:16], op=ALU.add),
                 reads=[R_abt, R_gc], writes=[R_gbt])
            P.op("act", lambda e: e.activation(out=gbt[:, 0:8], in_=gbt[:, 0:8], func=AF.Exp), reads=[R_gbt], writes=[R_gbt])
            P.op("act", lambda e: e.activation(out=gbt[:, 0:8], in_=gbt[:, 0:8], func=AF.Ln, bias=onec[:, 0:1]),
                 reads=[R_gbt, R_gc], writes=[R_gbt])
            P.op("dve", lambda e: e.tensor_tensor(out=gbt[:, 0:8], in0=gbt[:, 0:8], in1=nea[:], op=ALU.mult),
                 reads=[R_gbt, R_gc], writes=[R_gbt])
            P.op("act", lambda e: e.activation(out=gbt[:, 8:16], in_=abt[:, 8:16], func=AF.Sigmoid),
                 reads=[R_abt], writes=[R_gbt])
            P.dma("sp", ds_gbt, GB[t0:t0 + 128, :], gbt[:], reads=[R_gbt], writes=[R_GB[tau]])

        names = ["kb", "vb", "kbg", "ktl", "kT", "qT", "kbT", "gbc", "Dd", "tmn", "tmx", "Ee", "Ff", "Em", "Ea", "Fm",
                 "X0", "X1", "XT0", "XT1", "R0", "R1", "Q0", "Q1", "Xf", "XTf", "Cm", "CmT", "W1", "W2", "AT", "uu", "wT", "vnew", "o1", "obf", "prev"]
        B = {n: sb("gs_" + n, [128, 4, 128]) for n in names}
        RB = {n: Res(n) for n in names}
        qkv_all = sb("gs_qkv", [128, 2, 1536])
        gb_all = sb("gs_gb", [128, 2, 16])
        qkv_ring = Ring(P, "gs_qkv", [qkv_all[:, i, :] for i in range(2)])
        gb_ring = Ring(P, "gs_gb", [gb_all[:, i, :] for i in range(2)])
        sm2 = sb("gs_sm2", [128, 24])
        R_sm2 = Res("sm2")
        Sst = sb("gs_S", [128, 4, 128])
        R_S = Res("S")
        ds_ob = P.dsem("gs_ob")
        ds_prev = P.dsem("gs_prev")

        def mm4(lhs_fn, rhs_fn, reads):
            ps, rps = gbank()
            for h in range(4):
                P.op("pe", lambda e, h=h, l=lhs_fn(h), r=rhs_fn(h): e.matmul(
                    ps[:, h * 128:(h + 1) * 128], lhsT=l, rhs=r, start=True, stop=True), reads=reads, writes=[rps])
            return v3(ps[:, :]), rps

        def tr4(src, rsrc, dstn, eng):
            ps, rps = gbank()
            for h in range(4):
                P.op("pe", lambda e, h=h, a=src(h): e.transpose(out=ps[:, h * 128:(h + 1) * 128], in_=a, identity=ident[:]),
                     reads=[rsrc, R_ident], writes=[rps])
            copy4(v3(ps[:, :]), rps, dstn, eng)

        def copy4(ps3, rps, dstn, eng):
            if eng == "act":
                P.op("act", lambda e: e.activation(out=B[dstn][:], in_=ps3, func=AF.Copy), reads=[rps], writes=[RB[dstn]])
            else:
                P.op("dve", lambda e: e.tensor_copy(out=B[dstn][:], in_=ps3), reads=[rps], writes=[RB[dstn]])

        def tt(eng, outn, in0, in1, op, reads):
            P.op(eng, lambda e: e.tensor_tensor(out=B[outn][:], in0=in0, in1=in1, op=op), reads=reads, writes=[RB[outn]])

        for d in range(2):
            order = [32, 33] + list(range(32)) if d == 0 else [33, 32] + list(range(31, -1, -1))
            U = msk4[:, 1, :] if d == 0 else msk4[:, 3, :]
            m_s = msk4[:, 0, :] if d == 0 else msk4[:, 2, :]
            m_i = msk4[:, 1, :] if d == 0 else msk4[:, 3, :]
            m_sT = msk4[:, 2, :] if d == 0 else msk4[:, 0, :]
            P.op("dve", lambda e: e.memset(Sst[:], 0.0), writes=[R_S])
            for tau in order:
                t0 = tile_rows(tau)
                wout = tau < 32
                qkv, rqkv, dsq = qkv_ring.next()
                gbv, rgbv, dsg = gb_ring.next()
                P.dma("sp", dsq, qkv, QKVN[t0:t0 + 128, :], reads=[R_QKVN[tau]], writes=[rqkv])
                P.dma("sp", dsg, gbv, GB[t0:t0 + 128, :], reads=[R_GB[tau]], writes=[rgbv])
                gs = gbv[:, d * 4:(d + 1) * 4]
                bs = gbv[:, 8 + d * 4:8 + (d + 1) * 4]
                qn = v3(qkv[:, 0:512])
                kn = v3(qkv[:, 512:1024])
                vn = v3(qkv[:, 1024:1536])
                ps, rps = gbank()
                P.op("pe", lambda e, ps=ps, U=U, gs=gs: e.matmul(ps[:, 0:4], lhsT=U, rhs=gs, start=True, stop=True),
                     reads=[R_gc, rgbv], writes=[rps])
                P.op("pe", lambda e, ps=ps, gs=gs: e.matmul(ps[:, 4:8], lhsT=ones3[:, 0, :], rhs=gs, start=True, stop=True),
                     reads=[R_gc, rgbv], writes=[rps])
                P.op("dve", lambda e, ps=ps: e.tensor_copy(out=sm2[:, 0:8], in_=ps[:, 0:8]), reads=[rps], writes=[R_sm2])
                P.op("dve", lambda e: e.tensor_tensor(out=sm2[:, 16:20], in0=sm2[:, 4:8], in1=sm2[:, 0:4], op=ALU.subtract),
                     reads=[R_sm2], writes=[R_sm2])
                P.op("act", lambda e: e.activation(out=sm2[:, 8:12], in_=sm2[:, 0:4], func=AF.Exp), reads=[R_sm2], writes=[R_sm2])
                P.op("act", lambda e: e.activation(out=sm2[:, 12:16], in_=sm2[:, 4:8], func=AF.Exp), reads=[R_sm2], writes=[R_sm2])
                P.op("act", lambda e: e.activation(out=sm2[:, 16:20], in_=sm2[:, 16:20], func=AF.Exp), reads=[R_sm2], writes=[R_sm2])
                gc, egc, etot, etl = sm2[:, 0:4], sm2[:, 8:12], sm2[:, 12:16], sm2[:, 16:20]
                tt("dve", "kb", kn, bc_last(bs), ALU.mult, [rqkv, rgbv])
                tt("pool", "vb", vn, bc_last(bs), ALU.mult, [rqkv, rgbv])
                tt("dve", "kbg", B["kb"][:], bc_last(egc), ALU.mult, [RB["kb"], R_sm2])
                tt("pool", "ktl", kn, bc_last(etl), ALU.mult, [rqkv, R_sm2])
                tr4(lambda h: kn[:, h, :], rqkv, "kT", "act")
                tr4(lambda h: B["kb"][:, h, :], RB["kb"], "kbT", "dve")
                if wout:
                    tr4(lambda h: qn[:, h, :], rqkv, "qT", "act")
                tt("pool", "gbc", ones3[:], bc_last(gs), ALU.mult, [R_gc, rgbv])
                psD, rpsD = mm4(lambda h: B["gbc"][:, h, :], lambda h: U, [RB["gbc"], R_gc])
                tt("dve", "Dd", psD, bc_last(gc), ALU.subtract, [rpsD, R_sm2])
                P.op("dve", lambda e: e.tensor_scalar_min(out=B["tmn"][:], in0=B["Dd"][:], scalar1=0.0),
                     reads=[RB["Dd"]], writes=[RB["tmn"]])
                P.op("pool", lambda e: e.tensor_scalar_max(out=B["tmx"][:], in0=B["Dd"][:], scalar1=0.0),
                     reads=[RB["Dd"]], writes=[RB["tmx"]])
                P.op("act", lambda e: e.activation(out=B["Ee"][:], in_=B["tmn"][:], func=AF.Exp),
                     reads=[RB["tmn"]], writes=[RB["Ee"]])
                P.op("act", lambda e: e.activation(out=B["Ff"][:], in_=B["tmx"][:], func=AF.Exp, scale=-1.0),
                     reads=[RB["tmx"]], writes=[RB["Ff"]])
                tt("pool", "Em", B["Ee"][:], bc_mid(m_s), ALU.mult, [RB["Ee"], R_gc])
                tt("pool", "Fm", B["Ff"][:], bc_mid(m_sT), ALU.mult, [RB["Ff"], R_gc])
                psK, rpsK = mm4(lambda h: B["kT"][:, h, :], lambda h: B["kbT"][:, h, :], [RB["kT"], RB["kbT"]])
                P.op("dve", lambda e, psK=psK: e.scalar_tensor_tensor(out=B["Xf"][:], in0=psK, scalar=-1.0, in1=B["Em"][:],
                                                                      op0=ALU.mult, op1=ALU.mult),
                     reads=[rpsK, RB["Em"]], writes=[RB["Xf"]])
                psK2, rpsK2 = mm4(lambda h: B["kbT"][:, h, :], lambda h: B["kT"][:, h, :], [RB["kT"], RB["kbT"]])
                P.op("dve", lambda e, psK2=psK2: e.scalar_tensor_tensor(out=B["XTf"][:], in0=psK2, scalar=-1.0, in1=B["Fm"][:],
                                                                        op0=ALU.mult, op1=ALU.mult),
                     reads=[rpsK2, RB["Fm"]], writes=[RB["XTf"]])
                if wout:
                    tt("pool", "Ea", B["Ee"][:], bc_mid(m_i), ALU.mult, [RB["Ee"], R_gc])
                    psA, rpsA = mm4(lambda h: B["kT"][:, h, :], lambda h: B["qT"][:, h, :], [RB["kT"], RB["qT"]])
                    tt("dve", "AT", psA, B["Ea"][:], ALU.mult, [rpsA, RB["Ea"]])
                tt("pool", "X0", B["Xf"][:], bc_mid(msk4[:, 4, :]), ALU.mult, [RB["Xf"], R_gc])
                tt("pool", "XT0", B["XTf"][:], bc_mid(msk4[:, 4, :]), ALU.mult, [RB["XTf"], R_gc])
                tt("dve", "R0", B["X0"][:], bc_mid(ident[:]), ALU.add, [RB["X0"], R_ident])
                tt("dve", "Q0", B["XT0"][:], bc_mid(ident[:]), ALU.add, [RB["XT0"], R_ident])
                for lvl in range(1, 4):
                    a, b_ = (lvl - 1) % 2, lvl % 2
                    Xa, XTa, Xb, XTb = f"X{a}", f"XT{a}", f"X{b_}", f"XT{b_}"
                    Ra, Rb, Qa, Qb = f"R{a}", f"R{b_}", f"Q{a}", f"Q{b_}"
                    pX, rpX = mm4(lambda h, XTa=XTa: B[XTa][:, h, :], lambda h, Xa=Xa: B[Xa][:, h, :], [RB[Xa], RB[XTa]])
                    pXT, rpXT = mm4(lambda h, Xa=Xa: B[Xa][:, h, :], lambda h, XTa=XTa: B[XTa][:, h, :], [RB[Xa], RB[XTa]])
                    copy4(pX, rpX, Xb, "act")
                    copy4(pXT, rpXT, XTb, "dve")
                    pR, rpR = mm4(lambda h, XTb=XTb: B[XTb][:, h, :], lambda h, Ra=Ra: B[Ra][:, h, :], [RB[XTb], RB[Ra]])
                    tt("dve", Rb, pR, B[Ra][:], ALU.add, [rpR, RB[Ra]])
                    pQ, rpQ = mm4(lambda h, Xb=Xb: B[Xb][:, h, :], lambda h, Qa=Qa: B[Qa][:, h, :], [RB[Xb], RB[Qa]])
                    tt("dve", Qb, pQ, B[Qa][:], ALU.add, [rpQ, RB[Qa]])
                curb = 1
                for si in range(3):
                    last = si == 2
                    offm = msk4[:, 5 + si, :]
                    a, b_ = curb, 1 - curb
                    Ra, Rb, Qa, Qb = f"R{a}", f"R{b_}", f"Q{a}", f"Q{b_}"
                    tt("pool", "Cm", B["XTf"][:], bc_mid(offm), ALU.mult, [RB["XTf"], R_gc])
                    pW1, rpW1 = mm4(lambda h: B["Cm"][:, h, :], lambda h, Ra=Ra: B[Ra][:, h, :], [RB["Cm"], RB[Ra]])
                    copy4(pW1, rpW1, "W1", "act")
                    pY, rpY = mm4(lambda h, Qa=Qa: B[Qa][:, h, :], lambda h: B["W1"][:, h, :], [RB[Qa], RB["W1"]])
                    tt("dve", Rb, pY, B[Ra][:], ALU.add, [rpY, RB[Ra]])
                    if not last:
                        tt("pool", "CmT", B["Xf"][:], bc_mid(offm), ALU.mult, [RB["Xf"], R_gc])
                        pW2, rpW2 = mm4(lambda h: B["CmT"][:, h, :], lambda h, Qa=Qa: B[Qa][:, h, :], [RB["CmT"], RB[Qa]])
                        copy4(pW2, rpW2, "W2", "dve")
                        pT, rpT = mm4(lambda h, Ra=Ra: B[Ra][:, h, :], lambda h: B["W2"][:, h, :], [RB[Ra], RB["W2"]])
                        tt("dve", Qb, pT, B[Qa][:], ALU.add, [rpT, RB[Qa]])
                    curb = b_
                assert curb == 0
                TT = "R0"
                pU, rpU = mm4(lambda h: B[TT][:, h, :], lambda h: B["vb"][:, h, :], [RB[TT], RB["vb"]])
                copy4(pU, rpU, "uu", "act")
                pW, rpW = mm4(lambda h: B["kbg"][:, h, :], lambda h: B[TT][:, h, :], [RB[TT], RB["kbg"]])
                copy4(pW, rpW, "wT", "dve")
                p1, rp1 = mm4(lambda h: B["wT"][:, h, :], lambda h: Sst[:, h, :], [RB["wT"], R_S])
                tt("dve", "vnew", B["uu"][:], p1, ALU.subtract, [RB["uu"], rp1])
                if wout:
                    p2, rp2 = mm4(lambda h: B["qT"][:, h, :], lambda h: Sst[:, h, :], [RB["qT"], R_S])
                    tt("dve", "o1", p2, bc_last(egc), ALU.mult, [rp2, R_sm2])
                    p3, rp3 = mm4(lambda h: B["AT"][:, h, :], lambda h: B["vnew"][:, h, :], [RB["AT"], RB["vnew"]])
                    tt("dve", "obf", B["o1"][:], p3, ALU.add, [RB["o1"], rp3])
                p4, rp4 = mm4(lambda h: B["ktl"][:, h, :], lambda h: B["vnew"][:, h, :], [RB["ktl"], RB["vnew"]])
                P.op("pool", lambda e: e.tensor_tensor(out=Sst[:], in0=Sst[:], in1=bc_last(etot), op=ALU.mult),
                     reads=[R_S, R_sm2], writes=[R_S])
                P.op("dve", lambda e, p4=p4: e.tensor_tensor(out=Sst[:], in0=Sst[:], in1=p4, op=ALU.add),
                     reads=[R_S, rp4], writes=[R_S])
                if wout:
                    obflat = B["obf"][:].rearrange("p h n -> p (h n)")
                    if d == 0:
                        P.dma("sp", ds_ob, OB[t0:t0 + 128, :], obflat, reads=[RB["obf"]], writes=[R_OB[tau]])
                    else:
                        P.dma("sp", ds_prev, B["prev"][:].rearrange("p h n -> p (h n)"), OB[t0:t0 + 128, :],
                              reads=[R_OB[tau]], writes=[RB["prev"]])
                        tt("pool", "obf", B["obf"][:], B["prev"][:], ALU.add, [RB["obf"], RB["prev"]])
                        P.dma("sp", ds_ob, OB[t0:t0 + 128, :], obflat, reads=[RB["obf"]], writes=[R_OB[tau]])
        P.flush()
        ph3.close()

    def phase4_group(E, g):
        tok0 = g * 512
        hx, zt, uTr, Gr = E.hx, E.zt, E.uTr, E.Gr
        for t in range(4):
            P.dma("sp", E.ds_hx[t], hx[:, t, :], X1[tok0 + t * 128:tok0 + (t + 1) * 128, :], writes=[E.R_hx[t]])
        for t in range(4):
            E.modulate_T(hx[:, t, :], E.R_hx[t], t, 3, 0)
        plan = [(win_d[24 + i], 1024) for i in range(4)]
        for c in range(8):
            plan += [(wpa_d[c], 512), (wpb_d[c], 512), (win_d[29 + c], 1024), (win_d[37 + c], 1024)]
        plan += [(wo_d[c], 1024) for c in range(8)]
        ws = E.WStream(plan)
        ures = E.R_uT[:4]
        u_rhs = lambda kc: uTr[:, kc, 0:512]
        for i in range(4):
            ps, rps = E.gemm_block(ws, 8, u_rhs, ures, 512)
            tb, rtb, _ = E.tmpB_ring.next()
            P.op("act", lambda e, tb=tb, ps=ps: e.activation(out=tb[:, :], in_=ps[:, :], func=AF.Copy),
                 reads=[rps], writes=[rtb])
            pt, rpt = E.tr_ps()
            for t in range(4):
                P.op("pe", lambda e, pt=pt, tb=tb, t=t: e.transpose(
                    out=pt[:, t * 128:(t + 1) * 128], in_=tb[:, t * 128:(t + 1) * 128], identity=ident[:]),
                    reads=[rtb, R_ident], writes=[rpt])
            P.op("act", lambda e, pt=pt, i=i: e.activation(
                out=E.zs[:, :, i * 128:(i + 1) * 128], in_=pt[:, :].rearrange("p (t c) -> p t c", t=4), func=AF.Silu),
                reads=[rpt], writes=[E.R_zs])
        for t in range(4):
            oat, roat, dsoa = E.oa_ring.next()
            obt, robt, dsob = E.ob_ring.next()
            r0 = tok0 + t * 128
            P.dma("sp", dsoa, oat, OA[r0:r0 + 128, :], writes=[roat])
            P.dma("sp", dsob, obt, OB[r0:r0 + 128, :], writes=[robt])
            obv = obt.rearrange("p (h d) -> p h d", h=4)
            P.op("dve", lambda e, obt=obt: e.tensor_tensor(out=E.sq[:], in0=obt, in1=obt, op=ALU.mult),
                 reads=[robt], writes=[E.R_sq])
            P.op("dve", lambda e: e.reduce_sum(out=E.ss[:, 0:4], in_=E.sq[:].rearrange("p (h d) -> p h d", h=4),
                                               axis=mybir.AxisListType.X), reads=[E.R_sq], writes=[E.R_ss])
            P.op("act", lambda e: e.activation(out=E.ss[:, 4:8], in_=E.ss[:, 0:4], func=AF.Sqrt, scale=1.0 / 128.0,
                                               bias=epsc[:, 1:2]), reads=[E.R_ss, R_ident], writes=[E.R_ss])
            P.op("dve", lambda e: e.reciprocal(out=E.ss[:, 4:8], in_=E.ss[:, 4:8]), reads=[E.R_ss], writes=[E.R_ss])
            P.op("dve", lambda e, obv=obv: e.tensor_tensor(
                out=obv, in0=obv, in1=E.ss[:, 4:8].unsqueeze(2).to_broadcast([128, 4, 128]), op=ALU.mult),
                reads=[robt, E.R_ss], writes=[robt])
            P.op("pool", lambda e, obv=obv: e.tensor_tensor(
                out=obv, in0=obv, in1=E.normw[:].unsqueeze(1).to_broadcast([128, 4, 128]), op=ALU.mult),
                reads=[robt, E.R_normw], writes=[robt])
            P.op("dve", lambda e, obt=obt, t=t: e.tensor_tensor(out=obt, in0=obt, in1=E.zs[:, t, :], op=ALU.mult),
                 reads=[robt, E.R_zs], writes=[robt])
            for (src, rsrc, base, eng) in ((oat, roat, 0, "act"), (obt, robt, 4, "dve")):
                pt, rpt = E.tr_ps()
                for kc in range(4):
                    P.op("pe", lambda e, pt=pt, src=src, kc=kc: e.transpose(
                        out=pt[:, kc * 128:(kc + 1) * 128], in_=src[:, kc * 128:(kc + 1) * 128], identity=ident[:]),
                        reads=[rsrc, R_ident], writes=[rpt])
                dst = Gr[:, base:base + 4, t * 128:(t + 1) * 128]
                src3 = pt[:, :].rearrange("p (k n) -> p k n", k=4)
                wr = [E.R_G[base + kc] for kc in range(4)]
                if eng == "act":
                    P.op("act", lambda e, dst=dst, src3=src3: e.activation(out=dst, in_=src3, func=AF.Copy),
                         reads=[rpt], writes=wr)
                else:
                    P.op("dve", lambda e, dst=dst, src3=src3: e.tensor_copy(out=dst, in_=src3), reads=[rpt], writes=wr)
        for c in range(8):
            psA, rpA = E.gemm_block(ws, 4, lambda kc: Gr[:, kc, 0:512], E.R_G[0:4], 512)
            psB, rpB = E.gemm_block(ws, 4, lambda kc: Gr[:, 4 + kc, 0:512], E.R_G[4:8], 512)
            psGa, rpGa = E.gemm_block(ws, 8, u_rhs, ures, 512)
            psGb, rpGb = E.gemm_block(ws, 8, u_rhs, ures, 512)
            ta1, rta1, _ = E.tmpA_ring.next()
            ta2, rta2, _ = E.tmpA_ring.next()
            P.op("act", lambda e, ta1=ta1, psGa=psGa: e.activation(out=ta1[:, :], in_=psGa[:, :], func=AF.Sigmoid),
                 reads=[rpGa], writes=[rta1])
            P.op("act", lambda e, ta2=ta2, psGb=psGb: e.activation(out=ta2[:, :], in_=psGb[:, :], func=AF.Sigmoid),
                 reads=[rpGb], writes=[rta2])
            P.op("dve", lambda e, ta1=ta1, psA=psA: e.tensor_tensor(out=ta1[:, :], in0=ta1[:, :], in1=psA[:, :], op=ALU.mult),
                 reads=[rta1, rpA], writes=[rta1])
            P.op("dve", lambda e, ta2=ta2, psB=psB: e.tensor_tensor(out=ta2[:, :], in0=ta2[:, :], in1=psB[:, :], op=ALU.mult),
                 reads=[rta2, rpB], writes=[rta2])
            P.op("pool", lambda e, ta1=ta1, ta2=ta2, c=c: e.tensor_tensor(out=Gr[:, 8 + c, :], in0=ta1[:, :], in1=ta2[:, :],
                                                                          op=ALU.add),
                 reads=[rta1, rta2], writes=[E.R_G[8 + c]])
        for c in range(8):
            ps, rps = E.gemm_block(ws, 8, lambda kc: Gr[:, 8 + kc, 0:512], E.R_G[8:16], 512)
            E.back_to_tokens(ps, rps, modT[:, 5 * 8 + c, 0:1], 4, hx, E.R_hx, zt, E.R_zt, c)
        for t in range(4):
            E.post_ln(zt[:, t, :], E.R_zt[t], 1)
        E.ffn_sublayer(4, 2, 0, wu2_d, wd2_d, 2, zt, E.R_zt, hx, E.R_hx)
        for t in range(4):
            P.dma("sp", E.ds_hx[t], out_d[tok0 + t * 128:tok0 + (t + 1) * 128, :], hx[:, t, :], reads=[E.R_hx[t]])

    if 4 in phases:
        ph4 = ExitStack()
        cur[0] = ph4
        E4 = make_env([1, 2], merge=True)
        E4.zs = sb("m_zs", [128, 4, 512])
        E4.R_zs = Res("zs")
        oa_all = sb("m_oa", [128, 2, 512])
        ob_all = sb("m_ob", [128, 2, 512])
        E4.oa_ring = Ring(P, "m_oa", [oa_all[:, i, :] for i in range(2)])
        E4.ob_ring = Ring(P, "m_ob", [ob_all[:, i, :] for i in range(2)])
        E4.sq = sb("m_sq", [128, 512])
        E4.R_sq = Res("msq")
        E4.ss = sb("m_ss", [128, 8])
        E4.R_ss = Res("mss")
        E4.normw = sb("m_normw", [128, 128])
        E4.R_normw = Res("normw")
        P.dma("sp", ds_misc, E4.normw[:], normw_d[:, :], writes=[E4.R_normw])
        for g in m_groups:
            phase4_group(E4, g)
        P.flush()
        ph4.close()
    P.flush()
    st.close()
    return nc, P


def host_inputs(inputs, b):
    f = np.float32
    g = lambda k: np.asarray(inputs[k], dtype=f)
    m = {}
    m["x"] = np.ascontiguousarray(g("x")[b])
    m["ctx"] = np.ascontiguousarray(g("ctx")[b])
    cT = np.stack([g("c")[b].reshape(8, 128).T, g("c_ctx").reshape(8, 128).T], axis=-1)
    m["cT"] = np.ascontiguousarray(cT)
    m["w_ada"] = np.ascontiguousarray(g("w_ada")[0])
    m["b_adaT"] = np.ascontiguousarray(g("b_ada")[0].reshape(72, 128).T)
    m["lngb"] = np.ascontiguousarray(np.concatenate([g("ln_g")[0], g("ln_b")[0]], axis=0))
    m["ident"] = np.eye(128, dtype=f)
    m["wu1"] = to_blocks(g("ffn1_w_in")[0])
    m["wd1"] = to_blocks(g("ffn1_w_out")[0])
    w_in = g("w_in")[0]
    pad = np.zeros((D, 112), f)
    w_in_p = np.concatenate([w_in[:, :3584], w_in[:, 3584:3600], pad, w_in[:, 3600:]], axis=1)
    m["win"] = to_blocks(w_in_p)
    m["wpa"] = to_blocks(g("w_pa")[0])
    m["wpb"] = to_blocks(g("w_pb")[0])
    m["wo"] = to_blocks(g("w_o")[0])
    m["wu2"] = to_blocks(g("ffn2_w_in")[0])
    m["wd2"] = to_blocks(g("ffn2_w_out")[0])
    if "tabs" not in _NA_CACHE:
        _NA_CACHE["tabs"] = na_tables()
    ridx, cidx, mask = _NA_CACHE["tabs"]
    rpb = g("na_rpb")[0]
    tab = rpb[:, ridx, cidx]
    m["na_tab"] = np.ascontiguousarray(tab.transpose(0, 2, 1, 3))
    m["na_mask"] = np.ascontiguousarray(mask.transpose(1, 0, 2))
    cw = g("gdn_conv_w")[0]
    m["g_convw"] = np.ascontiguousarray(cw.reshape(5, 12, 128).transpose(2, 1, 0))
    if "gconst" not in _NA_CACHE:
        _NA_CACHE["gconst"] = gdn_consts()
    gmask, rope = _NA_CACHE["gconst"]
    m["g_mask"] = gmask
    m["g_rope"] = rope
    adt = np.concatenate([g("gdn_a_log")[0].reshape(8), g("gdn_dt_bias")[0].reshape(8)])
    m["g_adt"] = np.ascontiguousarray(np.broadcast_to(adt[None, :], (128, 16)))
    m["g_normw"] = np.ascontiguousarray(np.broadcast_to(g("gdn_norm_w")[0][None, :], (128, 128)))
    return m


def gdn_consts():
    r = np.arange(128)[:, None]
    c = np.arange(128)[None, :]
    def same(n):
        return (r // n) == (c // n)
    gmask = np.stack([c > r, c >= r, c < r, c <= r, same(16), same(32) & ~same(16), same(64) & ~same(32),
                      ~same(64)], axis=1).astype(np.float32)
    n_freq = 32
    freqs = (10000.0 ** (-np.arange(n_freq, dtype=np.float32) / n_freq)).astype(np.float32)
    t = np.arange(SEQ)
    pos = np.stack([t // GRID_W, t % GRID_W], axis=-1).astype(np.float32)
    ang = (pos[:, :, None] * freqs).astype(np.float32)
    cs = np.stack([np.cos(ang), np.sin(ang)], axis=1).astype(np.float32)
    rope = cs.reshape(32, 128, 128).transpose(1, 0, 2)
    return np.ascontiguousarray(gmask), np.ascontiguousarray(rope)


def kernel(**inputs):
    nc, _ = build_program()
    shared = host_inputs(inputs, 0)
    in_maps = []
    for b in range(8):
        m = dict(shared)
        if b > 0:
            per = host_inputs_core(inputs, b)
            m.update(per)
        in_maps.append(m)
    res = run_bass_kernel_spmd(nc, in_maps, core_ids=list(range(8)))
    return np.stack([np.asarray(r["out"], dtype=np.float32) for r in res.results], axis=0)


def host_inputs_core(inputs, b):
    f = np.float32
    g = lambda k: np.asarray(inputs[k], dtype=f)
    m = {}
    m["x"] = np.ascontiguousarray(g("x")[b])
    m["ctx"] = np.ascontiguousarray(g("ctx")[b])
    cT = np.stack([g("c")[b].reshape(8, 128).T, g("c_ctx").reshape(8, 128).T], axis=-1)
    m["cT"] = np.ascontiguousarray(cT)
    return m
```

```python
import numpy as np
from contextlib import ExitStack
import concourse.bass as bass
import concourse.mybir as mybir
from concourse.bass_utils import run_bass_kernel_spmd

F32 = mybir.dt.float32
F32R = mybir.dt.float32r
AF = mybir.ActivationFunctionType
ALU = mybir.AluOpType

ENGS = ("pe", "act", "dve", "pool", "sp")


class Res:
    __slots__ = ("name", "w", "r")

    def __init__(self, name=""):
        self.name = name
        self.w = None
        self.r = {}


class DSem:
    __slots__ = ("name", "total", "handle")

    def __init__(self, name):
        self.name = name
        self.total = 0
        self.handle = None


class OpRec:
    __slots__ = ("eng", "fn", "waits", "signal", "sigidx", "dsem", "dval", "cwaits", "retired")

    def __init__(self, eng, fn):
        self.eng = eng
        self.fn = fn
        self.waits = []
        self.cwaits = []
        self.signal = False
        self.sigidx = 0
        self.dsem = None
        self.dval = 0
        self.retired = False


class Prog:
    def __init__(self, nc, stack):
        self.nc = nc
        self.stack = stack
        self.ops = {e: [] for e in ENGS}
        self.dsems = []
        self.nops = 0
        self.esem = {e: stack.enter_context(nc.semaphore("s_" + e)) for e in ENGS}
        self.sigcount = {e: 0 for e in ENGS}
        self.known = {e: {} for e in ENGS}

    def dsem(self, name):
        d = DSem(name)
        self.dsems.append(d)
        return d

    def _deps(self, rec, reads, writes):
        eng = rec.eng
        is_dma = rec.dsem is not None
        deps = []
        for r in reads:
            if r.w is not None:
                deps.append((r.w, True))
        for w in writes:
            if w.w is not None:
                deps.append((w.w, False))
            for o in w.r.values():
                deps.append((o, False))
        for o, raw in deps:
            if o.retired:
                continue
            if o.dsem is not None:
                rec.waits.append((o.dsem, max(o.dsem.total, o.dval)))
                continue
            if o.eng == eng and not is_dma:
                if eng == "pe":
                    continue
                if not raw:
                    continue
            o.signal = True
            rec.cwaits.append(o)
        rkey = ("dma", id(rec.dsem)) if is_dma else eng
        for r in reads:
            r.r[rkey] = rec
        for w in writes:
            w.w = rec
            w.r = {}

    def op(self, eng, fn, reads=(), writes=()):
        rec = OpRec(eng, fn)
        self._deps(rec, reads, writes)
        self.ops[eng].append(rec)
        self.nops += 1
        return rec

    def dma(self, queue, dsem, out, in_, reads=(), writes=(), **kw):
        rec = OpRec(queue, lambda e: e.dma_start(out=out, in_=in_, **kw))
        rec.dsem = dsem
        self._deps(rec, reads, writes)
        dsem.total += 16
        rec.dval = dsem.total
        self.ops[queue].append(rec)
        self.nops += 1
        return rec

    def barrier(self):
        last = {}
        for e in ENGS:
            for rec in reversed(self.ops[e]):
                if rec.dsem is None and rec.fn is not None:
                    rec.signal = True
                    last[e] = rec
                    break
        dw = [(d, d.total) for d in self.dsems if d.total > 0]
        for e in ENGS:
            rec = OpRec(e, None)
            rec.waits = list(dw)
            rec.cwaits = [o for k, o in last.items() if k != e]
            self.ops[e].append(rec)

    def flush(self):
        self.barrier()
        nc = self.nc
        esem = self.esem
        for d in self.dsems:
            if d.handle is None:
                d.handle = self.stack.enter_context(nc.semaphore("d_" + d.name))
        for e in ENGS:
            for rec in self.ops[e]:
                if rec.signal and rec.dsem is None and rec.fn is not None:
                    self.sigcount[e] += 1
                    rec.sigidx = self.sigcount[e]

        def run(e, engobj):
            known = self.known[e]
            for rec in self.ops[e]:
                ws = []
                for (d, v) in rec.waits:
                    ws.append((d.handle, id(d), v))
                for o in rec.cwaits:
                    ws.append((esem[o.eng], o.eng, o.sigidx))
                for (h, key, v) in ws:
                    if known.get(key, 0) >= v:
                        continue
                    known[key] = v
                    engobj.wait_ge(h, v)
                if rec.fn is None:
                    continue
                ins = rec.fn(engobj)
                if rec.dsem is not None:
                    ins.then_inc(rec.dsem.handle, 16)
                elif rec.signal:
                    ins.then_inc(esem[e], 1)

        with nc.Block() as block:
            @block.tensor
            def _(pe):
                run("pe", pe)

            @block.scalar
            def _(act):
                run("act", act)

            @block.vector
            def _(dve):
                run("dve", dve)

            @block.gpsimd
            def _(pool):
                run("pool", pool)

            @block.sync
            def _(sp):
                run("sp", sp)
        for e in ENGS:
            for rec in self.ops[e]:
                rec.retired = True
            self.ops[e] = []


class Ring:
    def __init__(self, P, name, aps, with_dsem=True):
        self.slots = []
        for i, ap in enumerate(aps):
            self.slots.append((ap, Res(f"{name}{i}"), P.dsem(f"{name}{i}") if with_dsem else None))
        self.i = 0

    def next(self):
        s = self.slots[self.i % len(self.slots)]
        self.i += 1
        return s


D = 1024
DFF = 2816
NJ = DFF // 128
SEQ = 4096
CTX = 256
NTOK = SEQ + CTX
GRID_W = 64
LN_EPS = 1e-6
NORM_EPS = 1e-6
ALPHA = 2.0 ** 0.25
NEG = -30000.0
N_WIN_BLK = 45


def to_blocks(W):
    K, N = W.shape
    KC, NCB = K // 128, N // 128
    return np.ascontiguousarray(
        W.reshape(KC, 128, NCB, 128).transpose(2, 1, 0, 3).reshape(NCB, 128, KC * 128))


def na_tables():
    def r0(r):
        return min(max(r - 4, 0), 56)

    def c0(c):
        return min(max(c - 8, 0), 48)
    specs = [(2, j) for j in range(0, 5)] + [(0, j) for j in range(4)] + [(1, j) for j in range(4)] \
        + [(30, j) for j in range(28, 32)] + [(31, j) for j in range(28, 32)]
    ridx = np.zeros((21, 128, 128), np.int64)
    cidx = np.zeros((21, 128, 128), np.int64)
    mask = np.full((21, 128, 128), NEG, np.float32)
    for ti, (b, j) in enumerate(specs):
        for k in range(128):
            krow, kcol = 2 * j + k // 64, k % 64
            for q in range(128):
                qrow, qcol = 2 * b + q // 64, q % 64
                ok = (r0(qrow) <= krow < r0(qrow) + 8) and (c0(qcol) <= kcol < c0(qcol) + 16)
                if ok:
                    ridx[ti, k, q] = krow - qrow + 7
                    cidx[ti, k, q] = kcol - qcol + 15
                    mask[ti, k, q] = 0.0
    return ridx, cidx, mask


def na_chunks(b):
    if b == 0:
        return [(j, 5 + j) for j in range(4)]
    if b == 1:
        return [(j, 9 + j) for j in range(4)]
    if b == 30:
        return [(j, 13 + j - 28) for j in range(28, 32)]
    if b == 31:
        return [(j, 17 + j - 28) for j in range(28, 32)]
    return [(b - 2 + d, d) for d in range(5)]


_NA_CACHE = {}


def build_program(dbg=None, groups=tuple(range(9)), phases=(1, 2, 3, 4), na_hps=(0, 1, 2, 3), m_groups=tuple(range(8))):
    nc = bass.Bass("TRN2", target_bir_lowering=False)
    dbg = dbg or set()
    uid = [0]

    def din(name, shape, dt=F32):
        return nc.dram_tensor(name, list(shape), dt, kind="ExternalInput").ap()

    def dscr(name, shape, dt=F32):
        kind = "ExternalOutput" if name in dbg else "Internal"
        return nc.dram_tensor(name, list(shape), dt, kind=kind).ap()

    x_d = din("x", [SEQ, D])
    ctx_d = din("ctx", [CTX, D])
    cT_d = din("cT", [128, 8, 2])
    wada_d = din("w_ada", [D, 9 * D])
    badaT_d = din("b_adaT", [128, 72])
    lngb_d = din("lngb", [6, D])
    ident_d = din("ident", [128, 128])
    wu1_d = din("wu1", [44, 128, 1024], F32R)
    wd1_d = din("wd1", [8, 128, 2816], F32R)
    win_d = din("win", [N_WIN_BLK, 128, 1024], F32R)
    wpa_d = din("wpa", [8, 128, 512], F32R)
    wpb_d = din("wpb", [8, 128, 512], F32R)
    wo_d = din("wo", [8, 128, 1024], F32R)
    wu2_d = din("wu2", [44, 128, 1024], F32R)
    wd2_d = din("wd2", [8, 128, 2816], F32R)
    natab_d = din("na_tab", [8, 128, 21, 128])
    namask_d = din("na_mask", [128, 21, 128])
    convw_d = din("g_convw", [128, 12, 5])
    gmask_d = din("g_mask", [128, 8, 128])
    rope_d = din("g_rope", [128, 32, 128])
    adt_d = din("g_adt", [128, 16])
    normw_d = din("g_normw", [128, 128])
    out_d = nc.dram_tensor("out", [SEQ, D], F32, kind="ExternalOutput").ap()

    X1 = dscr("X1", [SEQ, D])
    QAT = dscr("QAT", [512, SEQ])
    KAT = dscr("KAT", [512, NTOK])
    VA = dscr("VA", [NTOK, 512])
    QKVT = dscr("QKVT", [1536, NTOK])
    AB = dscr("AB", [NTOK, 16])
    OA = dscr("OA", [SEQ, 512])
    OB = dscr("OB", [SEQ, 512])
    QKVN = dscr("QKVN", [NTOK, 1536])
    GB = dscr("GB", [NTOK, 16])
    DBG1 = dscr("DBG1", [128, 144])

    st = ExitStack()
    P = Prog(nc, st)
    cur = [st]

    def sb(name, shape, dt=F32):
        uid[0] += 1
        return cur[0].enter_context(nc.sbuf_tensor(f"sb{uid[0]}_{name}", list(shape), dt))

    ident = sb("ident", [128, 128])
    modT = sb("modT", [128, 72, 2])
    mod1p = sb("mod1p", [128, 72, 2])
    modh = sb("modh", [128, 72, 2])
    scT = sb("scT", [128, 8, 2])
    badaT = sb("badaT", [128, 72])
    stats = sb("stats", [128, 8, 16])
    epsc = sb("epsc", [128, 2])
    R_ident, R_mod = Res("ident"), Res("mod")
    P.op("dve", lambda e: e.memset(epsc[:, 0:1], LN_EPS), writes=[R_ident])
    P.op("dve", lambda e: e.memset(epsc[:, 1:2], NORM_EPS), writes=[R_ident])
    ds_misc = P.dsem("misc")

    psum = [st.enter_context(nc.psum_tensor(f"ps{i}", [128, 512], F32)) for i in range(8)]
    R_ps = [Res(f"ps{i}") for i in range(8)]

    ph0 = ExitStack()
    cur[0] = ph0
    wa_all = sb("wa_all", [128, 2, 6144])
    P.dma("sp", ds_misc, ident[:], ident_d[:, :], writes=[R_ident])
    P.dma("sp", ds_misc, scT[:], cT_d[:, :, :], writes=[R_mod])
    P.dma("sp", ds_misc, badaT[:], badaT_d[:, :], writes=[R_mod])
    P.op("act", lambda e: e.activation(out=scT[:], in_=scT[:], func=AF.Silu), reads=[R_mod], writes=[R_mod])
    wa_bufs = [wa_all[:, i, :] for i in range(2)]
    wa_ring = Ring(P, "wa", wa_bufs)
    for pn in range(12):
        ap, res, ds = wa_ring.next()
        apv = ap.rearrange("p (k n) -> p k n", k=8)
        P.dma("sp", ds, apv, wada_d[:, pn * 768:(pn + 1) * 768].rearrange("(k p) n -> p k n", p=128), writes=[res])
        for cc in range(6):
            ch = pn * 6 + cc
            for kc in range(8):
                P.op("pe", lambda e, apv=apv, cc=cc, ch=ch, kc=kc: e.matmul(
                    psum[7][:, ch * 2:ch * 2 + 2], lhsT=apv[:, kc, cc * 128:(cc + 1) * 128], rhs=scT[:, kc, :],
                    start=(kc == 0), stop=(kc == 7)), reads=[res, R_mod], writes=[R_ps[7]])
    ps7v = psum[7][:, 0:144].rearrange("p (c t) -> p c t", t=2)
    for t in range(2):
        P.op("dve", lambda e, t=t: e.tensor_tensor(out=modT[:, :, t], in0=ps7v[:, :, t], in1=badaT[:], op=ALU.add),
             reads=[R_ps[7], R_mod], writes=[R_mod])
    P.op("dve", lambda e: e.tensor_scalar_add(out=mod1p[:], in0=modT[:], scalar1=1.0), reads=[R_mod], writes=[R_mod])
    P.op("dve", lambda e: e.tensor_scalar_mul(out=modh[:], in0=modT[:], scalar1=0.5), reads=[R_mod], writes=[R_mod])
    if "DBG1" in dbg:
        P.dma("sp", ds_misc, DBG1[:, :], modT[:].rearrange("p c t -> p (c t)"), reads=[R_mod])
    P.flush()
    ph0.close()

    stat_i = [0]

    def ln_normalize(src, rsrc, dst, rdst):
        k = stat_i[0] % 8
        stat_i[0] += 1
        s = stats[:, k, :]
        rs = Res("st")
        P.op("dve", lambda e: e.bn_stats(out=s[:, 0:6], in_=src[:, 0:512]), reads=[rsrc], writes=[rs])
        P.op("dve", lambda e: e.bn_stats(out=s[:, 6:12], in_=src[:, 512:1024]), reads=[rsrc], writes=[rs])
        P.op("dve", lambda e: e.bn_aggr(out=s[:, 12:14], in_=s[:, 0:12]), reads=[rs], writes=[rs])
        P.op("act", lambda e: e.activation(out=s[:, 14:15], in_=s[:, 13:14], func=AF.Sqrt, bias=epsc[:, 0:1]),
             reads=[rs, R_ident], writes=[rs])
        P.op("dve", lambda e: e.reciprocal(out=s[:, 14:15], in_=s[:, 14:15]), reads=[rs], writes=[rs])
        P.op("dve", lambda e: e.scalar_tensor_tensor(out=s[:, 15:16], in0=s[:, 12:13], scalar=-1.0, in1=s[:, 14:15],
                                                     op0=ALU.mult, op1=ALU.mult), reads=[rs], writes=[rs])
        P.op("act", lambda e: e.activation(out=dst, in_=src, func=AF.Identity, scale=s[:, 14:15], bias=s[:, 15:16]),
             reads=[rsrc, rs], writes=[rdst])

    class Env:
        pass

    def make_env(rows, merge=False):
        E = Env()
        nr = len(rows)
        lngb = sb("lngb", [128, 2 * nr, D])
        R_lngb = Res("lngb")
        for i, r in enumerate(rows):
            P.dma("sp", ds_misc, lngb[:, i, :], lngb_d[r].partition_broadcast(128), writes=[R_lngb])
            P.dma("sp", ds_misc, lngb[:, nr + i, :], lngb_d[3 + r].partition_broadcast(128), writes=[R_lngb])
        lnslot = {r: i for i, r in enumerate(rows)}
        hx = sb("hx", [128, 4, D])
        zt = sb("zt", [128, 4, D])
        xn_all = sb("xn", [128, 2, D])
        xn = [xn_all[:, i, :] for i in range(2)]
        uTr = sb("uT", [128, 8, 512], F32R)
        Gr = sb("G", [128, NJ, 512], F32R)
        wb_all = sb("wb", [128, 2, 2816], F32R)
        wbs_all = sb("wbs", [128, 6, 1024], F32R)
        wowner = {}
        tmpA_all = sb("tmpA", [128, 3, 512])
        tmpB_all = sb("tmpB", [128, 3, 512])
        R_hx = [Res(f"hx{t}") for t in range(4)]
        R_zt = [Res(f"zt{t}") for t in range(4)]
        R_xn = [Res("xn0"), Res("xn1")]
        R_uT = [Res(f"uT{t}") for t in range(4)]
        R_G = [Res(f"G{j}") for j in range(NJ)]
        tag = "m" if merge else "f"
        ds_hx = [P.dsem(f"{tag}hx{t}") for t in range(4)]
        ds_zt = [P.dsem(f"{tag}zt{t}") for t in range(4)]
        wring = Ring(P, tag + "wb", [wb_all[:, i, :] for i in range(2)])
        wring_s = Ring(P, tag + "wbs", [wbs_all[:, i, :] for i in range(6)])
        tmpA_ring = Ring(P, tag + "tA", [tmpA_all[:, i, :] for i in range(3)])
        tmpB_ring = Ring(P, tag + "tB", [tmpB_all[:, i, :] for i in range(3)])
        xn_i = [0]
        ps_gemm_i = [0]
        ps_tr_i = [0]

        def gemm_ps():
            k = ps_gemm_i[0] % 4
            ps_gemm_i[0] += 1
            return psum[k], R_ps[k]

        def tr_ps():
            k = 4 + ps_tr_i[0] % 3
            ps_tr_i[0] += 1
            return psum[k], R_ps[k]

        class WStream:
            def __init__(self, plan, depth=5):
                self.plan = plan
                self.loaded = []
                self.i = 0
                self.npop = 0
                self.depth = depth
                self.pre()

            def pre(self):
                while len(self.loaded) < self.depth and self.i < len(self.plan):
                    src, ncol = self.plan[self.i]
                    ring = wring_s if ncol <= 1024 else wring
                    slot = ring.slots[ring.i % len(ring.slots)]
                    owner = wowner.get(id(slot[1]))
                    if owner is not None and owner[0] is self and owner[1] >= self.npop:
                        break
                    ap, res, ds = ring.next()
                    wowner[id(res)] = (self, self.i)
                    self.i += 1
                    P.dma("pool", ds, ap[:, 0:ncol], src, writes=[res])
                    self.loaded.append((ap, res))

            def get(self):
                self.pre()
                ap, res = self.loaded.pop(0)
                self.npop += 1
                return ap, res

        def modulate_T(src, rsrc, t, base, sel):
            k = xn_i[0] % 2
            xn_i[0] += 1
            ln_normalize(src, rsrc, xn[k], R_xn[k])
            for half in range(2):
                ps, rps = tr_ps()
                for q in range(4):
                    kc = half * 4 + q
                    P.op("pe", lambda e, ps=ps, q=q, kc=kc, k=k: e.transpose(
                        out=ps[:, q * 128:(q + 1) * 128], in_=xn[k][:, kc * 128:(kc + 1) * 128], identity=ident[:]),
                        reads=[R_xn[k], R_ident], writes=[rps])
                for q in range(4):
                    kc = half * 4 + q
                    ci_shift = base * 8 + kc
                    ci_scale = (base + 1) * 8 + kc
                    dst = uTr[:, kc, t * 128:(t + 1) * 128]
                    if q % 2 == 0:
                        P.op("act", lambda e, ps=ps, q=q, dst=dst, a=ci_scale, b=ci_shift: e.activation(
                            out=dst, in_=ps[:, q * 128:(q + 1) * 128], func=AF.Identity,
                            scale=mod1p[:, a, sel:sel + 1], bias=modT[:, b, sel:sel + 1]),
                            reads=[rps, R_mod], writes=[R_uT[t]])
                    else:
                        P.op("dve", lambda e, ps=ps, q=q, dst=dst, a=ci_scale, b=ci_shift: e.tensor_scalar(
                            out=dst, in0=ps[:, q * 128:(q + 1) * 128], scalar1=mod1p[:, a, sel:sel + 1],
                            scalar2=modT[:, b, sel:sel + 1], op0=ALU.mult, op1=ALU.add),
                            reads=[rps, R_mod], writes=[R_uT[t]])

        def gemm_block(ws, KC, rhs_fn, rhs_res, ntok):
            wap, wres = ws.get()
            ps, rps = gemm_ps()
            for kc in range(KC):
                P.op("pe", lambda e, ps=ps, wap=wap, kc=kc: e.matmul(
                    ps[:, 0:ntok], lhsT=wap[:, kc * 128:(kc + 1) * 128], rhs=rhs_fn(kc),
                    start=(kc == 0), stop=(kc == KC - 1)), reads=[wres] + list(rhs_res), writes=[rps])
            return ps, rps

        def post_ln(buf, rbuf, lnrow):
            i = lnslot[lnrow]
            ln_normalize(buf, rbuf, buf, rbuf)
            P.op("dve", lambda e: e.tensor_tensor(out=buf, in0=buf, in1=lngb[:, i, :], op=ALU.mult),
                 reads=[rbuf, R_lngb], writes=[rbuf])
            P.op("dve", lambda e: e.tensor_tensor(out=buf, in0=buf, in1=lngb[:, nr + i, :], op=ALU.add),
                 reads=[rbuf, R_lngb], writes=[rbuf])

        def back_to_tokens(ps, rps, scale_ap, ntile, src, rsrc, dst, rdst, c):
            ntok = ntile * 128
            tb, rtb, _ = tmpB_ring.next()
            P.op("act", lambda e: e.activation(out=tb[:, 0:ntok], in_=ps[:, 0:ntok], func=AF.Identity, scale=scale_ap),
                 reads=[rps, R_mod], writes=[rtb])
            pt, rpt = tr_ps()
            for t in range(ntile):
                P.op("pe", lambda e, t=t: e.transpose(
                    out=pt[:, t * 128:(t + 1) * 128], in_=tb[:, t * 128:(t + 1) * 128], identity=ident[:]),
                    reads=[rtb, R_ident], writes=[rpt])
            P.op("dve", lambda e: e.scalar_tensor_tensor(
                out=dst[:, 0:ntile, c * 128:(c + 1) * 128], in0=src[:, 0:ntile, c * 128:(c + 1) * 128], scalar=ALPHA,
                in1=pt[:, 0:ntok].rearrange("p (t d) -> p t d", t=ntile), op0=ALU.mult, op1=ALU.add),
                reads=[rpt] + rsrc[:ntile], writes=rdst[:ntile])

        def ffn_sublayer(ntile, sub, sel, wu_d, wd_d, lnrow, src, rsrc, dst, rdst):
            ntok = ntile * 128
            base = 3 * sub
            for t in range(ntile):
                modulate_T(src[:, t, :], rsrc[t], t, base, sel)
            plan = []
            for j in range(NJ):
                plan.append((wu_d[j], 1024))
                plan.append((wu_d[NJ + j], 1024))
            for c in range(8):
                plan.append((wd_d[c], 2816))
            ws = WStream(plan)
            ures = R_uT[:ntile]
            for j in range(NJ):
                psa, rpa = gemm_block(ws, 8, lambda kc: uTr[:, kc, 0:ntok], ures, ntok)
                psb, rpb = gemm_block(ws, 8, lambda kc: uTr[:, kc, 0:ntok], ures, ntok)
                ta, rta, _ = tmpA_ring.next()
                P.op("act", lambda e, ta=ta, psa=psa: e.activation(out=ta[:, 0:ntok], in_=psa[:, 0:ntok], func=AF.Silu),
                     reads=[rpa], writes=[rta])
                P.op("dve", lambda e, ta=ta, psb=psb, j=j: e.tensor_tensor(
                    out=Gr[:, j, 0:ntok], in0=ta[:, 0:ntok], in1=psb[:, 0:ntok], op=ALU.mult),
                    reads=[rta, rpb], writes=[R_G[j]])
            for c in range(8):
                ps, rps = gemm_block(ws, NJ, lambda kc: Gr[:, kc, 0:ntok], R_G, ntok)
                gi = (base + 2) * 8 + c
                back_to_tokens(ps, rps, modh[:, gi, sel:sel + 1], ntile, src, rsrc, dst, rdst, c)
            for t in range(ntile):
                post_ln(dst[:, t, :], rdst[t], lnrow)

        E.__dict__.update(locals())
        return E

    def phase1_group(E, g):
        is_ctx = (g == 8)
        ntile = 2 if is_ctx else 4
        ntok = ntile * 128
        sel = 1 if is_ctx else 0
        tok0 = SEQ if is_ctx else g * 512
        hx, zt, uTr = E.hx, E.zt, E.uTr
        for t in range(ntile):
            src = ctx_d[t * 128:(t + 1) * 128, :] if is_ctx else x_d[g * 512 + t * 128:g * 512 + (t + 1) * 128, :]
            P.dma("sp", E.ds_hx[t], hx[:, t, :], src, writes=[E.R_hx[t]])
        E.ffn_sublayer(ntile, 0, sel, wu1_d, wd1_d, 0, hx, E.R_hx, zt, E.R_zt)
        if not is_ctx:
            for t in range(ntile):
                P.dma("sp", E.ds_zt[t], X1[tok0 + t * 128:tok0 + (t + 1) * 128, :], zt[:, t, :], reads=[E.R_zt[t]])
        for t in range(ntile):
            E.modulate_T(zt[:, t, :], E.R_zt[t], t, 3, sel)
        blocks = list(range(0 if not is_ctx else 4, 24)) + [28]
        ws = E.WStream([(win_d[cb], 1024) for cb in blocks])
        ures = E.R_uT[:ntile]
        for cb in blocks:
            ps, rps = E.gemm_block(ws, 8, lambda kc: uTr[:, kc, 0:ntok], ures, ntok)
            tb, rtb, dsb = E.tmpB_ring.next()
            if cb < 4:
                P.op("act", lambda e, tb=tb, ps=ps: e.activation(out=tb[:, 0:ntok], in_=ps[:, 0:ntok], func=AF.Copy,
                                                                 scale=0.125), reads=[rps], writes=[rtb])
                P.dma("sp", dsb, QAT[cb * 128:(cb + 1) * 128, tok0:tok0 + ntok], tb[:, 0:ntok], reads=[rtb])
            elif cb < 8 or (12 <= cb < 24):
                P.op("dve", lambda e, tb=tb, ps=ps: e.tensor_copy(out=tb[:, 0:ntok], in_=ps[:, 0:ntok]),
                     reads=[rps], writes=[rtb])
                if cb < 8:
                    dst = KAT[(cb - 4) * 128:(cb - 3) * 128, tok0:tok0 + ntok]
                else:
                    dst = QKVT[(cb - 12) * 128:(cb - 11) * 128, tok0:tok0 + ntok]
                P.dma("sp", dsb, dst, tb[:, 0:ntok], reads=[rtb])
            else:
                P.op("act", lambda e, tb=tb, ps=ps: e.activation(out=tb[:, 0:ntok], in_=ps[:, 0:ntok], func=AF.Copy),
                     reads=[rps], writes=[rtb])
                pt, rpt = E.tr_ps()
                for t in range(ntile):
                    P.op("pe", lambda e, pt=pt, tb=tb, t=t: e.transpose(
                        out=pt[:, t * 128:(t + 1) * 128], in_=tb[:, t * 128:(t + 1) * 128], identity=ident[:]),
                        reads=[rtb, R_ident], writes=[rpt])
                ta, rta, dsa = E.tmpA_ring.next()
                P.op("dve", lambda e, ta=ta, pt=pt: e.tensor_copy(out=ta[:, 0:ntok], in_=pt[:, 0:ntok]),
                     reads=[rpt], writes=[rta])
                tav = ta[:, 0:ntok].rearrange("p (t c) -> p t c", t=ntile)
                if cb == 28:
                    dst = AB[tok0:tok0 + ntok, :].rearrange("(t p) c -> p t c", p=128)
                    P.dma("sp", dsa, dst, tav[:, :, 0:16], reads=[rta])
                else:
                    dst = VA[tok0:tok0 + ntok, (cb - 8) * 128:(cb - 7) * 128].rearrange("(t p) c -> p t c", p=128)
                    P.dma("sp", dsa, dst, tav, reads=[rta])

    if 1 in phases:
        ph1 = ExitStack()
        cur[0] = ph1
        E1 = make_env([0])
        for g in groups:
            phase1_group(E1, g)
        P.flush()
        ph1.close()

    if 2 in phases:
        ph2 = ExitStack()
        cur[0] = ph2
        KT = sb("naK", [128, NTOK])
        QT = sb("naQ", [128, SEQ])
        vv = sb("naV", [128, 34, 2, 65])
        tabs = sb("naT", [128, 2, 21, 128])
        msk = sb("naM", [128, 21, 128])
        pT_all = sb("naP", [128, 6, 128])
        sT_all = sb("naS", [128, 4, 128])
        ob_all = sb("naO", [128, 2, 128])
        rec_all = sb("naR", [128, 4])
        R_KT, R_QT, R_vv, R_msk = Res("KT"), Res("QT"), Res("vv"), Res("msk")
        R_tab = [Res("tab0"), Res("tab1")]
        ds_na = P.dsem("na")
        pT_ring = Ring(P, "naP", [pT_all[:, i, :] for i in range(6)], with_dsem=False)
        sT_ring = Ring(P, "naS", [sT_all[:, i, :] for i in range(4)], with_dsem=False)
        ob_ring = Ring(P, "naO", [ob_all[:, i, :] for i in range(2)])
        rec_ring = Ring(P, "naR", [rec_all[:, i:i + 1] for i in range(4)], with_dsem=False)
        P.op("dve", lambda e: e.memset(vv[:, :, :, 64:65], 1.0), writes=[R_vv])
        P.dma("sp", ds_na, msk[:], namask_d[:, :, :], writes=[R_msk])
        cnt_s = [0]
        cnt_o = [0]
        for hp in na_hps:
            P.dma("sp", ds_na, KT[:], KAT[hp * 128:(hp + 1) * 128, :], writes=[R_KT])
            P.dma("sp", ds_na, QT[:], QAT[hp * 128:(hp + 1) * 128, :], writes=[R_QT])
            for h2 in range(2):
                c0 = hp * 128 + h2 * 64
                P.dma("sp", ds_na, vv[:, :, h2, 0:64], VA[:, c0:c0 + 64].rearrange("(t p) c -> p t c", p=128),
                      writes=[R_vv])
                P.dma("sp", ds_na, tabs[:, h2], natab_d[hp * 2 + h2], writes=[R_tab[h2]])
                P.op("dve", lambda e, h2=h2: e.tensor_tensor(out=tabs[:, h2], in0=tabs[:, h2], in1=msk[:], op=ALU.add),
                     reads=[R_tab[h2], R_msk], writes=[R_tab[h2]])
            for b in range(32):
                ob, rob, dsob = ob_ring.next()
                chunks = na_chunks(b) + [(32, None), (33, None)]
                nch = len(chunks)
                pos = []
                for h2 in range(2):
                    kk = 4 + (cnt_o[0] % 4)
                    cnt_o[0] += 1
                    pos.append((psum[kk], R_ps[kk]))

                def pv(pT, rpT, j, ci, h2, nch=nch, pos=pos):
                    po, rpo = pos[h2]
                    P.op("pe", lambda e: e.matmul(po[:, 0:65], lhsT=pT, rhs=vv[:, j, h2, :],
                                                  start=(ci == 0), stop=(ci == nch - 1)),
                         reads=[rpT, R_vv], writes=[rpo])
                pending = []
                for ci, (j, ti) in enumerate(chunks):
                    cur_s = []
                    for h2 in range(2):
                        lo, hi = h2 * 64, (h2 + 1) * 64
                        k = cnt_s[0] % 4
                        cnt_s[0] += 1
                        ps_s, rps_s = psum[k], R_ps[k]
                        P.op("pe", lambda e, ps_s=ps_s, j=j, b=b, lo=lo, hi=hi: e.matmul(
                            ps_s[:, 0:128], lhsT=KT[lo:hi, j * 128:(j + 1) * 128], rhs=QT[lo:hi, b * 128:(b + 1) * 128],
                            start=True, stop=True), reads=[R_KT, R_QT], writes=[rps_s])
                        cur_s.append((ps_s, rps_s))
                    for p in pending:
                        pv(*p)
                    pending = []
                    for h2 in range(2):
                        ps_s, rps_s = cur_s[h2]
                        pT, rpT, _ = pT_ring.next()
                        if ti is not None:
                            sT, rsT, _ = sT_ring.next()
                            P.op("dve", lambda e, sT=sT, ps_s=ps_s, ti=ti, h2=h2: e.tensor_tensor(
                                out=sT, in0=ps_s[:, 0:128], in1=tabs[:, h2, ti, :], op=ALU.add),
                                reads=[rps_s, R_tab[h2]], writes=[rsT])
                            P.op("act", lambda e, pT=pT, sT=sT: e.activation(out=pT, in_=sT, func=AF.Exp),
                                 reads=[rsT], writes=[rpT])
                        else:
                            P.op("act", lambda e, pT=pT, ps_s=ps_s: e.activation(out=pT, in_=ps_s[:, 0:128], func=AF.Exp),
                                 reads=[rps_s], writes=[rpT])
                        pending.append((pT, rpT, j, ci, h2))
                for p in pending:
                    pv(*p)
                for h2 in range(2):
                    lo, hi = h2 * 64, (h2 + 1) * 64
                    po, rpo = pos[h2]
                    rc, rrc, _ = rec_ring.next()
                    P.op("dve", lambda e, rc=rc, po=po: e.reciprocal(out=rc, in_=po[:, 64:65]), reads=[rpo], writes=[rrc])
                    P.op("act", lambda e, rc=rc, po=po, ob=ob, lo=lo, hi=hi: e.activation(
                        out=ob[:, lo:hi], in_=po[:, 0:64], func=AF.Identity, scale=rc), reads=[rpo, rrc], writes=[rob])
                P.dma("sp", dsob, OA[b * 128:(b + 1) * 128, hp * 128:(hp + 1) * 128], ob, reads=[rob])
        P.flush()
        ph2.close()

    if 3 in phases:
        ph3 = ExitStack()
        cur[0] = ph3
        convw = sb("g_convw", [128, 12, 5])
        msk4 = sb("g_msk", [128, 8, 128])
        ropet = sb("g_rope", [128, 32, 128])
        adt = sb("g_adt", [128, 16])
        nea = sb("g_nea", [128, 8])
        onec = sb("g_onec", [128, 1])
        ones3 = sb("g_ones3", [128, 4, 128])
        R_gc = Res("gconst")
        ds_g = P.dsem("gconst")
        P.dma("sp", ds_g, convw[:], convw_d[:, :, :], writes=[R_gc])
        P.dma("sp", ds_g, msk4[:], gmask_d[:, :, :], writes=[R_gc])
        P.dma("sp", ds_g, ropet[:], rope_d[:, :, :], writes=[R_gc])
        P.dma("sp", ds_g, adt[:], adt_d[:, :], writes=[R_gc])
        P.op("dve", lambda e: e.memset(onec[:], 1.0), writes=[R_gc])
        P.op("dve", lambda e: e.memset(ones3[:], 1.0), writes=[R_gc])
        P.op("act", lambda e: e.activation(out=nea[:], in_=adt[:, 0:8], func=AF.Exp), reads=[R_gc], writes=[R_gc])
        P.op("dve", lambda e: e.tensor_scalar_mul(out=nea[:], in0=nea[:], scalar1=-1.0), reads=[R_gc], writes=[R_gc])
        gbank_i = [0]

        def gbank():
            k = gbank_i[0] % 8
            gbank_i[0] += 1
            return psum[k], R_ps[k]

        def v3(ap, h=4):
            return ap.rearrange("p (h n) -> p h n", h=h)

        def bc_last(ap2, h=4, n=128):
            return ap2.unsqueeze(2).to_broadcast([128, h, n])

        def bc_mid(ap2, h=4, n=128):
            return ap2.unsqueeze(1).to_broadcast([128, h, n])

        def tile_rows(tau):
            return tau * 128 if tau < 32 else SEQ + (tau - 32) * 128

        R_QKVN = [Res(f"qkvn{t}") for t in range(34)]
        R_GB = [Res(f"gb{t}") for t in range(34)]
        R_OB = [Res(f"ob{t}") for t in range(32)]

        cw = sb("g_cw", [128, 12, 132])
        acc = sb("g_acc", [128, 12, 128])
        tmpc = sb("g_tmpc", [128, 12, 128])
        tm = sb("g_tm", [128, 1536])
        sq = sb("g_sq", [128, 1024])
        rp = sb("g_rp", [128, 1024])
        qkn = sb("g_qkn", [128, 1024])
        tA = sb("g_tA", [128, 512])
        tB = sb("g_tB", [128, 512])
        smp = sb("g_smp", [128, 16])
        abt = sb("g_abt", [128, 16])
        gbt = sb("g_gbt", [128, 16])
        R_cw, R_acc, R_tmpc, R_tm, R_sq, R_rp, R_qkn = (Res(n) for n in ("cw", "acc", "tmpc", "tm", "sq", "rp", "qkn"))
        R_tA, R_tB, R_smp, R_abt, R_gbt = (Res(n) for n in ("tA", "tB", "smp", "abt", "gbt"))
        ds_cw, ds_tm, ds_qkn, ds_abt, ds_gbt = (P.dsem(n) for n in ("g_cw", "g_tm", "g_qkn", "g_abt", "g_gbt"))
        for tau in range(34):
            t0 = tile_rows(tau)
            seg_lo, seg_hi = (0, SEQ) if tau < 32 else (SEQ, NTOK)
            lo, hi = max(t0 - 2, seg_lo), min(t0 + 130, seg_hi)
            off = lo - (t0 - 2)
            if off > 0:
                P.op("pool", lambda e: e.memset(cw[:, :, 0:2], 0.0), writes=[R_cw])
            if hi < t0 + 130:
                P.op("pool", lambda e: e.memset(cw[:, :, 130:132], 0.0), writes=[R_cw])
            P.dma("sp", ds_cw, cw[:, :, off:off + hi - lo], QKVT[:, lo:hi].rearrange("(c p) n -> p c n", p=128),
                  writes=[R_cw])
            P.op("dve", lambda e: e.tensor_tensor(out=acc[:], in0=cw[:, :, 0:128],
                                                  in1=convw[:, :, 0:1].to_broadcast([128, 12, 128]), op=ALU.mult),
                 reads=[R_cw, R_gc], writes=[R_acc])
            for k in range(1, 5):
                P.op("pool", lambda e, k=k: e.tensor_tensor(out=tmpc[:], in0=cw[:, :, k:k + 128],
                                                            in1=convw[:, :, k:k + 1].to_broadcast([128, 12, 128]),
                                                            op=ALU.mult), reads=[R_cw, R_gc], writes=[R_tmpc])
                P.op("dve", lambda e: e.tensor_tensor(out=acc[:], in0=acc[:], in1=tmpc[:], op=ALU.add),
                     reads=[R_acc, R_tmpc], writes=[R_acc])
            P.op("act", lambda e: e.activation(out=acc[:], in_=acc[:], func=AF.Silu), reads=[R_acc], writes=[R_acc])
            for c4 in range(3):
                ps, rps = gbank()
                for q in range(4):
                    P.op("pe", lambda e, ps=ps, q=q, c4=c4: e.transpose(
                        out=ps[:, q * 128:(q + 1) * 128], in_=acc[:, c4 * 4 + q, :], identity=ident[:]),
                        reads=[R_acc, R_ident], writes=[rps])
                if c4 % 2 == 0:
                    P.op("act", lambda e, ps=ps, c4=c4: e.activation(out=tm[:, c4 * 512:(c4 + 1) * 512], in_=ps[:, :],
                                                                     func=AF.Copy), reads=[rps], writes=[R_tm])
                else:
                    P.op("dve", lambda e, ps=ps, c4=c4: e.tensor_copy(out=tm[:, c4 * 512:(c4 + 1) * 512], in_=ps[:, :]),
                         reads=[rps], writes=[R_tm])
            P.op("dve", lambda e: e.tensor_tensor(out=sq[:], in0=tm[:, 0:1024], in1=tm[:, 0:1024], op=ALU.mult),
                 reads=[R_tm], writes=[R_sq])
            P.op("dve", lambda e: e.reduce_sum(out=smp[:, 0:8], in_=v3(sq[:], 8), axis=mybir.AxisListType.X),
                 reads=[R_sq], writes=[R_smp])
            P.op("act", lambda e: e.activation(out=smp[:, 8:16], in_=smp[:, 0:8], func=AF.Sqrt, bias=epsc[:, 1:2]),
                 reads=[R_smp, R_ident], writes=[R_smp])
            P.op("dve", lambda e: e.reciprocal(out=smp[:, 8:16], in_=smp[:, 8:16]), reads=[R_smp], writes=[R_smp])
            P.op("dve", lambda e: e.tensor_scalar_mul(out=smp[:, 8:12], in0=smp[:, 8:12], scalar1=128.0 ** -0.5),
                 reads=[R_smp], writes=[R_smp])
            if tau < 32:
                x5 = tm[:, 0:1024].rearrange("p (i a h f) -> p i a h f", i=8, a=2, h=2)
                r5 = rp[:].rearrange("p (i a h f) -> p i a h f", i=8, a=2, h=2)
                cs = ropet[:, tau, :].rearrange("p (s a f) -> p s a f", s=2, a=2)
                tA4 = tA[:].rearrange("p (i a f) -> p i a f", i=8, a=2)
                tB4 = tB[:].rearrange("p (i a f) -> p i a f", i=8, a=2)
                for (xa, xb, half, op) in ((0, 1, 0, ALU.subtract), (1, 0, 1, ALU.add)):
                    for ax in range(2):
                        cosb = cs[:, 0, ax, :].unsqueeze(1).to_broadcast([128, 8, 32])
                        sinb = cs[:, 1, ax, :].unsqueeze(1).to_broadcast([128, 8, 32])
                        P.op("pool", lambda e, xa=xa, ax=ax, cosb=cosb: e.tensor_tensor(
                            out=tA4[:, :, ax, :], in0=x5[:, :, ax, xa, :], in1=cosb, op=ALU.mult),
                            reads=[R_tm, R_gc], writes=[R_tA])
                        P.op("dve", lambda e, xb=xb, ax=ax, sinb=sinb: e.tensor_tensor(
                            out=tB4[:, :, ax, :], in0=x5[:, :, ax, xb, :], in1=sinb, op=ALU.mult),
                            reads=[R_tm, R_gc], writes=[R_tB])
                        P.op("dve", lambda e, half=half, op=op, ax=ax: e.tensor_tensor(
                            out=r5[:, :, ax, half, :], in0=tA4[:, :, ax, :], in1=tB4[:, :, ax, :], op=op),
                            reads=[R_tA, R_tB], writes=[R_rp])
                srcqk, rsrc = rp, R_rp
            else:
                srcqk, rsrc = tm, R_tm
            P.op("dve", lambda e, srcqk=srcqk: e.tensor_tensor(out=v3(qkn[:], 8), in0=v3(srcqk[:, 0:1024], 8),
                                                               in1=bc_last(smp[:, 8:16], 8), op=ALU.mult),
                 reads=[rsrc, R_smp], writes=[R_qkn])
            P.dma("sp", ds_qkn, QKVN[t0:t0 + 128, 0:1024], qkn[:], reads=[R_qkn], writes=[R_QKVN[tau]])
            P.dma("sp", ds_tm, QKVN[t0:t0 + 128, 1024:1536], tm[:, 1024:1536], reads=[R_tm], writes=[R_QKVN[tau]])
            P.dma("sp", ds_abt, abt[:], AB[t0:t0 + 128, :], writes=[R_abt])
            P.op("dve", lambda e: e.tensor_tensor(out=gbt[:, 0:8], in0=abt[:, 0:8], in1=adt[:, 8:16], op=ALU.add),
                 reads=[R_abt, R_gc], writes=[R_gbt])
            P.op("act", lambda e: e.activation(out=gbt[:, 0:8], in_=gbt[:, 0:8], func=AF.Exp), reads=[R_gbt], writes=[R_gbt])
            P.op("act", lambda e: e.activation(out=gbt[:, 0:8], in_=gbt[:, 0:8], func=AF.Ln, bias=onec[:, 0:1]),
                 reads=[R_gbt, R_gc], writes=[R_gbt])
            P.op("dve", lambda e: e.tensor_tensor(out=gbt[:, 0:8], in0=gbt[:, 0:8], in1=nea[:], op=ALU.mult),
                 reads=[R_gbt, R_gc], writes=[R_gbt])
            P.op("act", lambda e: e.activation(out=gbt[:, 8:16], in_=abt[:, 8:16], func=AF.Sigmoid),
                 reads=[R_abt], writes=[R_gbt])
            P.dma("sp", ds_gbt, GB[t0:t0 + 128, :], gbt[:], reads=[R_gbt], writes=[R_GB[tau]])

        names = ["kb", "vb", "kbg", "ktl", "kT", "qT", "kbT", "gbc", "Dd", "tmn", "tmx", "Ee", "Ff", "Em", "Ea", "Fm",
                 "X0", "X1", "XT0", "XT1", "R0", "R1", "Q0", "Q1", "Xf", "XTf", "Cm", "CmT", "W1", "W2", "AT", "uu", "wT", "vnew", "o1", "obf", "prev"]
        B = {n: sb("gs_" + n, [128, 4, 128]) for n in names}
        RB = {n: Res(n) for n in names}
        qkv_all = sb("gs_qkv", [128, 2, 1536])
        gb_all = sb("gs_gb", [128, 2, 16])
        qkv_ring = Ring(P, "gs_qkv", [qkv_all[:, i, :] for i in range(2)])
        gb_ring = Ring(P, "gs_gb", [gb_all[:, i, :] for i in range(2)])
        sm2 = sb("gs_sm2", [128, 24])
        R_sm2 = Res("sm2")
        Sst = sb("gs_S", [128, 4, 128])
        R_S = Res("S")
        ds_ob = P.dsem("gs_ob")
        ds_prev = P.dsem("gs_prev")

        def mm4(lhs_fn, rhs_fn, reads):
            ps, rps = gbank()
            for h in range(4):
                P.op("pe", lambda e, h=h, l=lhs_fn(h), r=rhs_fn(h): e.matmul(
                    ps[:, h * 128:(h + 1) * 128], lhsT=l, rhs=r, start=True, stop=True), reads=reads, writes=[rps])
            return v3(ps[:, :]), rps

        def tr4(src, rsrc, dstn, eng):
            ps, rps = gbank()
            for h in range(4):
                P.op("pe", lambda e, h=h, a=src(h): e.transpose(out=ps[:, h * 128:(h + 1) * 128], in_=a, identity=ident[:]),
                     reads=[rsrc, R_ident], writes=[rps])
            copy4(v3(ps[:, :]), rps, dstn, eng)

        def copy4(ps3, rps, dstn, eng):
            if eng == "act":
                P.op("act", lambda e: e.activation(out=B[dstn][:], in_=ps3, func=AF.Copy), reads=[rps], writes=[RB[dstn]])
            else:
                P.op("dve", lambda e: e.tensor_copy(out=B[dstn][:], in_=ps3), reads=[rps], writes=[RB[dstn]])

        def tt(eng, outn, in0, in1, op, reads):
            P.op(eng, lambda e: e.tensor_tensor(out=B[outn][:], in0=in0, in1=in1, op=op), reads=reads, writes=[RB[outn]])

        for d in range(2):
            order = [32, 33] + list(range(32)) if d == 0 else [33, 32] + list(range(31, -1, -1))
            U = msk4[:, 1, :] if d == 0 else msk4[:, 3, :]
            m_s = msk4[:, 0, :] if d == 0 else msk4[:, 2, :]
            m_i = msk4[:, 1, :] if d == 0 else msk4[:, 3, :]
            m_sT = msk4[:, 2, :] if d == 0 else msk4[:, 0, :]
            P.op("dve", lambda e: e.memset(Sst[:], 0.0), writes=[R_S])
            for tau in order:
                t0 = tile_rows(tau)
                wout = tau < 32
                qkv, rqkv, dsq = qkv_ring.next()
                gbv, rgbv, dsg = gb_ring.next()
                P.dma("sp", dsq, qkv, QKVN[t0:t0 + 128, :], reads=[R_QKVN[tau]], writes=[rqkv])
                P.dma("sp", dsg, gbv, GB[t0:t0 + 128, :], reads=[R_GB[tau]], writes=[rgbv])
                gs = gbv[:, d * 4:(d + 1) * 4]
                bs = gbv[:, 8 + d * 4:8 + (d + 1) * 4]
                qn = v3(qkv[:, 0:512])
                kn = v3(qkv[:, 512:1024])
                vn = v3(qkv[:, 1024:1536])
                ps, rps = gbank()
                P.op("pe", lambda e, ps=ps, U=U, gs=gs: e.matmul(ps[:, 0:4], lhsT=U, rhs=gs, start=True, stop=True),
                     reads=[R_gc, rgbv], writes=[rps])
                P.op("pe", lambda e, ps=ps, gs=gs: e.matmul(ps[:, 4:8], lhsT=ones3[:, 0, :], rhs=gs, start=True, stop=True),
                     reads=[R_gc, rgbv], writes=[rps])
                P.op("dve", lambda e, ps=ps: e.tensor_copy(out=sm2[:, 0:8], in_=ps[:, 0:8]), reads=[rps], writes=[R_sm2])
                P.op("dve", lambda e: e.tensor_tensor(out=sm2[:, 16:20], in0=sm2[:, 4:8], in1=sm2[:, 0:4], op=ALU.subtract),
                     reads=[R_sm2], writes=[R_sm2])
                P.op("act", lambda e: e.activation(out=sm2[:, 8:12], in_=sm2[:, 0:4], func=AF.Exp), reads=[R_sm2], writes=[R_sm2])
                P.op("act", lambda e: e.activation(out=sm2[:, 12:16], in_=sm2[:, 4:8], func=AF.Exp), reads=[R_sm2], writes=[R_sm2])
                P.op("act", lambda e: e.activation(out=sm2[:, 16:20], in_=sm2[:, 16:20], func=AF.Exp), reads=[R_sm2], writes=[R_sm2])
                gc, egc, etot, etl = sm2[:, 0:4], sm2[:, 8:12], sm2[:, 12:16], sm2[:, 16:20]
                tt("dve", "kb", kn, bc_last(bs), ALU.mult, [rqkv, rgbv])
                tt("pool", "vb", vn, bc_last(bs), ALU.mult, [rqkv, rgbv])
                tt("dve", "kbg", B["kb"][:], bc_last(egc), ALU.mult, [RB["kb"], R_sm2])
                tt("pool", "ktl", kn, bc_last(etl), ALU.mult, [rqkv, R_sm2])
                tr4(lambda h: kn[:, h, :], rqkv, "kT", "act")
                tr4(lambda h: B["kb"][:, h, :], RB["kb"], "kbT", "dve")
                if wout:
                    tr4(lambda h: qn[:, h, :], rqkv, "qT", "act")
                tt("pool", "gbc", ones3[:], bc_last(gs), ALU.mult, [R_gc, rgbv])
                psD, rpsD = mm4(lambda h: B["gbc"][:, h, :], lambda h: U, [RB["gbc"], R_gc])
                tt("dve", "Dd", psD, bc_last(gc), ALU.subtract, [rpsD, R_sm2])
                P.op("dve", lambda e: e.tensor_scalar_min(out=B["tmn"][:], in0=B["Dd"][:], scalar1=0.0),
                     reads=[RB["Dd"]], writes=[RB["tmn"]])
                P.op("pool", lambda e: e.tensor_scalar_max(out=B["tmx"][:], in0=B["Dd"][:], scalar1=0.0),
                     reads=[RB["Dd"]], writes=[RB["tmx"]])
                P.op("act", lambda e: e.activation(out=B["Ee"][:], in_=B["tmn"][:], func=AF.Exp),
                     reads=[RB["tmn"]], writes=[RB["Ee"]])
                P.op("act", lambda e: e.activation(out=B["Ff"][:], in_=B["tmx"][:], func=AF.Exp, scale=-1.0),
                     reads=[RB["tmx"]], writes=[RB["Ff"]])
                tt("pool", "Em", B["Ee"][:], bc_mid(m_s), ALU.mult, [RB["Ee"], R_gc])
                tt("pool", "Fm", B["Ff"][:], bc_mid(m_sT), ALU.mult, [RB["Ff"], R_gc])
                psK, rpsK = mm4(lambda h: B["kT"][:, h, :], lambda h: B["kbT"][:, h, :], [RB["kT"], RB["kbT"]])
                P.op("dve", lambda e, psK=psK: e.scalar_tensor_tensor(out=B["Xf"][:], in0=psK, scalar=-1.0, in1=B["Em"][:],
                                                                      op0=ALU.mult, op1=ALU.mult),
                     reads=[rpsK, RB["Em"]], writes=[RB["Xf"]])
                psK2, rpsK2 = mm4(lambda h: B["kbT"][:, h, :], lambda h: B["kT"][:, h, :], [RB["kT"], RB["kbT"]])
                P.op("dve", lambda e, psK2=psK2: e.scalar_tensor_tensor(out=B["XTf"][:], in0=psK2, scalar=-1.0, in1=B["Fm"][:],
                                                                        op0=ALU.mult, op1=ALU.mult),
                     reads=[rpsK2, RB["Fm"]], writes=[RB["XTf"]])
                if wout:
                    tt("pool", "Ea", B["Ee"][:], bc_mid(m_i), ALU.mult, [RB["Ee"], R_gc])
                    psA, rpsA = mm4(lambda h: B["kT"][:, h, :], lambda h: B["qT"][:, h, :], [RB["kT"], RB["qT"]])
                    tt("dve", "AT", psA, B["Ea"][:], ALU.mult, [rpsA, RB["Ea"]])
                tt("pool", "X0", B["Xf"][:], bc_mid(msk4[:, 4, :]), ALU.mult, [RB["Xf"], R_gc])
                tt("pool", "XT0", B["XTf"][:], bc_mid(msk4[:, 4, :]), ALU.mult, [RB["XTf"], R_gc])
                tt("dve", "R0", B["X0"][:], bc_mid(ident[:]), ALU.add, [RB["X0"], R_ident])
                tt("dve", "Q0", B["XT0"][:], bc_mid(ident[:]), ALU.add, [RB["XT0"], R_ident])
                for lvl in range(1, 4):
                    a, b_ = (lvl - 1) % 2, lvl % 2
                    Xa, XTa, Xb, XTb = f"X{a}", f"XT{a}", f"X{b_}", f"XT{b_}"
                    Ra, Rb, Qa, Qb = f"R{a}", f"R{b_}", f"Q{a}", f"Q{b_}"
                    pX, rpX = mm4(lambda h, XTa=XTa: B[XTa][:, h, :], lambda h, Xa=Xa: B[Xa][:, h, :], [RB[Xa], RB[XTa]])
                    pXT, rpXT = mm4(lambda h, Xa=Xa: B[Xa][:, h, :], lambda h, XTa=XTa: B[XTa][:, h, :], [RB[Xa], RB[XTa]])
                    copy4(pX, rpX, Xb, "act")
                    copy4(pXT, rpXT, XTb, "dve")
                    pR, rpR = mm4(lambda h, XTb=XTb: B[XTb][:, h, :], lambda h, Ra=Ra: B[Ra][:, h, :], [RB[XTb], RB[Ra]])
                    tt("dve", Rb, pR, B[Ra][:], ALU.add, [rpR, RB[Ra]])
                    pQ, rpQ = mm4(lambda h, Xb=Xb: B[Xb][:, h, :], lambda h, Qa=Qa: B[Qa][:, h, :], [RB[Xb], RB[Qa]])
                    tt("dve", Qb, pQ, B[Qa][:], ALU.add, [rpQ, RB[Qa]])
                curb = 1
                for si in range(3):
                    last = si == 2
                    offm = msk4[:, 5 + si, :]
                    a, b_ = curb, 1 - curb
                    Ra, Rb, Qa, Qb = f"R{a}", f"R{b_}", f"Q{a}", f"Q{b_}"
                    tt("pool", "Cm", B["XTf"][:], bc_mid(offm), ALU.mult, [RB["XTf"], R_gc])
                    pW1, rpW1 = mm4(lambda h: B["Cm"][:, h, :], lambda h, Ra=Ra: B[Ra][:, h, :], [RB["Cm"], RB[Ra]])
                    copy4(pW1, rpW1, "W1", "act")
                    pY, rpY = mm4(lambda h, Qa=Qa: B[Qa][:, h, :], lambda h: B["W1"][:, h, :], [RB[Qa], RB["W1"]])
                    tt("dve", Rb, pY, B[Ra][:], ALU.add, [rpY, RB[Ra]])
                    if not last:
                        tt("pool", "CmT", B["Xf"][:], bc_mid(offm), ALU.mult, [RB["Xf"], R_gc])
                        pW2, rpW2 = mm4(lambda h: B["CmT"][:, h, :], lambda h, Qa=Qa: B[Qa][:, h, :], [RB["CmT"], RB[Qa]])
                        copy4(pW2, rpW2, "W2", "dve")
                        pT, rpT = mm4(lambda h, Ra=Ra: B[Ra][:, h, :], lambda h: B["W2"][:, h, :], [RB[Ra], RB["W2"]])
                        tt("dve", Qb, pT, B[Qa][:], ALU.add, [rpT, RB[Qa]])
                    curb = b_
                assert curb == 0
                TT = "R0"
                pU, rpU = mm4(lambda h: B[TT][:, h, :], lambda h: B["vb"][:, h, :], [RB[TT], RB["vb"]])
                copy4(pU, rpU, "uu", "act")
                pW, rpW = mm4(lambda h: B["kbg"][:, h, :], lambda h: B[TT][:, h, :], [RB[TT], RB["kbg"]])
                copy4(pW, rpW, "wT", "dve")
                p1, rp1 = mm4(lambda h: B["wT"][:, h, :], lambda h: Sst[:, h, :], [RB["wT"], R_S])
                tt("dve", "vnew", B["uu"][:], p1, ALU.subtract, [RB["uu"], rp1])
                if wout:
                    p2, rp2 = mm4(lambda h: B["qT"][:, h, :], lambda h: Sst[:, h, :], [RB["qT"], R_S])
                    tt("dve", "o1", p2, bc_last(egc), ALU.mult, [rp2, R_sm2])
                    p3, rp3 = mm4(lambda h: B["AT"][:, h, :], lambda h: B["vnew"][:, h, :], [RB["AT"], RB["vnew"]])
                    tt("dve", "obf", B["o1"][:], p3, ALU.add, [RB["o1"], rp3])
                p4, rp4 = mm4(lambda h: B["ktl"][:, h, :], lambda h: B["vnew"][:, h, :], [RB["ktl"], RB["vnew"]])
                P.op("pool", lambda e: e.tensor_tensor(out=Sst[:], in0=Sst[:], in1=bc_last(etot), op=ALU.mult),
                     reads=[R_S, R_sm2], writes=[R_S])
                P.op("dve", lambda e, p4=p4: e.tensor_tensor(out=Sst[:], in0=Sst[:], in1=p4, op=ALU.add),
                     reads=[R_S, rp4], writes=[R_S])
                if wout:
                    obflat = B["obf"][:].rearrange("p h n -> p (h n)")
                    if d == 0:
                        P.dma("sp", ds_ob, OB[t0:t0 + 128, :], obflat, reads=[RB["obf"]], writes=[R_OB[tau]])
                    else:
                        P.dma("sp", ds_prev, B["prev"][:].rearrange("p h n -> p (h n)"), OB[t0:t0 + 128, :],
                              reads=[R_OB[tau]], writes=[RB["prev"]])
                        tt("pool", "obf", B["obf"][:], B["prev"][:], ALU.add, [RB["obf"], RB["prev"]])
                        P.dma("sp", ds_ob, OB[t0:t0 + 128, :], obflat, reads=[RB["obf"]], writes=[R_OB[tau]])
        P.flush()
        ph3.close()

    def phase4_group(E, g):
        tok0 = g * 512
        hx, zt, uTr, Gr = E.hx, E.zt, E.uTr, E.Gr
        for t in range(4):
            P.dma("sp", E.ds_hx[t], hx[:, t, :], X1[tok0 + t * 128:tok0 + (t + 1) * 128, :], writes=[E.R_hx[t]])
        for t in range(4):
            E.modulate_T(hx[:, t, :], E.R_hx[t], t, 3, 0)
        plan = [(win_d[24 + i], 1024) for i in range(4)]
        for c in range(8):
            plan += [(wpa_d[c], 512), (wpb_d[c], 512), (win_d[29 + c], 1024), (win_d[37 + c], 1024)]
        plan += [(wo_d[c], 1024) for c in range(8)]
        ws = E.WStream(plan)
        ures = E.R_uT[:4]
        u_rhs = lambda kc: uTr[:, kc, 0:512]
        for i in range(4):
            ps, rps = E.gemm_block(ws, 8, u_rhs, ures, 512)
            tb, rtb, _ = E.tmpB_ring.next()
            P.op("act", lambda e, tb=tb, ps=ps: e.activation(out=tb[:, :], in_=ps[:, :], func=AF.Copy),
                 reads=[rps], writes=[rtb])
            pt, rpt = E.tr_ps()
            for t in range(4):
                P.op("pe", lambda e, pt=pt, tb=tb, t=t: e.transpose(
                    out=pt[:, t * 128:(t + 1) * 128], in_=tb[:, t * 128:(t + 1) * 128], identity=ident[:]),
                    reads=[rtb, R_ident], writes=[rpt])
            P.op("act", lambda e, pt=pt, i=i: e.activation(
                out=E.zs[:, :, i * 128:(i + 1) * 128], in_=pt[:, :].rearrange("p (t c) -> p t c", t=4), func=AF.Silu),
                reads=[rpt], writes=[E.R_zs])
        for t in range(4):
            oat, roat, dsoa = E.oa_ring.next()
            obt, robt, dsob = E.ob_ring.next()
            r0 = tok0 + t * 128
            P.dma("sp", dsoa, oat, OA[r0:r0 + 128, :], writes=[roat])
            P.dma("sp", dsob, obt, OB[r0:r0 + 128, :], writes=[robt])
            obv = obt.rearrange("p (h d) -> p h d", h=4)
            P.op("dve", lambda e, obt=obt: e.tensor_tensor(out=E.sq[:], in0=obt, in1=obt, op=ALU.mult),
                 reads=[robt], writes=[E.R_sq])
            P.op("dve", lambda e: e.reduce_sum(out=E.ss[:, 0:4], in_=E.sq[:].rearrange("p (h d) -> p h d", h=4),
                                               axis=mybir.AxisListType.X), reads=[E.R_sq], writes=[E.R_ss])
            P.op("act", lambda e: e.activation(out=E.ss[:, 4:8], in_=E.ss[:, 0:4], func=AF.Sqrt, scale=1.0 / 128.0,
                                               bias=epsc[:, 1:2]), reads=[E.R_ss, R_ident], writes=[E.R_ss])
            P.op("dve", lambda e: e.reciprocal(out=E.ss[:, 4:8], in_=E.ss[:, 4:8]), reads=[E.R_ss], writes=[E.R_ss])
            P.op("dve", lambda e, obv=obv: e.tensor_tensor(
                out=obv, in0=obv, in1=E.ss[:, 4:8].unsqueeze(2).to_broadcast([128, 4, 128]), op=ALU.mult),
                reads=[robt, E.R_ss], writes=[robt])
            P.op("dve", lambda e, obv=obv: e.tensor_tensor(
                out=obv, in0=obv, in1=E.normw[:].unsqueeze(1).to_broadcast([128, 4, 128]), op=ALU.mult),
                reads=[robt, E.R_normw], writes=[robt])
            P.op("dve", lambda e, obt=obt, t=t: e.tensor_tensor(out=obt, in0=obt, in1=E.zs[:, t, :], op=ALU.mult),
                 reads=[robt, E.R_zs], writes=[robt])
            for (src, rsrc, base, eng) in ((oat, roat, 0, "act"), (obt, robt, 4, "dve")):
                pt, rpt = E.tr_ps()
                for kc in range(4):
                    P.op("pe", lambda e, pt=pt, src=src, kc=kc: e.transpose(
                        out=pt[:, kc * 128:(kc + 1) * 128], in_=src[:, kc * 128:(kc + 1) * 128], identity=ident[:]),
                        reads=[rsrc, R_ident], writes=[rpt])
                dst = Gr[:, base:base + 4, t * 128:(t + 1) * 128]
                src3 = pt[:, :].rearrange("p (k n) -> p k n", k=4)
                wr = [E.R_G[base + kc] for kc in range(4)]
                if eng == "act":
                    P.op("act", lambda e, dst=dst, src3=src3: e.activation(out=dst, in_=src3, func=AF.Copy),
                         reads=[rpt], writes=wr)
                else:
                    P.op("dve", lambda e, dst=dst, src3=src3: e.tensor_copy(out=dst, in_=src3), reads=[rpt], writes=wr)
        for c in range(8):
            psA, rpA = E.gemm_block(ws, 4, lambda kc: Gr[:, kc, 0:512], E.R_G[0:4], 512)
            psB, rpB = E.gemm_block(ws, 4, lambda kc: Gr[:, 4 + kc, 0:512], E.R_G[4:8], 512)
            psGa, rpGa = E.gemm_block(ws, 8, u_rhs, ures, 512)
            psGb, rpGb = E.gemm_block(ws, 8, u_rhs, ures, 512)
            ta1, rta1, _ = E.tmpA_ring.next()
            ta2, rta2, _ = E.tmpA_ring.next()
            P.op("act", lambda e, ta1=ta1, psGa=psGa: e.activation(out=ta1[:, :], in_=psGa[:, :], func=AF.Sigmoid),
                 reads=[rpGa], writes=[rta1])
            P.op("act", lambda e, ta2=ta2, psGb=psGb: e.activation(out=ta2[:, :], in_=psGb[:, :], func=AF.Sigmoid),
                 reads=[rpGb], writes=[rta2])
            P.op("dve", lambda e, ta1=ta1, psA=psA: e.tensor_tensor(out=ta1[:, :], in0=ta1[:, :], in1=psA[:, :], op=ALU.mult),
                 reads=[rta1, rpA], writes=[rta1])
            P.op("dve", lambda e, ta2=ta2, psB=psB: e.tensor_tensor(out=ta2[:, :], in0=ta2[:, :], in1=psB[:, :], op=ALU.mult),
                 reads=[rta2, rpB], writes=[rta2])
            P.op("dve", lambda e, ta1=ta1, ta2=ta2, c=c: e.tensor_tensor(out=Gr[:, 8 + c, :], in0=ta1[:, :], in1=ta2[:, :],
                                                                          op=ALU.add),
                 reads=[rta1, rta2], writes=[E.R_G[8 + c]])
        for c in range(8):
            ps, rps = E.gemm_block(ws, 8, lambda kc: Gr[:, 8 + kc, 0:512], E.R_G[8:16], 512)
            E.back_to_tokens(ps, rps, modT[:, 5 * 8 + c, 0:1], 4, hx, E.R_hx, zt, E.R_zt, c)
        for t in range(4):
            E.post_ln(zt[:, t, :], E.R_zt[t], 1)
        E.ffn_sublayer(4, 2, 0, wu2_d, wd2_d, 2, zt, E.R_zt, hx, E.R_hx)
        for t in range(4):
            P.dma("sp", E.ds_hx[t], out_d[tok0 + t * 128:tok0 + (t + 1) * 128, :], hx[:, t, :], reads=[E.R_hx[t]])

    if 4 in phases:
        ph4 = ExitStack()
        cur[0] = ph4
        E4 = make_env([1, 2], merge=True)
        E4.zs = sb("m_zs", [128, 4, 512])
        E4.R_zs = Res("zs")
        oa_all = sb("m_oa", [128, 2, 512])
        ob_all = sb("m_ob", [128, 2, 512])
        E4.oa_ring = Ring(P, "m_oa", [oa_all[:, i, :] for i in range(2)])
        E4.ob_ring = Ring(P, "m_ob", [ob_all[:, i, :] for i in range(2)])
        E4.sq = sb("m_sq", [128, 512])
        E4.R_sq = Res("msq")
        E4.ss = sb("m_ss", [128, 8])
        E4.R_ss = Res("mss")
        E4.normw = sb("m_normw", [128, 128])
        E4.R_normw = Res("normw")
        P.dma("sp", ds_misc, E4.normw[:], normw_d[:, :], writes=[E4.R_normw])
        for g in m_groups:
            phase4_group(E4, g)
        P.flush()
        ph4.close()
    P.flush()
    st.close()
    return nc, P


def host_inputs(inputs, b):
    f = np.float32
    g = lambda k: np.asarray(inputs[k], dtype=f)
    m = {}
    m["x"] = np.ascontiguousarray(g("x")[b])
    m["ctx"] = np.ascontiguousarray(g("ctx")[b])
    cT = np.stack([g("c")[b].reshape(8, 128).T, g("c_ctx").reshape(8, 128).T], axis=-1)
    m["cT"] = np.ascontiguousarray(cT)
    m["w_ada"] = np.ascontiguousarray(g("w_ada")[0])
    m["b_adaT"] = np.ascontiguousarray(g("b_ada")[0].reshape(72, 128).T)
    m["lngb"] = np.ascontiguousarray(np.concatenate([g("ln_g")[0], g("ln_b")[0]], axis=0))
    m["ident"] = np.eye(128, dtype=f)
    m["wu1"] = to_blocks(g("ffn1_w_in")[0])
    m["wd1"] = to_blocks(g("ffn1_w_out")[0])
    w_in = g("w_in")[0]
    pad = np.zeros((D, 112), f)
    w_in_p = np.concatenate([w_in[:, :3584], w_in[:, 3584:3600], pad, w_in[:, 3600:]], axis=1)
    m["win"] = to_blocks(w_in_p)
    m["wpa"] = to_blocks(g("w_pa")[0])
    m["wpb"] = to_blocks(g("w_pb")[0])
    m["wo"] = to_blocks(g("w_o")[0])
    m["wu2"] = to_blocks(g("ffn2_w_in")[0])
    m["wd2"] = to_blocks(g("ffn2_w_out")[0])
    if "tabs" not in _NA_CACHE:
        _NA_CACHE["tabs"] = na_tables()
    ridx, cidx, mask = _NA_CACHE["tabs"]
    rpb = g("na_rpb")[0]
    tab = rpb[:, ridx, cidx]
    m["na_tab"] = np.ascontiguousarray(tab.transpose(0, 2, 1, 3))
    m["na_mask"] = np.ascontiguousarray(mask.transpose(1, 0, 2))
    cw = g("gdn_conv_w")[0]
    m["g_convw"] = np.ascontiguousarray(cw.reshape(5, 12, 128).transpose(2, 1, 0))
    if "gconst" not in _NA_CACHE:
        _NA_CACHE["gconst"] = gdn_consts()
    gmask, rope = _NA_CACHE["gconst"]
    m["g_mask"] = gmask
    m["g_rope"] = rope
    adt = np.concatenate([g("gdn_a_log")[0].reshape(8), g("gdn_dt_bias")[0].reshape(8)])
    m["g_adt"] = np.ascontiguousarray(np.broadcast_to(adt[None, :], (128, 16)))
    m["g_normw"] = np.ascontiguousarray(np.broadcast_to(g("gdn_norm_w")[0][None, :], (128, 128)))
    return m


def gdn_consts():
    r = np.arange(128)[:, None]
    c = np.arange(128)[None, :]
    def same(n):
        return (r // n) == (c // n)
    gmask = np.stack([c > r, c >= r, c < r, c <= r, same(16), same(32) & ~same(16), same(64) & ~same(32),
                      ~same(64)], axis=1).astype(np.float32)
    n_freq = 32
    freqs = (10000.0 ** (-np.arange(n_freq, dtype=np.float32) / n_freq)).astype(np.float32)
    t = np.arange(SEQ)
    pos = np.stack([t // GRID_W, t % GRID_W], axis=-1).astype(np.float32)
    ang = (pos[:, :, None] * freqs).astype(np.float32)
    cs = np.stack([np.cos(ang), np.sin(ang)], axis=1).astype(np.float32)
    rope = cs.reshape(32, 128, 128).transpose(1, 0, 2)
    return np.ascontiguousarray(gmask), np.ascontiguousarray(rope)


def kernel(**inputs):
    nc, _ = build_program()
    shared = host_inputs(inputs, 0)
    in_maps = []
    for b in range(8):
        m = dict(shared)
        if b > 0:
            per = host_inputs_core(inputs, b)
            m.update(per)
        in_maps.append(m)
    res = run_bass_kernel_spmd(nc, in_maps, core_ids=list(range(8)))
    return np.stack([np.asarray(r["out"], dtype=np.float32) for r in res.results], axis=0)


def host_inputs_core(inputs, b):
    f = np.float32
    g = lambda k: np.asarray(inputs[k], dtype=f)
    m = {}
    m["x"] = np.ascontiguousarray(g("x")[b])
    m["ctx"] = np.ascontiguousarray(g("ctx")[b])
    cT = np.stack([g("c")[b].reshape(8, 128).T, g("c_ctx").reshape(8, 128).T], axis=-1)
    m["cT"] = np.ascontiguousarray(cT)
    return m
```

```python
import numpy as np
from contextlib import ExitStack
import concourse.bass as bass
import concourse.mybir as mybir
from concourse.bass_utils import run_bass_kernel_spmd

F32 = mybir.dt.float32
F32R = mybir.dt.float32r
AF = mybir.ActivationFunctionType
ALU = mybir.AluOpType

ENGS = ("pe", "act", "dve", "pool", "sp")


class Res:
    __slots__ = ("name", "w", "r")

    def __init__(self, name=""):
        self.name = name
        self.w = None
        self.r = {}


class DSem:
    __slots__ = ("name", "total", "handle")

    def __init__(self, name):
        self.name = name
        self.total = 0
        self.handle = None


class OpRec:
    __slots__ = ("eng", "fn", "waits", "signal", "sigidx", "dsem", "dval", "cwaits", "retired")

    def __init__(self, eng, fn):
        self.eng = eng
        self.fn = fn
        self.waits = []
        self.cwaits = []
        self.signal = False
        self.sigidx = 0
        self.dsem = None
        self.dval = 0
        self.retired = False


class Prog:
    def __init__(self, nc, stack):
        self.nc = nc
        self.stack = stack
        self.ops = {e: [] for e in ENGS}
        self.dsems = []
        self.nops = 0
        self.esem = {e: stack.enter_context(nc.semaphore("s_" + e)) for e in ENGS}
        self.sigcount = {e: 0 for e in ENGS}
        self.known = {e: {} for e in ENGS}

    def dsem(self, name):
        d = DSem(name)
        self.dsems.append(d)
        return d

    def _deps(self, rec, reads, writes):
        eng = rec.eng
        is_dma = rec.dsem is not None
        deps = []
        for r in reads:
            if r.w is not None:
                deps.append((r.w, True))
        for w in writes:
            if w.w is not None:
                deps.append((w.w, False))
            for o in w.r.values():
                deps.append((o, False))
        for o, raw in deps:
            if o.retired:
                continue
            if o.dsem is not None:
                rec.waits.append((o.dsem, max(o.dsem.total, o.dval)))
                continue
            if o.eng == eng and not is_dma:
                if eng == "pe":
                    continue
                if not raw:
                    continue
            o.signal = True
            rec.cwaits.append(o)
        rkey = ("dma", id(rec.dsem)) if is_dma else eng
        for r in reads:
            r.r[rkey] = rec
        for w in writes:
            w.w = rec
            w.r = {}

    def op(self, eng, fn, reads=(), writes=()):
        rec = OpRec(eng, fn)
        self._deps(rec, reads, writes)
        self.ops[eng].append(rec)
        self.nops += 1
        return rec

    def dma(self, queue, dsem, out, in_, reads=(), writes=(), **kw):
        rec = OpRec(queue, lambda e: e.dma_start(out=out, in_=in_, **kw))
        rec.dsem = dsem
        self._deps(rec, reads, writes)
        dsem.total += 16
        rec.dval = dsem.total
        self.ops[queue].append(rec)
        self.nops += 1
        return rec

    def barrier(self):
        last = {}
        for e in ENGS:
            for rec in reversed(self.ops[e]):
                if rec.dsem is None and rec.fn is not None:
                    rec.signal = True
                    last[e] = rec
                    break
        dw = [(d, d.total) for d in self.dsems if d.total > 0]
        for e in ENGS:
            rec = OpRec(e, None)
            rec.waits = list(dw)
            rec.cwaits = [o for k, o in last.items() if k != e]
            self.ops[e].append(rec)

    def flush(self):
        self.barrier()
        nc = self.nc
        esem = self.esem
        for d in self.dsems:
            if d.handle is None:
                d.handle = self.stack.enter_context(nc.semaphore("d_" + d.name))
        for e in ENGS:
            for rec in self.ops[e]:
                if rec.signal and rec.dsem is None and rec.fn is not None:
                    self.sigcount[e] += 1
                    rec.sigidx = self.sigcount[e]

        def run(e, engobj):
            known = self.known[e]
            for rec in self.ops[e]:
                ws = []
                for (d, v) in rec.waits:
                    ws.append((d.handle, id(d), v))
                for o in rec.cwaits:
                    ws.append((esem[o.eng], o.eng, o.sigidx))
                for (h, key, v) in ws:
                    if known.get(key, 0) >= v:
                        continue
                    known[key] = v
                    engobj.wait_ge(h, v)
                if rec.fn is None:
                    continue
                ins = rec.fn(engobj)
                if rec.dsem is not None:
                    ins.then_inc(rec.dsem.handle, 16)
                elif rec.signal:
                    ins.then_inc(esem[e], 1)

        with nc.Block() as block:
            @block.tensor
            def _(pe):
                run("pe", pe)

            @block.scalar
            def _(act):
                run("act", act)

            @block.vector
            def _(dve):
                run("dve", dve)

            @block.gpsimd
            def _(pool):
                run("pool", pool)

            @block.sync
            def _(sp):
                run("sp", sp)
        for e in ENGS:
            for rec in self.ops[e]:
                rec.retired = True
            self.ops[e] = []


class Ring:
    def __init__(self, P, name, aps, with_dsem=True):
        self.slots = []
        for i, ap in enumerate(aps):
            self.slots.append((ap, Res(f"{name}{i}"), P.dsem(f"{name}{i}") if with_dsem else None))
        self.i = 0

    def next(self):
        s = self.slots[self.i % len(self.slots)]
        self.i += 1
        return s


D = 1024
DFF = 2816
NJ = DFF // 128
SEQ = 4096
CTX = 256
NTOK = SEQ + CTX
GRID_W = 64
LN_EPS = 1e-6
NORM_EPS = 1e-6
ALPHA = 2.0 ** 0.25
NEG = -30000.0
N_WIN_BLK = 45


def to_blocks(W):
    K, N = W.shape
    KC, NCB = K // 128, N // 128
    return np.ascontiguousarray(
        W.reshape(KC, 128, NCB, 128).transpose(2, 1, 0, 3).reshape(NCB, 128, KC * 128))


def na_tables():
    def r0(r):
        return min(max(r - 4, 0), 56)

    def c0(c):
        return min(max(c - 8, 0), 48)
    specs = [(2, j) for j in range(0, 5)] + [(0, j) for j in range(4)] + [(1, j) for j in range(4)] \
        + [(30, j) for j in range(28, 32)] + [(31, j) for j in range(28, 32)]
    ridx = np.zeros((21, 128, 128), np.int64)
    cidx = np.zeros((21, 128, 128), np.int64)
    mask = np.full((21, 128, 128), NEG, np.float32)
    for ti, (b, j) in enumerate(specs):
        for k in range(128):
            krow, kcol = 2 * j + k // 64, k % 64
            for q in range(128):
                qrow, qcol = 2 * b + q // 64, q % 64
                ok = (r0(qrow) <= krow < r0(qrow) + 8) and (c0(qcol) <= kcol < c0(qcol) + 16)
                if ok:
                    ridx[ti, k, q] = krow - qrow + 7
                    cidx[ti, k, q] = kcol - qcol + 15
                    mask[ti, k, q] = 0.0
    return ridx, cidx, mask


def na_chunks(b):
    if b == 0:
        return [(j, 5 + j) for j in range(4)]
    if b == 1:
        return [(j, 9 + j) for j in range(4)]
    if b == 30:
        return [(j, 13 + j - 28) for j in range(28, 32)]
    if b == 31:
        return [(j, 17 + j - 28) for j in range(28, 32)]
    return [(b - 2 + d, d) for d in range(5)]


_NA_CACHE = {}


def build_program(dbg=None, groups=tuple(range(9)), phases=(1, 2, 3, 4), na_hps=(0, 1, 2, 3), m_groups=tuple(range(8))):
    nc = bass.Bass("TRN2", target_bir_lowering=False)
    dbg = dbg or set()
    uid = [0]

    def din(name, shape, dt=F32):
        return nc.dram_tensor(name, list(shape), dt, kind="ExternalInput").ap()

    def dscr(name, shape, dt=F32):
        kind = "ExternalOutput" if name in dbg else "Internal"
        return nc.dram_tensor(name, list(shape), dt, kind=kind).ap()

    x_d = din("x", [SEQ, D])
    ctx_d = din("ctx", [CTX, D])
    cT_d = din("cT", [128, 8, 2])
    wada_d = din("w_ada", [D, 9 * D])
    badaT_d = din("b_adaT", [128, 72])
    lngb_d = din("lngb", [6, D])
    ident_d = din("ident", [128, 128])
    wu1_d = din("wu1", [44, 128, 1024], F32R)
    wd1_d = din("wd1", [8, 128, 2816], F32R)
    win_d = din("win", [N_WIN_BLK, 128, 1024], F32R)
    wpa_d = din("wpa", [8, 128, 512], F32R)
    wpb_d = din("wpb", [8, 128, 512], F32R)
    wo_d = din("wo", [8, 128, 1024], F32R)
    wu2_d = din("wu2", [44, 128, 1024], F32R)
    wd2_d = din("wd2", [8, 128, 2816], F32R)
    natab_d = din("na_tab", [8, 128, 21, 128])
    namask_d = din("na_mask", [128, 21, 128])
    convw_d = din("g_convw", [128, 12, 5])
    gmask_d = din("g_mask", [128, 8, 128])
    rope_d = din("g_rope", [128, 32, 128])
    adt_d = din("g_adt", [128, 16])
    normw_d = din("g_normw", [128, 128])
    out_d = nc.dram_tensor("out", [SEQ, D], F32, kind="ExternalOutput").ap()

    X1 = dscr("X1", [SEQ, D])
    QAT = dscr("QAT", [512, SEQ])
    KAT = dscr("KAT", [512, NTOK])
    VA = dscr("VA", [NTOK, 512])
    QKVT = dscr("QKVT", [1536, NTOK])
    AB = dscr("AB", [NTOK, 16])
    OA = dscr("OA", [SEQ, 512])
    OB = dscr("OB", [SEQ, 512])
    QKVN = dscr("QKVN", [NTOK, 1536])
    GB = dscr("GB", [NTOK, 16])
    DBG1 = dscr("DBG1", [128, 144])

    st = ExitStack()
    P = Prog(nc, st)
    cur = [st]

    def sb(name, shape, dt=F32):
        uid[0] += 1
        return cur[0].enter_context(nc.sbuf_tensor(f"sb{uid[0]}_{name}", list(shape), dt))

    ident = sb("ident", [128, 128])
    modT = sb("modT", [128, 72, 2])
    mod1p = sb("mod1p", [128, 72, 2])
    modh = sb("modh", [128, 72, 2])
    scT = sb("scT", [128, 8, 2])
    badaT = sb("badaT", [128, 72])
    stats = sb("stats", [128, 8, 16])
    epsc = sb("epsc", [128, 2])
    R_ident, R_mod = Res("ident"), Res("mod")
    P.op("dve", lambda e: e.memset(epsc[:, 0:1], LN_EPS), writes=[R_ident])
    P.op("dve", lambda e: e.memset(epsc[:, 1:2], NORM_EPS), writes=[R_ident])
    ds_misc = P.dsem("misc")

    psum = [st.enter_context(nc.psum_tensor(f"ps{i}", [128, 512], F32)) for i in range(8)]
    R_ps = [Res(f"ps{i}") for i in range(8)]

    ph0 = ExitStack()
    cur[0] = ph0
    wa_all = sb("wa_all", [128, 2, 6144])
    P.dma("sp", ds_misc, ident[:], ident_d[:, :], writes=[R_ident])
    P.dma("sp", ds_misc, scT[:], cT_d[:, :, :], writes=[R_mod])
    P.dma("sp", ds_misc, badaT[:], badaT_d[:, :], writes=[R_mod])
    P.op("act", lambda e: e.activation(out=scT[:], in_=scT[:], func=AF.Silu), reads=[R_mod], writes=[R_mod])
    wa_bufs = [wa_all[:, i, :] for i in range(2)]
    wa_ring = Ring(P, "wa", wa_bufs)
    for pn in range(12):
        ap, res, ds = wa_ring.next()
        apv = ap.rearrange("p (k n) -> p k n", k=8)
        P.dma("sp", ds, apv, wada_d[:, pn * 768:(pn + 1) * 768].rearrange("(k p) n -> p k n", p=128), writes=[res])
        for cc in range(6):
            ch = pn * 6 + cc
            for kc in range(8):
                P.op("pe", lambda e, apv=apv, cc=cc, ch=ch, kc=kc: e.matmul(
                    psum[7][:, ch * 2:ch * 2 + 2], lhsT=apv[:, kc, cc * 128:(cc + 1) * 128], rhs=scT[:, kc, :],
                    start=(kc == 0), stop=(kc == 7)), reads=[res, R_mod], writes=[R_ps[7]])
    ps7v = psum[7][:, 0:144].rearrange("p (c t) -> p c t", t=2)
    for t in range(2):
        P.op("dve", lambda e, t=t: e.tensor_tensor(out=modT[:, :, t], in0=ps7v[:, :, t], in1=badaT[:], op=ALU.add),
             reads=[R_ps[7], R_mod], writes=[R_mod])
    P.op("dve", lambda e: e.tensor_scalar_add(out=mod1p[:], in0=modT[:], scalar1=1.0), reads=[R_mod], writes=[R_mod])
    P.op("dve", lambda e: e.tensor_scalar_mul(out=modh[:], in0=modT[:], scalar1=0.5), reads=[R_mod], writes=[R_mod])
    if "DBG1" in dbg:
        P.dma("sp", ds_misc, DBG1[:, :], modT[:].rearrange("p c t -> p (c t)"), reads=[R_mod])
    P.flush()
    ph0.close()

    stat_i = [0]

    def ln_normalize(src, rsrc, dst, rdst):
        k = stat_i[0] % 8
        stat_i[0] += 1
        s = stats[:, k, :]
        rs = Res("st")
        P.op("dve", lambda e: e.bn_stats(out=s[:, 0:6], in_=src[:, 0:512]), reads=[rsrc], writes=[rs])
        P.op("dve", lambda e: e.bn_stats(out=s[:, 6:12], in_=src[:, 512:1024]), reads=[rsrc], writes=[rs])
        P.op("dve", lambda e: e.bn_aggr(out=s[:, 12:14], in_=s[:, 0:12]), reads=[rs], writes=[rs])
        P.op("act", lambda e: e.activation(out=s[:, 14:15], in_=s[:, 13:14], func=AF.Sqrt, bias=epsc[:, 0:1]),
             reads=[rs, R_ident], writes=[rs])
        P.op("dve", lambda e: e.reciprocal(out=s[:, 14:15], in_=s[:, 14:15]), reads=[rs], writes=[rs])
        P.op("dve", lambda e: e.scalar_tensor_tensor(out=s[:, 15:16], in0=s[:, 12:13], scalar=-1.0, in1=s[:, 14:15],
                                                     op0=ALU.mult, op1=ALU.mult), reads=[rs], writes=[rs])
        P.op("act", lambda e: e.activation(out=dst, in_=src, func=AF.Identity, scale=s[:, 14:15], bias=s[:, 15:16]),
             reads=[rsrc, rs], writes=[rdst])

    class Env:
        pass

    def make_env(rows, merge=False):
        E = Env()
        nr = len(rows)
        lngb = sb("lngb", [128, 2 * nr, D])
        R_lngb = Res("lngb")
        for i, r in enumerate(rows):
            P.dma("sp", ds_misc, lngb[:, i, :], lngb_d[r].partition_broadcast(128), writes=[R_lngb])
            P.dma("sp", ds_misc, lngb[:, nr + i, :], lngb_d[3 + r].partition_broadcast(128), writes=[R_lngb])
        lnslot = {r: i for i, r in enumerate(rows)}
        hx = sb("hx", [128, 4, D])
        zt = sb("zt", [128, 4, D])
        xn_all = sb("xn", [128, 2, D])
        xn = [xn_all[:, i, :] for i in range(2)]
        uTr = sb("uT", [128, 8, 512], F32R)
        Gr = sb("G", [128, NJ, 512], F32R)
        wb_all = sb("wb", [128, 2, 2816], F32R)
        wbs_all = sb("wbs", [128, 6, 1024], F32R)
        wowner = {}
        tmpA_all = sb("tmpA", [128, 3, 512])
        tmpB_all = sb("tmpB", [128, 3, 512])
        R_hx = [Res(f"hx{t}") for t in range(4)]
        R_zt = [Res(f"zt{t}") for t in range(4)]
        R_xn = [Res("xn0"), Res("xn1")]
        R_uT = [Res(f"uT{t}") for t in range(4)]
        R_G = [Res(f"G{j}") for j in range(NJ)]
        tag = "m" if merge else "f"
        ds_hx = [P.dsem(f"{tag}hx{t}") for t in range(4)]
        ds_zt = [P.dsem(f"{tag}zt{t}") for t in range(4)]
        wring = Ring(P, tag + "wb", [wb_all[:, i, :] for i in range(2)])
        wring_s = Ring(P, tag + "wbs", [wbs_all[:, i, :] for i in range(6)])
        tmpA_ring = Ring(P, tag + "tA", [tmpA_all[:, i, :] for i in range(3)])
        tmpB_ring = Ring(P, tag + "tB", [tmpB_all[:, i, :] for i in range(3)])
        xn_i = [0]
        ps_gemm_i = [0]
        ps_tr_i = [0]

        def gemm_ps():
            k = ps_gemm_i[0] % 4
            ps_gemm_i[0] += 1
            return psum[k], R_ps[k]

        def tr_ps():
            k = 4 + ps_tr_i[0] % 3
            ps_tr_i[0] += 1
            return psum[k], R_ps[k]

        class WStream:
            def __init__(self, plan, depth=5):
                self.plan = plan
                self.loaded = []
                self.i = 0
                self.npop = 0
                self.depth = depth
                self.pre()

            def pre(self):
                while len(self.loaded) < self.depth and self.i < len(self.plan):
                    src, ncol = self.plan[self.i]
                    ring = wring_s if ncol <= 1024 else wring
                    slot = ring.slots[ring.i % len(ring.slots)]
                    owner = wowner.get(id(slot[1]))
                    if owner is not None and owner[0] is self and owner[1] >= self.npop:
                        break
                    ap, res, ds = ring.next()
                    wowner[id(res)] = (self, self.i)
                    self.i += 1
                    P.dma("pool", ds, ap[:, 0:ncol], src, writes=[res])
                    self.loaded.append((ap, res))

            def get(self):
                self.pre()
                ap, res = self.loaded.pop(0)
                self.npop += 1
                return ap, res

        def modulate_T(src, rsrc, t, base, sel):
            k = xn_i[0] % 2
            xn_i[0] += 1
            ln_normalize(src, rsrc, xn[k], R_xn[k])
            for half in range(2):
                ps, rps = tr_ps()
                for q in range(4):
                    kc = half * 4 + q
                    P.op("pe", lambda e, ps=ps, q=q, kc=kc, k=k: e.transpose(
                        out=ps[:, q * 128:(q + 1) * 128], in_=xn[k][:, kc * 128:(kc + 1) * 128], identity=ident[:]),
                        reads=[R_xn[k], R_ident], writes=[rps])
                for q in range(4):
                    kc = half * 4 + q
                    ci_shift = base * 8 + kc
                    ci_scale = (base + 1) * 8 + kc
                    dst = uTr[:, kc, t * 128:(t + 1) * 128]
                    if q % 2 == 0:
                        P.op("act", lambda e, ps=ps, q=q, dst=dst, a=ci_scale, b=ci_shift: e.activation(
                            out=dst, in_=ps[:, q * 128:(q + 1) * 128], func=AF.Identity,
                            scale=mod1p[:, a, sel:sel + 1], bias=modT[:, b, sel:sel + 1]),
                            reads=[rps, R_mod], writes=[R_uT[t]])
                    else:
                        P.op("dve", lambda e, ps=ps, q=q, dst=dst, a=ci_scale, b=ci_shift: e.tensor_scalar(
                            out=dst, in0=ps[:, q * 128:(q + 1) * 128], scalar1=mod1p[:, a, sel:sel + 1],
                            scalar2=modT[:, b, sel:sel + 1], op0=ALU.mult, op1=ALU.add),
                            reads=[rps, R_mod], writes=[R_uT[t]])

        def gemm_block(ws, KC, rhs_fn, rhs_res, ntok):
            wap, wres = ws.get()
            ps, rps = gemm_ps()
            for kc in range(KC):
                P.op("pe", lambda e, ps=ps, wap=wap, kc=kc: e.matmul(
                    ps[:, 0:ntok], lhsT=wap[:, kc * 128:(kc + 1) * 128], rhs=rhs_fn(kc),
                    start=(kc == 0), stop=(kc == KC - 1)), reads=[wres] + list(rhs_res), writes=[rps])
            return ps, rps

        def post_ln(buf, rbuf, lnrow):
            i = lnslot[lnrow]
            ln_normalize(buf, rbuf, buf, rbuf)
            P.op("dve", lambda e: e.tensor_tensor(out=buf, in0=buf, in1=lngb[:, i, :], op=ALU.mult),
                 reads=[rbuf, R_lngb], writes=[rbuf])
            P.op("dve", lambda e: e.tensor_tensor(out=buf, in0=buf, in1=lngb[:, nr + i, :], op=ALU.add),
                 reads=[rbuf, R_lngb], writes=[rbuf])

        def back_to_tokens(ps, rps, scale_ap, ntile, src, rsrc, dst, rdst, c):
            ntok = ntile * 128
            tb, rtb, _ = tmpB_ring.next()
            P.op("act", lambda e: e.activation(out=tb[:, 0:ntok], in_=ps[:, 0:ntok], func=AF.Identity, scale=scale_ap),
                 reads=[rps, R_mod], writes=[rtb])
            pt, rpt = tr_ps()
            for t in range(ntile):
                P.op("pe", lambda e, t=t: e.transpose(
                    out=pt[:, t * 128:(t + 1) * 128], in_=tb[:, t * 128:(t + 1) * 128], identity=ident[:]),
                    reads=[rtb, R_ident], writes=[rpt])
            P.op("dve", lambda e: e.scalar_tensor_tensor(
                out=dst[:, 0:ntile, c * 128:(c + 1) * 128], in0=src[:, 0:ntile, c * 128:(c + 1) * 128], scalar=ALPHA,
                in1=pt[:, 0:ntok].rearrange("p (t d) -> p t d", t=ntile), op0=ALU.mult, op1=ALU.add),
                reads=[rpt] + rsrc[:ntile], writes=rdst[:ntile])

        def ffn_sublayer(ntile, sub, sel, wu_d, wd_d, lnrow, src, rsrc, dst, rdst):
            ntok = ntile * 128
            base = 3 * sub
            for t in range(ntile):
                modulate_T(src[:, t, :], rsrc[t], t, base, sel)
            plan = []
            for j in range(NJ):
                plan.append((wu_d[j], 1024))
                plan.append((wu_d[NJ + j], 1024))
            for c in range(8):
                plan.append((wd_d[c], 2816))
            ws = WStream(plan)
            ures = R_uT[:ntile]
            for j in range(NJ):
                psa, rpa = gemm_block(ws, 8, lambda kc: uTr[:, kc, 0:ntok], ures, ntok)
                psb, rpb = gemm_block(ws, 8, lambda kc: uTr[:, kc, 0:ntok], ures, ntok)
                ta, rta, _ = tmpA_ring.next()
                P.op("act", lambda e, ta=ta, psa=psa: e.activation(out=ta[:, 0:ntok], in_=psa[:, 0:ntok], func=AF.Silu),
                     reads=[rpa], writes=[rta])
                P.op("dve", lambda e, ta=ta, psb=psb, j=j: e.tensor_tensor(
                    out=Gr[:, j, 0:ntok], in0=ta[:, 0:ntok], in1=psb[:, 0:ntok], op=ALU.mult),
                    reads=[rta, rpb], writes=[R_G[j]])
            for c in range(8):
                ps, rps = gemm_block(ws, NJ, lambda kc: Gr[:, kc, 0:ntok], R_G, ntok)
                gi = (base + 2) * 8 + c
                back_to_tokens(ps, rps, modh[:, gi, sel:sel + 1], ntile, src, rsrc, dst, rdst, c)
            for t in range(ntile):
                post_ln(dst[:, t, :], rdst[t], lnrow)

        E.__dict__.update(locals())
        return E

    def phase1_group(E, g):
        is_ctx = (g == 8)
        ntile = 2 if is_ctx else 4
        ntok = ntile * 128
        sel = 1 if is_ctx else 0
        tok0 = SEQ if is_ctx else g * 512
        hx, zt, uTr = E.hx, E.zt, E.uTr
        for t in range(ntile):
            src = ctx_d[t * 128:(t + 1) * 128, :] if is_ctx else x_d[g * 512 + t * 128:g * 512 + (t + 1) * 128, :]
            P.dma("sp", E.ds_hx[t], hx[:, t, :], src, writes=[E.R_hx[t]])
        E.ffn_sublayer(ntile, 0, sel, wu1_d, wd1_d, 0, hx, E.R_hx, zt, E.R_zt)
        if not is_ctx:
            for t in range(ntile):
                P.dma("sp", E.ds_zt[t], X1[tok0 + t * 128:tok0 + (t + 1) * 128, :], zt[:, t, :], reads=[E.R_zt[t]])
        for t in range(ntile):
            E.modulate_T(zt[:, t, :], E.R_zt[t], t, 3, sel)
        blocks = list(range(0 if not is_ctx else 4, 24)) + [28]
        ws = E.WStream([(win_d[cb], 1024) for cb in blocks])
        ures = E.R_uT[:ntile]
        for cb in blocks:
            ps, rps = E.gemm_block(ws, 8, lambda kc: uTr[:, kc, 0:ntok], ures, ntok)
            tb, rtb, dsb = E.tmpB_ring.next()
            if cb < 4:
                P.op("act", lambda e, tb=tb, ps=ps: e.activation(out=tb[:, 0:ntok], in_=ps[:, 0:ntok], func=AF.Copy,
                                                                 scale=0.125), reads=[rps], writes=[rtb])
                P.dma("sp", dsb, QAT[cb * 128:(cb + 1) * 128, tok0:tok0 + ntok], tb[:, 0:ntok], reads=[rtb])
            elif cb < 8 or (12 <= cb < 24):
                P.op("dve", lambda e, tb=tb, ps=ps: e.tensor_copy(out=tb[:, 0:ntok], in_=ps[:, 0:ntok]),
                     reads=[rps], writes=[rtb])
                if cb < 8:
                    dst = KAT[(cb - 4) * 128:(cb - 3) * 128, tok0:tok0 + ntok]
                else:
                    dst = QKVT[(cb - 12) * 128:(cb - 11) * 128, tok0:tok0 + ntok]
                P.dma("sp", dsb, dst, tb[:, 0:ntok], reads=[rtb])
            else:
                P.op("act", lambda e, tb=tb, ps=ps: e.activation(out=tb[:, 0:ntok], in_=ps[:, 0:ntok], func=AF.Copy),
                     reads=[rps], writes=[rtb])
                pt, rpt = E.tr_ps()
                for t in range(ntile):
                    P.op("pe", lambda e, pt=pt, tb=tb, t=t: e.transpose(
                        out=pt[:, t * 128:(t + 1) * 128], in_=tb[:, t * 128:(t + 1) * 128], identity=ident[:]),
                        reads=[rtb, R_ident], writes=[rpt])
                ta, rta, dsa = E.tmpA_ring.next()
                P.op("dve", lambda e, ta=ta, pt=pt: e.tensor_copy(out=ta[:, 0:ntok], in_=pt[:, 0:ntok]),
                     reads=[rpt], writes=[rta])
                tav = ta[:, 0:ntok].rearrange("p (t c) -> p t c", t=ntile)
                if cb == 28:
                    dst = AB[tok0:tok0 + ntok, :].rearrange("(t p) c -> p t c", p=128)
                    P.dma("sp", dsa, dst, tav[:, :, 0:16], reads=[rta])
                else:
                    dst = VA[tok0:tok0 + ntok, (cb - 8) * 128:(cb - 7) * 128].rearrange("(t p) c -> p t c", p=128)
                    P.dma("sp", dsa, dst, tav, reads=[rta])

    if 1 in phases:
        ph1 = ExitStack()
        cur[0] = ph1
        E1 = make_env([0])
        for g in groups:
            phase1_group(E1, g)
        P.flush()
        ph1.close()

    if 2 in phases:
        ph2 = ExitStack()
        cur[0] = ph2
        KT = sb("naK", [128, NTOK])
        QT = sb("naQ", [128, SEQ])
        vv = sb("naV", [128, 34, 2, 65])
        tabs = sb("naT", [128, 2, 21, 128])
        msk = sb("naM", [128, 21, 128])
        pT_all = sb("naP", [128, 6, 128])
        sT_all = sb("naS", [128, 4, 128])
        ob_all = sb("naO", [128, 2, 128])
        rec_all = sb("naR", [128, 4])
        R_KT, R_QT, R_vv, R_msk = Res("KT"), Res("QT"), Res("vv"), Res("msk")
        R_tab = [Res("tab0"), Res("tab1")]
        ds_na = P.dsem("na")
        pT_ring = Ring(P, "naP", [pT_all[:, i, :] for i in range(6)], with_dsem=False)
        sT_ring = Ring(P, "naS", [sT_all[:, i, :] for i in range(4)], with_dsem=False)
        ob_ring = Ring(P, "naO", [ob_all[:, i, :] for i in range(2)])
        rec_ring = Ring(P, "naR", [rec_all[:, i:i + 1] for i in range(4)], with_dsem=False)
        P.op("dve", lambda e: e.memset(vv[:, :, :, 64:65], 1.0), writes=[R_vv])
        P.dma("sp", ds_na, msk[:], namask_d[:, :, :], writes=[R_msk])
        cnt_s = [0]
        cnt_o = [0]
        for hp in na_hps:
            P.dma("sp", ds_na, KT[:], KAT[hp * 128:(hp + 1) * 128, :], writes=[R_KT])
            P.dma("sp", ds_na, QT[:], QAT[hp * 128:(hp + 1) * 128, :], writes=[R_QT])
            for h2 in range(2):
                c0 = hp * 128 + h2 * 64
                P.dma("sp", ds_na, vv[:, :, h2, 0:64], VA[:, c0:c0 + 64].rearrange("(t p) c -> p t c", p=128),
                      writes=[R_vv])
                P.dma("sp", ds_na, tabs[:, h2], natab_d[hp * 2 + h2], writes=[R_tab[h2]])
                P.op("dve", lambda e, h2=h2: e.tensor_tensor(out=tabs[:, h2], in0=tabs[:, h2], in1=msk[:], op=ALU.add),
                     reads=[R_tab[h2], R_msk], writes=[R_tab[h2]])
            for b in range(32):
                ob, rob, dsob = ob_ring.next()
                chunks = na_chunks(b) + [(32, None), (33, None)]
                nch = len(chunks)
                pos = []
                for h2 in range(2):
                    kk = 4 + (cnt_o[0] % 4)
                    cnt_o[0] += 1
                    pos.append((psum[kk], R_ps[kk]))

                def pv(pT, rpT, j, ci, h2, nch=nch, pos=pos):
                    po, rpo = pos[h2]
                    P.op("pe", lambda e: e.matmul(po[:, 0:65], lhsT=pT, rhs=vv[:, j, h2, :],
                                                  start=(ci == 0), stop=(ci == nch - 1)),
                         reads=[rpT, R_vv], writes=[rpo])
                pending = []
                for ci, (j, ti) in enumerate(chunks):
                    cur_s = []
                    for h2 in range(2):
                        lo, hi = h2 * 64, (h2 + 1) * 64
                        k = cnt_s[0] % 4
                        cnt_s[0] += 1
                        ps_s, rps_s = psum[k], R_ps[k]
                        P.op("pe", lambda e, ps_s=ps_s, j=j, b=b, lo=lo, hi=hi: e.matmul(
                            ps_s[:, 0:128], lhsT=KT[lo:hi, j * 128:(j + 1) * 128], rhs=QT[lo:hi, b * 128:(b + 1) * 128],
                            start=True, stop=True), reads=[R_KT, R_QT], writes=[rps_s])
                        cur_s.append((ps_s, rps_s))
                    for p in pending:
                        pv(*p)
                    pending = []
                    for h2 in range(2):
                        ps_s, rps_s = cur_s[h2]
                        pT, rpT, _ = pT_ring.next()
                        if ti is not None:
                            sT, rsT, _ = sT_ring.next()
                            P.op("dve", lambda e, sT=sT, ps_s=ps_s, ti=ti, h2=h2: e.tensor_tensor(
                                out=sT, in0=ps_s[:, 0:128], in1=tabs[:, h2, ti, :], op=ALU.add),
                                reads=[rps_s, R_tab[h2]], writes=[rsT])
                            P.op("act", lambda e, pT=pT, sT=sT: e.activation(out=pT, in_=sT, func=AF.Exp),
                                 reads=[rsT], writes=[rpT])
                        else:
                            P.op("act", lambda e, pT=pT, ps_s=ps_s: e.activation(out=pT, in_=ps_s[:, 0:128], func=AF.Exp),
                                 reads=[rps_s], writes=[rpT])
                        pending.append((pT, rpT, j, ci, h2))
                for p in pending:
                    pv(*p)
                for h2 in range(2):
                    lo, hi = h2 * 64, (h2 + 1) * 64
                    po, rpo = pos[h2]
                    rc, rrc, _ = rec_ring.next()
                    P.op("dve", lambda e, rc=rc, po=po: e.reciprocal(out=rc, in_=po[:, 64:65]), reads=[rpo], writes=[rrc])
                    P.op("act", lambda e, rc=rc, po=po, ob=ob, lo=lo, hi=hi: e.activation(
                        out=ob[:, lo:hi], in_=po[:, 0:64], func=AF.Identity, scale=rc), reads=[rpo, rrc], writes=[rob])
                P.dma("sp", dsob, OA[b * 128:(b + 1) * 128, hp * 128:(hp + 1) * 128], ob, reads=[rob])
        P.flush()
        ph2.close()

    if 3 in phases:
        ph3 = ExitStack()
        cur[0] = ph3
        convw = sb("g_convw", [128, 12, 5])
        msk4 = sb("g_msk", [128, 8, 128])
        adt = sb("g_adt", [128, 16])
        nea = sb("g_nea", [128, 8])
        onec = sb("g_onec", [128, 1])
        ones3 = sb("g_ones3", [128, 4, 128])
        R_gc = Res("gconst")
        ds_g = P.dsem("gconst")
        P.dma("sp", ds_g, convw[:], convw_d[:, :, :], writes=[R_gc])
        P.dma("sp", ds_g, msk4[:], gmask_d[:, :, :], writes=[R_gc])
        P.dma("sp", ds_g, adt[:], adt_d[:, :], writes=[R_gc])
        P.op("dve", lambda e: e.memset(onec[:], 1.0), writes=[R_gc])
        P.op("dve", lambda e: e.memset(ones3[:], 1.0), writes=[R_gc])
        P.op("act", lambda e: e.activation(out=nea[:], in_=adt[:, 0:8], func=AF.Exp), reads=[R_gc], writes=[R_gc])
        P.op("dve", lambda e: e.tensor_scalar_mul(out=nea[:], in0=nea[:], scalar1=-1.0), reads=[R_gc], writes=[R_gc])
        gbank_i = [0]

        def gbank():
            k = gbank_i[0] % 8
            gbank_i[0] += 1
            return psum[k], R_ps[k]

        def v3(ap, h=4):
            return ap.rearrange("p (h n) -> p h n", h=h)

        def bc_last(ap2, h=4, n=128):
            return ap2.unsqueeze(2).to_broadcast([128, h, n])

        def bc_mid(ap2, h=4, n=128):
            return ap2.unsqueeze(1).to_broadcast([128, h, n])

        def tile_rows(tau):
            return tau * 128 if tau < 32 else SEQ + (tau - 32) * 128

        R_QKVN = [Res(f"qkvn{t}") for t in range(34)]
        R_GB = [Res(f"gb{t}") for t in range(34)]
        R_OB = [Res(f"ob{t}") for t in range(32)]

        def run_interleaved(gens):
            alive = list(gens)
            while alive:
                for g in list(alive):
                    try:
                        next(g)
                    except StopIteration:
                        alive.remove(g)

        ph3a = ExitStack()
        cur[0] = ph3a
        ropet = sb("g_rope", [128, 32, 128])
        R_rope = Res("rope")
        P.dma("sp", ds_g, ropet[:], rope_d[:, :, :], writes=[R_rope])

        def prep_chain(par):
            cw = sb(f"g_cw{par}", [128, 12, 132])
            acc = sb(f"g_acc{par}", [128, 12, 128])
            tmpc = sb(f"g_tmpc{par}", [128, 12, 128])
            tm = sb(f"g_tm{par}", [128, 1536])
            sq = sb(f"g_sq{par}", [128, 1024])
            rp = sb(f"g_rp{par}", [128, 1024])
            qkn = sb(f"g_qkn{par}", [128, 1024])
            tA = sb(f"g_tA{par}", [128, 512])
            tB = sb(f"g_tB{par}", [128, 512])
            smp = sb(f"g_smp{par}", [128, 16])
            abt = sb(f"g_abt{par}", [128, 16])
            gbt = sb(f"g_gbt{par}", [128, 16])
            R_cw, R_acc, R_tmpc, R_tm, R_sq, R_rp, R_qkn = (Res(n) for n in ("cw", "acc", "tmpc", "tm", "sq", "rp", "qkn"))
            R_tA, R_tB, R_smp, R_abt, R_gbt = (Res(n) for n in ("tA", "tB", "smp", "abt", "gbt"))
            ds_cw, ds_tm, ds_qkn, ds_abt, ds_gbt = (P.dsem(f"{n}{par}") for n in ("g_cw", "g_tm", "g_qkn", "g_abt", "g_gbt"))
            bi = [0]

            def bank():
                k = 4 * par + bi[0] % 4
                bi[0] += 1
                return psum[k], R_ps[k]
            yield
            for tau in range(par, 34, 2):
                t0 = tile_rows(tau)
                seg_lo, seg_hi = (0, SEQ) if tau < 32 else (SEQ, NTOK)
                lo, hi = max(t0 - 2, seg_lo), min(t0 + 130, seg_hi)
                off = lo - (t0 - 2)
                if off > 0:
                    P.op("pool", lambda e: e.memset(cw[:, :, 0:2], 0.0), writes=[R_cw])
                if hi < t0 + 130:
                    P.op("pool", lambda e: e.memset(cw[:, :, 130:132], 0.0), writes=[R_cw])
                P.dma("sp", ds_cw, cw[:, :, off:off + hi - lo], QKVT[:, lo:hi].rearrange("(c p) n -> p c n", p=128),
                      writes=[R_cw])
                P.dma("sp", ds_abt, abt[:], AB[t0:t0 + 128, :], writes=[R_abt])
                yield
                P.op("dve", lambda e: e.tensor_tensor(out=acc[:], in0=cw[:, :, 0:128],
                                                      in1=convw[:, :, 0:1].to_broadcast([128, 12, 128]), op=ALU.mult),
                     reads=[R_cw, R_gc], writes=[R_acc])
                yield
                for k in range(1, 5):
                    P.op("pool", lambda e, k=k: e.tensor_tensor(out=tmpc[:], in0=cw[:, :, k:k + 128],
                                                                in1=convw[:, :, k:k + 1].to_broadcast([128, 12, 128]),
                                                                op=ALU.mult), reads=[R_cw, R_gc], writes=[R_tmpc])
                    yield
                    P.op("dve", lambda e: e.tensor_tensor(out=acc[:], in0=acc[:], in1=tmpc[:], op=ALU.add),
                         reads=[R_acc, R_tmpc], writes=[R_acc])
                    yield
                P.op("act", lambda e: e.activation(out=acc[:], in_=acc[:], func=AF.Silu), reads=[R_acc], writes=[R_acc])
                yield
                P.op("dve", lambda e: e.tensor_tensor(out=gbt[:, 0:8], in0=abt[:, 0:8], in1=adt[:, 8:16], op=ALU.add),
                     reads=[R_abt, R_gc], writes=[R_gbt])
                P.op("act", lambda e: e.activation(out=gbt[:, 0:8], in_=gbt[:, 0:8], func=AF.Exp), reads=[R_gbt], writes=[R_gbt])
                yield
                P.op("act", lambda e: e.activation(out=gbt[:, 0:8], in_=gbt[:, 0:8], func=AF.Ln, bias=onec[:, 0:1]),
                     reads=[R_gbt, R_gc], writes=[R_gbt])
                yield
                P.op("dve", lambda e: e.tensor_tensor(out=gbt[:, 0:8], in0=gbt[:, 0:8], in1=nea[:], op=ALU.mult),
                     reads=[R_gbt, R_gc], writes=[R_gbt])
                P.op("act", lambda e: e.activation(out=gbt[:, 8:16], in_=abt[:, 8:16], func=AF.Sigmoid),
                     reads=[R_abt], writes=[R_gbt])
                P.dma("sp", ds_gbt, GB[t0:t0 + 128, :], gbt[:], reads=[R_gbt], writes=[R_GB[tau]])
                yield
                for c4 in range(3):
                    ps, rps = bank()
                    for q in range(4):
                        P.op("pe", lambda e, ps=ps, q=q, c4=c4: e.transpose(
                            out=ps[:, q * 128:(q + 1) * 128], in_=acc[:, c4 * 4 + q, :], identity=ident[:]),
                            reads=[R_acc, R_ident], writes=[rps])
                    if c4 % 2 == 0:
                        P.op("act", lambda e, ps=ps, c4=c4: e.activation(out=tm[:, c4 * 512:(c4 + 1) * 512], in_=ps[:, :],
                                                                         func=AF.Copy), reads=[rps], writes=[R_tm])
                    else:
                        P.op("dve", lambda e, ps=ps, c4=c4: e.tensor_copy(out=tm[:, c4 * 512:(c4 + 1) * 512], in_=ps[:, :]),
                             reads=[rps], writes=[R_tm])
                    yield
                P.dma("sp", ds_tm, QKVN[t0:t0 + 128, 1024:1536], tm[:, 1024:1536], reads=[R_tm], writes=[R_QKVN[tau]])
                P.op("dve", lambda e: e.tensor_tensor(out=sq[:], in0=tm[:, 0:1024], in1=tm[:, 0:1024], op=ALU.mult),
                     reads=[R_tm], writes=[R_sq])
                yield
                P.op("dve", lambda e: e.reduce_sum(out=smp[:, 0:8], in_=v3(sq[:], 8), axis=mybir.AxisListType.X),
                     reads=[R_sq], writes=[R_smp])
                yield
                P.op("act", lambda e: e.activation(out=smp[:, 8:16], in_=smp[:, 0:8], func=AF.Sqrt, bias=epsc[:, 1:2]),
                     reads=[R_smp, R_ident], writes=[R_smp])
                yield
                P.op("dve", lambda e: e.reciprocal(out=smp[:, 8:16], in_=smp[:, 8:16]), reads=[R_smp], writes=[R_smp])
                yield
                P.op("dve", lambda e: e.tensor_scalar_mul(out=smp[:, 8:12], in0=smp[:, 8:12], scalar1=128.0 ** -0.5),
                     reads=[R_smp], writes=[R_smp])
                yield
                if tau < 32:
                    x5 = tm[:, 0:1024].rearrange("p (i a h f) -> p i a h f", i=8, a=2, h=2)
                    r5 = rp[:].rearrange("p (i a h f) -> p i a h f", i=8, a=2, h=2)
                    cs = ropet[:, tau, :].rearrange("p (s a f) -> p s a f", s=2, a=2)
                    tA4 = tA[:].rearrange("p (i a f) -> p i a f", i=8, a=2)
                    tB4 = tB[:].rearrange("p (i a f) -> p i a f", i=8, a=2)
                    for (xa, xb, half, op) in ((0, 1, 0, ALU.subtract), (1, 0, 1, ALU.add)):
                        for ax in range(2):
                            cosb = cs[:, 0, ax, :].unsqueeze(1).to_broadcast([128, 8, 32])
                            sinb = cs[:, 1, ax, :].unsqueeze(1).to_broadcast([128, 8, 32])
                            P.op("pool", lambda e, xa=xa, ax=ax, cosb=cosb: e.tensor_tensor(
                                out=tA4[:, :, ax, :], in0=x5[:, :, ax, xa, :], in1=cosb, op=ALU.mult),
                                reads=[R_tm, R_rope], writes=[R_tA])
                            P.op("dve", lambda e, xb=xb, ax=ax, sinb=sinb: e.tensor_tensor(
                                out=tB4[:, :, ax, :], in0=x5[:, :, ax, xb, :], in1=sinb, op=ALU.mult),
                                reads=[R_tm, R_rope], writes=[R_tB])
                            yield
                            P.op("dve", lambda e, half=half, op=op, ax=ax: e.tensor_tensor(
                                out=r5[:, :, ax, half, :], in0=tA4[:, :, ax, :], in1=tB4[:, :, ax, :], op=op),
                                reads=[R_tA, R_tB], writes=[R_rp])
                            yield
                    srcqk, rsrc = rp, R_rp
                else:
                    srcqk, rsrc = tm, R_tm
                P.op("dve", lambda e, srcqk=srcqk: e.tensor_tensor(out=v3(qkn[:], 8), in0=v3(srcqk[:, 0:1024], 8),
                                                                   in1=bc_last(smp[:, 8:16], 8), op=ALU.mult),
                     reads=[rsrc, R_smp], writes=[R_qkn])
                P.dma("sp", ds_qkn, QKVN[t0:t0 + 128, 0:1024], qkn[:], reads=[R_qkn], writes=[R_QKVN[tau]])
                yield

        run_interleaved([prep_chain(0)])
        run_interleaved([prep_chain(1)])
        P.flush()
        ph3a.close()

        ph3b = ExitStack()
        cur[0] = ph3b
        names = ["kb", "vb", "kbg", "ktl", "kT", "qT", "kbT", "gbc", "Dd", "tmn", "tmx", "Ee", "Ff", "Em", "Ea", "Fm",
                 "X0", "X1", "XT0", "XT1", "R0", "R1", "Q0", "Q1", "Xf", "XTf", "AT"]
        alias = {"Cm": "gbc", "CmT": "Dd", "W1": "tmn", "W2": "tmx", "uu": "Ee", "wT": "Ff", "vnew": "Em", "o1": "Fm",
                 "obf": "Ea", "prev": "kbT"}

        def scan_chain(d):
            B = {n: sb(f"gs{d}_" + n, [128, 4, 128]) for n in names}
            RB = {n: Res(n) for n in names}
            for k_, v_ in alias.items():
                B[k_] = B[v_]
                RB[k_] = RB[v_]
            qkv_all = sb(f"gs{d}_qkv", [128, 2, 1536])
            gb_all = sb(f"gs{d}_gb", [128, 2, 16])
            qkv_ring = Ring(P, f"gs{d}_qkv", [qkv_all[:, i, :] for i in range(2)])
            gb_ring = Ring(P, f"gs{d}_gb", [gb_all[:, i, :] for i in range(2)])
            sm2 = sb(f"gs{d}_sm2", [128, 24])
            R_sm2 = Res("sm2")
            Sst = sb(f"gs{d}_S", [128, 4, 128])
            R_S = Res("S")
            ds_ob = P.dsem(f"gs{d}_ob")
            ds_prev = P.dsem(f"gs{d}_prev")
            bi = [0]

            def bank():
                k = 4 * d + bi[0] % 4
                bi[0] += 1
                return psum[k], R_ps[k]

            def mm4(lhs_fn, rhs_fn, reads):
                ps, rps = bank()
                for h in range(4):
                    P.op("pe", lambda e, h=h, l=lhs_fn(h), r=rhs_fn(h): e.matmul(
                        ps[:, h * 128:(h + 1) * 128], lhsT=l, rhs=r, start=True, stop=True), reads=reads, writes=[rps])
                return v3(ps[:, :]), rps

            def copy4(ps3, rps, dstn, eng):
                if eng == "act":
                    P.op("act", lambda e: e.activation(out=B[dstn][:], in_=ps3, func=AF.Copy), reads=[rps], writes=[RB[dstn]])
                else:
                    P.op("dve", lambda e: e.tensor_copy(out=B[dstn][:], in_=ps3), reads=[rps], writes=[RB[dstn]])

            def tr4(src, rsrc, dstn, eng):
                ps, rps = bank()
                for h in range(4):
                    P.op("pe", lambda e, h=h, a=src(h): e.transpose(out=ps[:, h * 128:(h + 1) * 128], in_=a, identity=ident[:]),
                         reads=[rsrc, R_ident], writes=[rps])
                copy4(v3(ps[:, :]), rps, dstn, eng)

            def tt(eng, outn, in0, in1, op, reads):
                P.op(eng, lambda e: e.tensor_tensor(out=B[outn][:], in0=in0, in1=in1, op=op), reads=reads, writes=[RB[outn]])

            order = [32, 33] + list(range(32)) if d == 0 else [33, 32] + list(range(31, -1, -1))
            U = msk4[:, 1, :] if d == 0 else msk4[:, 3, :]
            m_s = msk4[:, 0, :] if d == 0 else msk4[:, 2, :]
            m_i = msk4[:, 1, :] if d == 0 else msk4[:, 3, :]
            m_sT = msk4[:, 2, :] if d == 0 else msk4[:, 0, :]
            P.op("dve", lambda e: e.memset(Sst[:], 0.0), writes=[R_S])
            yield
            for tau in order:
                t0 = tile_rows(tau)
                wout = tau < 32
                qkv, rqkv, dsq = qkv_ring.next()
                gbv, rgbv, dsg = gb_ring.next()
                P.dma("sp", dsq, qkv, QKVN[t0:t0 + 128, :], reads=[R_QKVN[tau]], writes=[rqkv])
                P.dma("sp", dsg, gbv, GB[t0:t0 + 128, :], reads=[R_GB[tau]], writes=[rgbv])
                gs = gbv[:, d * 4:(d + 1) * 4]
                bs = gbv[:, 8 + d * 4:8 + (d + 1) * 4]
                qn = v3(qkv[:, 0:512])
                kn = v3(qkv[:, 512:1024])
                vn = v3(qkv[:, 1024:1536])
                ps, rps = bank()
                P.op("pe", lambda e, ps=ps, U=U, gs=gs: e.matmul(ps[:, 0:4], lhsT=U, rhs=gs, start=True, stop=True),
                     reads=[R_gc, rgbv], writes=[rps])
                P.op("pe", lambda e, ps=ps, gs=gs: e.matmul(ps[:, 4:8], lhsT=ones3[:, 0, :], rhs=gs, start=True, stop=True),
                     reads=[R_gc, rgbv], writes=[rps])
                yield
                P.op("dve", lambda e, ps=ps: e.tensor_copy(out=sm2[:, 0:8], in_=ps[:, 0:8]), reads=[rps], writes=[R_sm2])
                yield
                P.op("dve", lambda e: e.tensor_tensor(out=sm2[:, 16:20], in0=sm2[:, 4:8], in1=sm2[:, 0:4], op=ALU.subtract),
                     reads=[R_sm2], writes=[R_sm2])
                yield
                P.op("act", lambda e: e.activation(out=sm2[:, 8:12], in_=sm2[:, 0:4], func=AF.Exp),
                     reads=[R_sm2], writes=[R_sm2])
                P.op("act", lambda e: e.activation(out=sm2[:, 12:16], in_=sm2[:, 4:8], func=AF.Exp), reads=[R_sm2], writes=[R_sm2])
                P.op("act", lambda e: e.activation(out=sm2[:, 16:20], in_=sm2[:, 16:20], func=AF.Exp), reads=[R_sm2], writes=[R_sm2])
                gc, egc, etot, etl = sm2[:, 0:4], sm2[:, 8:12], sm2[:, 12:16], sm2[:, 16:20]
                tt("dve", "kb", kn, bc_last(bs), ALU.mult, [rqkv, rgbv])
                tt("pool", "vb", vn, bc_last(bs), ALU.mult, [rqkv, rgbv])
                tr4(lambda h: kn[:, h, :], rqkv, "kT", "act")
                yield
                tt("pool", "gbc", ones3[:], bc_last(gs), ALU.mult, [R_gc, rgbv])
                tr4(lambda h: B["kb"][:, h, :], RB["kb"], "kbT", "dve")
                yield
                tt("dve", "kbg", B["kb"][:], bc_last(egc), ALU.mult, [RB["kb"], R_sm2])
                tt("pool", "ktl", kn, bc_last(etl), ALU.mult, [rqkv, R_sm2])
                if wout:
                    tr4(lambda h: qn[:, h, :], rqkv, "qT", "act")
                yield
                psD, rpsD = mm4(lambda h: B["gbc"][:, h, :], lambda h: U, [RB["gbc"], R_gc])
                yield
                tt("dve", "Dd", psD, bc_last(gc), ALU.subtract, [rpsD, R_sm2])
                yield
                P.op("dve", lambda e: e.tensor_scalar_min(out=B["tmn"][:], in0=B["Dd"][:], scalar1=0.0),
                     reads=[RB["Dd"]], writes=[RB["tmn"]])
                P.op("pool", lambda e: e.tensor_scalar_max(out=B["tmx"][:], in0=B["Dd"][:], scalar1=0.0),
                     reads=[RB["Dd"]], writes=[RB["tmx"]])
                yield
                P.op("act", lambda e: e.activation(out=B["Ee"][:], in_=B["tmn"][:], func=AF.Exp),
                     reads=[RB["tmn"]], writes=[RB["Ee"]])
                P.op("act", lambda e: e.activation(out=B["Ff"][:], in_=B["tmx"][:], func=AF.Exp, scale=-1.0),
                     reads=[RB["tmx"]], writes=[RB["Ff"]])
                yield
                tt("pool", "Em", B["Ee"][:], bc_mid(m_s), ALU.mult, [RB["Ee"], R_gc])
                tt("dve", "Fm", B["Ff"][:], bc_mid(m_sT), ALU.mult, [RB["Ff"], R_gc])
                psK, rpsK = mm4(lambda h: B["kT"][:, h, :], lambda h: B["kbT"][:, h, :], [RB["kT"], RB["kbT"]])
                psK2, rpsK2 = mm4(lambda h: B["kbT"][:, h, :], lambda h: B["kT"][:, h, :], [RB["kT"], RB["kbT"]])
                yield
                P.op("dve", lambda e, psK=psK: e.scalar_tensor_tensor(out=B["Xf"][:], in0=psK, scalar=-1.0, in1=B["Em"][:],
                                                                      op0=ALU.mult, op1=ALU.mult),
                     reads=[rpsK, RB["Em"]], writes=[RB["Xf"]])
                P.op("dve", lambda e, psK2=psK2: e.scalar_tensor_tensor(out=B["XTf"][:], in0=psK2, scalar=-1.0, in1=B["Fm"][:],
                                                                        op0=ALU.mult, op1=ALU.mult),
                     reads=[rpsK2, RB["Fm"]], writes=[RB["XTf"]])
                yield
                if wout:
                    tt("pool", "Ea", B["Ee"][:], bc_mid(m_i), ALU.mult, [RB["Ee"], R_gc])
                    psA, rpsA = mm4(lambda h: B["kT"][:, h, :], lambda h: B["qT"][:, h, :], [RB["kT"], RB["qT"]])
                    yield
                    tt("dve", "AT", psA, B["Ea"][:], ALU.mult, [rpsA, RB["Ea"]])
                tt("pool", "X0", B["Xf"][:], bc_mid(msk4[:, 4, :]), ALU.mult, [RB["Xf"], R_gc])
                tt("pool", "XT0", B["XTf"][:], bc_mid(msk4[:, 4, :]), ALU.mult, [RB["XTf"], R_gc])
                yield
                tt("dve", "R0", B["X0"][:], bc_mid(ident[:]), ALU.add, [RB["X0"], R_ident])
                tt("dve", "Q0", B["XT0"][:], bc_mid(ident[:]), ALU.add, [RB["XT0"], R_ident])
                yield
                for lvl in range(1, 4):
                    a, b_ = (lvl - 1) % 2, lvl % 2
                    Xa, XTa, Xb, XTb = f"X{a}", f"XT{a}", f"X{b_}", f"XT{b_}"
                    Ra, Rb, Qa, Qb = f"R{a}", f"R{b_}", f"Q{a}", f"Q{b_}"
                    pX, rpX = mm4(lambda h, XTa=XTa: B[XTa][:, h, :], lambda h, Xa=Xa: B[Xa][:, h, :], [RB[Xa], RB[XTa]])
                    pXT, rpXT = mm4(lambda h, Xa=Xa: B[Xa][:, h, :], lambda h, XTa=XTa: B[XTa][:, h, :], [RB[Xa], RB[XTa]])
                    yield
                    copy4(pX, rpX, Xb, "act")
                    copy4(pXT, rpXT, XTb, "dve")
                    yield
                    pR, rpR = mm4(lambda h, XTb=XTb: B[XTb][:, h, :], lambda h, Ra=Ra: B[Ra][:, h, :], [RB[XTb], RB[Ra]])
                    pQ, rpQ = mm4(lambda h, Xb=Xb: B[Xb][:, h, :], lambda h, Qa=Qa: B[Qa][:, h, :], [RB[Xb], RB[Qa]])
                    yield
                    tt("dve", Rb, pR, B[Ra][:], ALU.add, [rpR, RB[Ra]])
                    tt("dve", Qb, pQ, B[Qa][:], ALU.add, [rpQ, RB[Qa]])
                    yield
                curb = 1
                for si in range(3):
                    last = si == 2
                    offm = msk4[:, 5 + si, :]
                    a, b_ = curb, 1 - curb
                    Ra, Rb, Qa, Qb = f"R{a}", f"R{b_}", f"Q{a}", f"Q{b_}"
                    tt("pool", "Cm", B["XTf"][:], bc_mid(offm), ALU.mult, [RB["XTf"], R_gc])
                    if not last:
                        tt("pool", "CmT", B["Xf"][:], bc_mid(offm), ALU.mult, [RB["Xf"], R_gc])
                    yield
                    pW1, rpW1 = mm4(lambda h: B["Cm"][:, h, :], lambda h, Ra=Ra: B[Ra][:, h, :], [RB["Cm"], RB[Ra]])
                    if not last:
                        pW2, rpW2 = mm4(lambda h: B["CmT"][:, h, :], lambda h, Qa=Qa: B[Qa][:, h, :], [RB["CmT"], RB[Qa]])
                    yield
                    copy4(pW1, rpW1, "W1", "act")
                    if not last:
                        copy4(pW2, rpW2, "W2", "dve")
                    yield
                    pY, rpY = mm4(lambda h, Qa=Qa: B[Qa][:, h, :], lambda h: B["W1"][:, h, :], [RB[Qa], RB["W1"]])
                    if not last:
                        pT, rpT = mm4(lambda h, Ra=Ra: B[Ra][:, h, :], lambda h: B["W2"][:, h, :], [RB[Ra], RB["W2"]])
                    yield
                    tt("dve", Rb, pY, B[Ra][:], ALU.add, [rpY, RB[Ra]])
                    if not last:
                        tt("dve", Qb, pT, B[Qa][:], ALU.add, [rpT, RB[Qa]])
                    yield
                    curb = b_
                assert curb == 0
                TT = "R0"
                pU, rpU = mm4(lambda h: B[TT][:, h, :], lambda h: B["vb"][:, h, :], [RB[TT], RB["vb"]])
                pW, rpW = mm4(lambda h: B["kbg"][:, h, :], lambda h: B[TT][:, h, :], [RB[TT], RB["kbg"]])
                yield
                copy4(pU, rpU, "uu", "act")
                copy4(pW, rpW, "wT", "dve")
                yield
                p1, rp1 = mm4(lambda h: B["wT"][:, h, :], lambda h: Sst[:, h, :], [RB["wT"], R_S])
                if wout:
                    p2, rp2 = mm4(lambda h: B["qT"][:, h, :], lambda h: Sst[:, h, :], [RB["qT"], R_S])
                yield
                tt("dve", "vnew", B["uu"][:], p1, ALU.subtract, [RB["uu"], rp1])
                if wout:
                    tt("dve", "o1", p2, bc_last(egc), ALU.mult, [rp2, R_sm2])
                yield
                if wout:
                    p3, rp3 = mm4(lambda h: B["AT"][:, h, :], lambda h: B["vnew"][:, h, :], [RB["AT"], RB["vnew"]])
                p4, rp4 = mm4(lambda h: B["ktl"][:, h, :], lambda h: B["vnew"][:, h, :], [RB["ktl"], RB["vnew"]])
                P.op("pool", lambda e: e.tensor_tensor(out=Sst[:], in0=Sst[:], in1=bc_last(etot), op=ALU.mult),
                     reads=[R_S, R_sm2], writes=[R_S])
                yield
                if wout:
                    tt("dve", "obf", B["o1"][:], p3, ALU.add, [RB["o1"], rp3])
                P.op("dve", lambda e, p4=p4: e.tensor_tensor(out=Sst[:], in0=Sst[:], in1=p4, op=ALU.add),
                     reads=[R_S, rp4], writes=[R_S])
                yield
                if wout:
                    obflat = B["obf"][:].rearrange("p h n -> p (h n)")
                    if R_OB[tau].w is None:
                        P.dma("sp", ds_ob, OB[t0:t0 + 128, :], obflat, reads=[RB["obf"]], writes=[R_OB[tau]])
                    else:
                        P.dma("sp", ds_prev, B["prev"][:].rearrange("p h n -> p (h n)"), OB[t0:t0 + 128, :],
                              reads=[R_OB[tau]], writes=[RB["prev"]])
                        tt("pool", "obf", B["obf"][:], B["prev"][:], ALU.add, [RB["obf"], RB["prev"]])
                        P.dma("sp", ds_ob, OB[t0:t0 + 128, :], obflat, reads=[RB["obf"]], writes=[R_OB[tau]])
                    yield

        run_interleaved([scan_chain(0), scan_chain(1)])
        P.flush()
        ph3b.close()
        ph3.close()

    def phase4_group(E, g):
        tok0 = g * 512
        hx, zt, uTr, Gr = E.hx, E.zt, E.uTr, E.Gr
        for t in range(4):
            P.dma("sp", E.ds_hx[t], hx[:, t, :], X1[tok0 + t * 128:tok0 + (t + 1) * 128, :], writes=[E.R_hx[t]])
        for t in range(4):
            E.modulate_T(hx[:, t, :], E.R_hx[t], t, 3, 0)
        plan = [(win_d[24 + i], 1024) for i in range(4)]
        for c in range(8):
            plan += [(wpa_d[c], 512), (wpb_d[c], 512), (win_d[29 + c], 1024), (win_d[37 + c], 1024)]
        plan += [(wo_d[c], 1024) for c in range(8)]
        ws = E.WStream(plan)
        ures = E.R_uT[:4]
        u_rhs = lambda kc: uTr[:, kc, 0:512]
        for i in range(4):
            ps, rps = E.gemm_block(ws, 8, u_rhs, ures, 512)
            tb, rtb, _ = E.tmpB_ring.next()
            P.op("act", lambda e, tb=tb, ps=ps: e.activation(out=tb[:, :], in_=ps[:, :], func=AF.Copy),
                 reads=[rps], writes=[rtb])
            pt, rpt = E.tr_ps()
            for t in range(4):
                P.op("pe", lambda e, pt=pt, tb=tb, t=t: e.transpose(
                    out=pt[:, t * 128:(t + 1) * 128], in_=tb[:, t * 128:(t + 1) * 128], identity=ident[:]),
                    reads=[rtb, R_ident], writes=[rpt])
            P.op("act", lambda e, pt=pt, i=i: e.activation(
                out=E.zs[:, :, i * 128:(i + 1) * 128], in_=pt[:, :].rearrange("p (t c) -> p t c", t=4), func=AF.Silu),
                reads=[rpt], writes=[E.R_zs])
        for t in range(4):
            oat, roat, dsoa = E.oa_ring.next()
            obt, robt, dsob = E.ob_ring.next()
            r0 = tok0 + t * 128
            P.dma("sp", dsoa, oat, OA[r0:r0 + 128, :], writes=[roat])
            P.dma("sp", dsob, obt, OB[r0:r0 + 128, :], writes=[robt])
            obv = obt.rearrange("p (h d) -> p h d", h=4)
            P.op("dve", lambda e, obt=obt: e.tensor_tensor(out=E.sq[:], in0=obt, in1=obt, op=ALU.mult),
                 reads=[robt], writes=[E.R_sq])
            P.op("dve", lambda e: e.reduce_sum(out=E.ss[:, 0:4], in_=E.sq[:].rearrange("p (h d) -> p h d", h=4),
                                               axis=mybir.AxisListType.X), reads=[E.R_sq], writes=[E.R_ss])
            P.op("act", lambda e: e.activation(out=E.ss[:, 4:8], in_=E.ss[:, 0:4], func=AF.Sqrt, scale=1.0 / 128.0,
                                               bias=epsc[:, 1:2]), reads=[E.R_ss, R_ident], writes=[E.R_ss])
            P.op("dve", lambda e: e.reciprocal(out=E.ss[:, 4:8], in_=E.ss[:, 4:8]), reads=[E.R_ss], writes=[E.R_ss])
            P.op("dve", lambda e, obv=obv: e.tensor_tensor(
                out=obv, in0=obv, in1=E.ss[:, 4:8].unsqueeze(2).to_broadcast([128, 4, 128]), op=ALU.mult),
                reads=[robt, E.R_ss], writes=[robt])
            P.op("dve", lambda e, obv=obv: e.tensor_tensor(
                out=obv, in0=obv, in1=E.normw[:].unsqueeze(1).to_broadcast([128, 4, 128]), op=ALU.mult),
                reads=[robt, E.R_normw], writes=[robt])
            P.op("dve", lambda e, obt=obt, t=t: e.tensor_tensor(out=obt, in0=obt, in1=E.zs[:, t, :], op=ALU.mult),
                 reads=[robt, E.R_zs], writes=[robt])
            for (src, rsrc, base, eng) in ((oat, roat, 0, "act"), (obt, robt, 4, "dve")):
                pt, rpt = E.tr_ps()
                for kc in range(4):
                    P.op("pe", lambda e, pt=pt, src=src, kc=kc: e.transpose(
                        out=pt[:, kc * 128:(kc + 1) * 128], in_=src[:, kc * 128:(kc + 1) * 128], identity=ident[:]),
                        reads=[rsrc, R_ident], writes=[rpt])
                dst = Gr[:, base:base + 4, t * 128:(t + 1) * 128]
                src3 = pt[:, :].rearrange("p (k n) -> p k n", k=4)
                wr = [E.R_G[base + kc] for kc in range(4)]
                if eng == "act":
                    P.op("act", lambda e, dst=dst, src3=src3: e.activation(out=dst, in_=src3, func=AF.Copy),
                         reads=[rpt], writes=wr)
                else:
                    P.op("dve", lambda e, dst=dst, src3=src3: e.tensor_copy(out=dst, in_=src3), reads=[rpt], writes=wr)
        for c in range(8):
            psA, rpA = E.gemm_block(ws, 4, lambda kc: Gr[:, kc, 0:512], E.R_G[0:4], 512)
            psB, rpB = E.gemm_block(ws, 4, lambda kc: Gr[:, 4 + kc, 0:512], E.R_G[4:8], 512)
            psGa, rpGa = E.gemm_block(ws, 8, u_rhs, ures, 512)
            psGb, rpGb = E.gemm_block(ws, 8, u_rhs, ures, 512)
            ta1, rta1, _ = E.tmpA_ring.next()
            ta2, rta2, _ = E.tmpA_ring.next()
            P.op("act", lambda e, ta1=ta1, psGa=psGa: e.activation(out=ta1[:, :], in_=psGa[:, :], func=AF.Sigmoid),
                 reads=[rpGa], writes=[rta1])
            P.op("act", lambda e, ta2=ta2, psGb=psGb: e.activation(out=ta2[:, :], in_=psGb[:, :], func=AF.Sigmoid),
                 reads=[rpGb], writes=[rta2])
            P.op("dve", lambda e, ta1=ta1, psA=psA: e.tensor_tensor(out=ta1[:, :], in0=ta1[:, :], in1=psA[:, :], op=ALU.mult),
                 reads=[rta1, rpA], writes=[rta1])
            P.op("dve", lambda e, ta2=ta2, psB=psB: e.tensor_tensor(out=ta2[:, :], in0=ta2[:, :], in1=psB[:, :], op=ALU.mult),
                 reads=[rta2, rpB], writes=[rta2])
            P.op("dve", lambda e, ta1=ta1, ta2=ta2, c=c: e.tensor_tensor(out=Gr[:, 8 + c, :], in0=ta1[:, :], in1=ta2[:, :],
                                                                          op=ALU.add),
                 reads=[rta1, rta2], writes=[E.R_G[8 + c]])
        for c in range(8):
            ps, rps = E.gemm_block(ws, 8, lambda kc: Gr[:, 8 + kc, 0:512], E.R_G[8:16], 512)
            E.back_to_tokens(ps, rps, modT[:, 5 * 8 + c, 0:1], 4, hx, E.R_hx, zt, E.R_zt, c)
        for t in range(4):
            E.post_ln(zt[:, t, :], E.R_zt[t], 1)
        E.ffn_sublayer(4, 2, 0, wu2_d, wd2_d, 2, zt, E.R_zt, hx, E.R_hx)
        for t in range(4):
            P.dma("sp", E.ds_hx[t], out_d[tok0 + t * 128:tok0 + (t + 1) * 128, :], hx[:, t, :], reads=[E.R_hx[t]])

    if 4 in phases:
        ph4 = ExitStack()
        cur[0] = ph4
        E4 = make_env([1, 2], merge=True)
        E4.zs = sb("m_zs", [128, 4, 512])
        E4.R_zs = Res("zs")
        oa_all = sb("m_oa", [128, 2, 512])
        ob_all = sb("m_ob", [128, 2, 512])
        E4.oa_ring = Ring(P, "m_oa", [oa_all[:, i, :] for i in range(2)])
        E4.ob_ring = Ring(P, "m_ob", [ob_all[:, i, :] for i in range(2)])
        E4.sq = sb("m_sq", [128, 512])
        E4.R_sq = Res("msq")
        E4.ss = sb("m_ss", [128, 8])
        E4.R_ss = Res("mss")
        E4.normw = sb("m_normw", [128, 128])
        E4.R_normw = Res("normw")
        P.dma("sp", ds_misc, E4.normw[:], normw_d[:, :], writes=[E4.R_normw])
        for g in m_groups:
            phase4_group(E4, g)
        P.flush()
        ph4.close()
    P.flush()
    st.close()
    return nc, P


def host_inputs(inputs, b):
    f = np.float32
    g = lambda k: np.asarray(inputs[k], dtype=f)
    m = {}
    m["x"] = np.ascontiguousarray(g("x")[b])
    m["ctx"] = np.ascontiguousarray(g("ctx")[b])
    cT = np.stack([g("c")[b].reshape(8, 128).T, g("c_ctx").reshape(8, 128).T], axis=-1)
    m["cT"] = np.ascontiguousarray(cT)
    m["w_ada"] = np.ascontiguousarray(g("w_ada")[0])
    m["b_adaT"] = np.ascontiguousarray(g("b_ada")[0].reshape(72, 128).T)
    m["lngb"] = np.ascontiguousarray(np.concatenate([g("ln_g")[0], g("ln_b")[0]], axis=0))
    m["ident"] = np.eye(128, dtype=f)
    m["wu1"] = to_blocks(g("ffn1_w_in")[0])
    m["wd1"] = to_blocks(g("ffn1_w_out")[0])
    w_in = g("w_in")[0]
    pad = np.zeros((D, 112), f)
    w_in_p = np.concatenate([w_in[:, :3584], w_in[:, 3584:3600], pad, w_in[:, 3600:]], axis=1)
    m["win"] = to_blocks(w_in_p)
    m["wpa"] = to_blocks(g("w_pa")[0])
    m["wpb"] = to_blocks(g("w_pb")[0])
    m["wo"] = to_blocks(g("w_o")[0])
    m["wu2"] = to_blocks(g("ffn2_w_in")[0])
    m["wd2"] = to_blocks(g("ffn2_w_out")[0])
    if "tabs" not in _NA_CACHE:
        _NA_CACHE["tabs"] = na_tables()
    ridx, cidx, mask = _NA_CACHE["tabs"]
    rpb = g("na_rpb")[0]
    tab = rpb[:, ridx, cidx]
    m["na_tab"] = np.ascontiguousarray(tab.transpose(0, 2, 1, 3))
    m["na_mask"] = np.ascontiguousarray(mask.transpose(1, 0, 2))
    cw = g("gdn_conv_w")[0]
    m["g_convw"] = np.ascontiguousarray(cw.reshape(5, 12, 128).transpose(2, 1, 0))
    if "gconst" not in _NA_CACHE:
        _NA_CACHE["gconst"] = gdn_consts()
    gmask, rope = _NA_CACHE["gconst"]
    m["g_mask"] = gmask
    m["g_rope"] = rope
    adt = np.concatenate([g("gdn_a_log")[0].reshape(8), g("gdn_dt_bias")[0].reshape(8)])
    m["g_adt"] = np.ascontiguousarray(np.broadcast_to(adt[None, :], (128, 16)))
    m["g_normw"] = np.ascontiguousarray(np.broadcast_to(g("gdn_norm_w")[0][None, :], (128, 128)))
    return m


def gdn_consts():
    r = np.arange(128)[:, None]
    c = np.arange(128)[None, :]
    def same(n):
        return (r // n) == (c // n)
    gmask = np.stack([c > r, c >= r, c < r, c <= r, same(16), same(32) & ~same(16), same(64) & ~same(32),
                      ~same(64)], axis=1).astype(np.float32)
    n_freq = 32
    freqs = (10000.0 ** (-np.arange(n_freq, dtype=np.float32) / n_freq)).astype(np.float32)
    t = np.arange(SEQ)
    pos = np.stack([t // GRID_W, t % GRID_W], axis=-1).astype(np.float32)
    ang = (pos[:, :, None] * freqs).astype(np.float32)
    cs = np.stack([np.cos(ang), np.sin(ang)], axis=1).astype(np.float32)
    rope = cs.reshape(32, 128, 128).transpose(1, 0, 2)
    return np.ascontiguousarray(gmask), np.ascontiguousarray(rope)


def kernel(**inputs):
    nc, _ = build_program()
    shared = host_inputs(inputs, 0)
    in_maps = []
    for b in range(8):
        m = dict(shared)
        if b > 0:
            per = host_inputs_core(inputs, b)
            m.update(per)
        in_maps.append(m)
    res = run_bass_kernel_spmd(nc, in_maps, core_ids=list(range(8)))
    return np.stack([np.asarray(r["out"], dtype=np.float32) for r in res.results], axis=0)


def host_inputs_core(inputs, b):
    f = np.float32
    g = lambda k: np.asarray(inputs[k], dtype=f)
    m = {}
    m["x"] = np.ascontiguousarray(g("x")[b])
    m["ctx"] = np.ascontiguousarray(g("ctx")[b])
    cT = np.stack([g("c")[b].reshape(8, 128).T, g("c_ctx").reshape(8, 128).T], axis=-1)
    m["cT"] = np.ascontiguousarray(cT)
    return m
```

```python
import numpy as np
from contextlib import ExitStack
import concourse.bass as bass
import concourse.mybir as mybir
from concourse.bass_utils import run_bass_kernel_spmd

F32 = mybir.dt.float32
F32R = mybir.dt.float32r
AF = mybir.ActivationFunctionType
ALU = mybir.AluOpType

ENGS = ("pe", "act", "dve", "pool", "sp")


class Res:
    __slots__ = ("name", "w", "r")

    def __init__(self, name=""):
        self.name = name
        self.w = None
        self.r = {}


class DSem:
    __slots__ = ("name", "total", "handle")

    def __init__(self, name):
        self.name = name
        self.total = 0
        self.handle = None


class OpRec:
    __slots__ = ("eng", "fn", "waits", "signal", "sigidx", "dsem", "dval", "cwaits", "retired")

    def __init__(self, eng, fn):
        self.eng = eng
        self.fn = fn
        self.waits = []
        self.cwaits = []
        self.signal = False
        self.sigidx = 0
        self.dsem = None
        self.dval = 0
        self.retired = False


class Prog:
    def __init__(self, nc, stack):
        self.nc = nc
        self.stack = stack
        self.ops = {e: [] for e in ENGS}
        self.dsems = []
        self.nops = 0
        self.esem = {e: stack.enter_context(nc.semaphore("s_" + e)) for e in ENGS}
        self.sigcount = {e: 0 for e in ENGS}
        self.known = {e: {} for e in ENGS}

    def dsem(self, name):
        d = DSem(name)
        self.dsems.append(d)
        return d

    def _deps(self, rec, reads, writes):
        eng = rec.eng
        is_dma = rec.dsem is not None
        deps = []
        for r in reads:
            if r.w is not None:
                deps.append((r.w, True))
        for w in writes:
            if w.w is not None:
                deps.append((w.w, False))
            for o in w.r.values():
                deps.append((o, False))
        for o, raw in deps:
            if o.retired:
                continue
            if o.dsem is not None:
                rec.waits.append((o.dsem, max(o.dsem.total, o.dval)))
                continue
            if o.eng == eng and not is_dma:
                if eng == "pe":
                    continue
                if not raw:
                    continue
            o.signal = True
            rec.cwaits.append(o)
        rkey = ("dma", id(rec.dsem)) if is_dma else eng
        for r in reads:
            r.r[rkey] = rec
        for w in writes:
            w.w = rec
            w.r = {}

    def op(self, eng, fn, reads=(), writes=()):
        rec = OpRec(eng, fn)
        self._deps(rec, reads, writes)
        self.ops[eng].append(rec)
        self.nops += 1
        return rec

    def dma(self, queue, dsem, out, in_, reads=(), writes=(), **kw):
        rec = OpRec(queue, lambda e: e.dma_start(out=out, in_=in_, **kw))
        rec.dsem = dsem
        self._deps(rec, reads, writes)
        dsem.total += 16
        rec.dval = dsem.total
        self.ops[queue].append(rec)
        self.nops += 1
        return rec

    def barrier(self):
        last = {}
        for e in ENGS:
            for rec in reversed(self.ops[e]):
                if rec.dsem is None and rec.fn is not None:
                    rec.signal = True
                    last[e] = rec
                    break
        dw = [(d, d.total) for d in self.dsems if d.total > 0]
        for e in ENGS:
            rec = OpRec(e, None)
            rec.waits = list(dw)
            rec.cwaits = [o for k, o in last.items() if k != e]
            self.ops[e].append(rec)

    def flush(self):
        self.barrier()
        nc = self.nc
        esem = self.esem
        for d in self.dsems:
            if d.handle is None:
                d.handle = self.stack.enter_context(nc.semaphore("d_" + d.name))
        for e in ENGS:
            for rec in self.ops[e]:
                if rec.signal and rec.dsem is None and rec.fn is not None:
                    self.sigcount[e] += 1
                    rec.sigidx = self.sigcount[e]

        def run(e, engobj):
            known = self.known[e]
            for rec in self.ops[e]:
                ws = []
                for (d, v) in rec.waits:
                    ws.append((d.handle, id(d), v))
                for o in rec.cwaits:
                    ws.append((esem[o.eng], o.eng, o.sigidx))
                for (h, key, v) in ws:
                    if known.get(key, 0) >= v:
                        continue
                    known[key] = v
                    engobj.wait_ge(h, v)
                if rec.fn is None:
                    continue
                ins = rec.fn(engobj)
                if rec.dsem is not None:
                    ins.then_inc(rec.dsem.handle, 16)
                elif rec.signal:
                    ins.then_inc(esem[e], 1)

        with nc.Block() as block:
            @block.tensor
            def _(pe):
                run("pe", pe)

            @block.scalar
            def _(act):
                run("act", act)

            @block.vector
            def _(dve):
                run("dve", dve)

            @block.gpsimd
            def _(pool):
                run("pool", pool)

            @block.sync
            def _(sp):
                run("sp", sp)
        for e in ENGS:
            for rec in self.ops[e]:
                rec.retired = True
            self.ops[e] = []


class Ring:
    def __init__(self, P, name, aps, with_dsem=True):
        self.slots = []
        for i, ap in enumerate(aps):
            self.slots.append((ap, Res(f"{name}{i}"), P.dsem(f"{name}{i}") if with_dsem else None))
        self.i = 0

    def next(self):
        s = self.slots[self.i % len(self.slots)]
        self.i += 1
        return s


D = 1024
DFF = 2816
NJ = DFF // 128
SEQ = 4096
CTX = 256
NTOK = SEQ + CTX
GRID_W = 64
LN_EPS = 1e-6
NORM_EPS = 1e-6
ALPHA = 2.0 ** 0.25
NEG = -30000.0
N_WIN_BLK = 45


def to_blocks(W):
    K, N = W.shape
    KC, NCB = K // 128, N // 128
    return np.ascontiguousarray(
        W.reshape(KC, 128, NCB, 128).transpose(2, 1, 0, 3).reshape(NCB, 128, KC * 128))


def na_tables():
    def r0(r):
        return min(max(r - 4, 0), 56)

    def c0(c):
        return min(max(c - 8, 0), 48)
    specs = [(2, j) for j in range(0, 5)] + [(0, j) for j in range(4)] + [(1, j) for j in range(4)] \
        + [(30, j) for j in range(28, 32)] + [(31, j) for j in range(28, 32)]
    ridx = np.zeros((21, 128, 128), np.int64)
    cidx = np.zeros((21, 128, 128), np.int64)
    mask = np.full((21, 128, 128), NEG, np.float32)
    for ti, (b, j) in enumerate(specs):
        for k in range(128):
            krow, kcol = 2 * j + k // 64, k % 64
            for q in range(128):
                qrow, qcol = 2 * b + q // 64, q % 64
                ok = (r0(qrow) <= krow < r0(qrow) + 8) and (c0(qcol) <= kcol < c0(qcol) + 16)
                if ok:
                    ridx[ti, k, q] = krow - qrow + 7
                    cidx[ti, k, q] = kcol - qcol + 15
                    mask[ti, k, q] = 0.0
    return ridx, cidx, mask


def na_chunks(b):
    if b == 0:
        return [(j, 5 + j) for j in range(4)]
    if b == 1:
        return [(j, 9 + j) for j in range(4)]
    if b == 30:
        return [(j, 13 + j - 28) for j in range(28, 32)]
    if b == 31:
        return [(j, 17 + j - 28) for j in range(28, 32)]
    return [(b - 2 + d, d) for d in range(5)]


_NA_CACHE = {}


def build_program(dbg=None, groups=tuple(range(9)), phases=(1, 2, 3, 4), na_hps=(0, 1, 2, 3), m_groups=tuple(range(8))):
    nc = bass.Bass("TRN2", target_bir_lowering=False)
    dbg = dbg or set()
    uid = [0]

    def din(name, shape, dt=F32):
        return nc.dram_tensor(name, list(shape), dt, kind="ExternalInput").ap()

    def dscr(name, shape, dt=F32):
        kind = "ExternalOutput" if name in dbg else "Internal"
        return nc.dram_tensor(name, list(shape), dt, kind=kind).ap()

    x_d = din("x", [SEQ, D])
    ctx_d = din("ctx", [CTX, D])
    cT_d = din("cT", [128, 8, 2])
    wada_d = din("w_ada", [D, 9 * D])
    badaT_d = din("b_adaT", [128, 72])
    lngb_d = din("lngb", [6, D])
    ident_d = din("ident", [128, 128])
    wu1_d = din("wu1", [44, 128, 1024], F32R)
    wd1_d = din("wd1", [8, 128, 2816], F32R)
    win_d = din("win", [N_WIN_BLK, 128, 1024], F32R)
    wpa_d = din("wpa", [8, 128, 512], F32R)
    wpb_d = din("wpb", [8, 128, 512], F32R)
    wo_d = din("wo", [8, 128, 1024], F32R)
    wu2_d = din("wu2", [44, 128, 1024], F32R)
    wd2_d = din("wd2", [8, 128, 2816], F32R)
    natab_d = din("na_tab", [8, 128, 21, 128])
    namask_d = din("na_mask", [128, 21, 128])
    convw_d = din("g_convw", [128, 12, 5])
    gmask_d = din("g_mask", [128, 8, 128])
    rope_d = din("g_rope", [128, 32, 128])
    adt_d = din("g_adt", [128, 16])
    normw_d = din("g_normw", [128, 128])
    out_d = nc.dram_tensor("out", [SEQ, D], F32, kind="ExternalOutput").ap()

    X1 = dscr("X1", [SEQ, D])
    QAT = dscr("QAT", [512, SEQ])
    KAT = dscr("KAT", [512, NTOK])
    VA = dscr("VA", [NTOK, 512])
    QKVT = dscr("QKVT", [1536, NTOK])
    AB = dscr("AB", [NTOK, 16])
    OA = dscr("OA", [SEQ, 512])
    OB = dscr("OB", [SEQ, 512])
    QKVN = dscr("QKVN", [NTOK, 1536])
    GB = dscr("GB", [NTOK, 16])
    DBG1 = dscr("DBG1", [128, 144])

    st = ExitStack()
    P = Prog(nc, st)
    cur = [st]

    def sb(name, shape, dt=F32):
        uid[0] += 1
        return cur[0].enter_context(nc.sbuf_tensor(f"sb{uid[0]}_{name}", list(shape), dt))

    ident = sb("ident", [128, 128])
    modT = sb("modT", [128, 72, 2])
    mod1p = sb("mod1p", [128, 72, 2])
    modh = sb("modh", [128, 72, 2])
    scT = sb("scT", [128, 8, 2])
    badaT = sb("badaT", [128, 72])
    stats = sb("stats", [128, 8, 16])
    epsc = sb("epsc", [128, 2])
    R_ident, R_mod = Res("ident"), Res("mod")
    P.op("dve", lambda e: e.memset(epsc[:, 0:1], LN_EPS), writes=[R_ident])
    P.op("dve", lambda e: e.memset(epsc[:, 1:2], NORM_EPS), writes=[R_ident])
    ds_misc = P.dsem("misc")

    psum = [st.enter_context(nc.psum_tensor(f"ps{i}", [128, 512], F32)) for i in range(8)]
    R_ps = [Res(f"ps{i}") for i in range(8)]

    ph0 = ExitStack()
    cur[0] = ph0
    wa_all = sb("wa_all", [128, 2, 6144])
    P.dma("sp", ds_misc, ident[:], ident_d[:, :], writes=[R_ident])
    P.dma("sp", ds_misc, scT[:], cT_d[:, :, :], writes=[R_mod])
    P.dma("sp", ds_misc, badaT[:], badaT_d[:, :], writes=[R_mod])
    P.op("act", lambda e: e.activation(out=scT[:], in_=scT[:], func=AF.Silu), reads=[R_mod], writes=[R_mod])
    wa_bufs = [wa_all[:, i, :] for i in range(2)]
    wa_ring = Ring(P, "wa", wa_bufs)
    for pn in range(12):
        ap, res, ds = wa_ring.next()
        apv = ap.rearrange("p (k n) -> p k n", k=8)
        P.dma("sp", ds, apv, wada_d[:, pn * 768:(pn + 1) * 768].rearrange("(k p) n -> p k n", p=128), writes=[res])
        for cc in range(6):
            ch = pn * 6 + cc
            for kc in range(8):
                P.op("pe", lambda e, apv=apv, cc=cc, ch=ch, kc=kc: e.matmul(
                    psum[7][:, ch * 2:ch * 2 + 2], lhsT=apv[:, kc, cc * 128:(cc + 1) * 128], rhs=scT[:, kc, :],
                    start=(kc == 0), stop=(kc == 7)), reads=[res, R_mod], writes=[R_ps[7]])
    ps7v = psum[7][:, 0:144].rearrange("p (c t) -> p c t", t=2)
    for t in range(2):
        P.op("dve", lambda e, t=t: e.tensor_tensor(out=modT[:, :, t], in0=ps7v[:, :, t], in1=badaT[:], op=ALU.add),
             reads=[R_ps[7], R_mod], writes=[R_mod])
    P.op("dve", lambda e: e.tensor_scalar_add(out=mod1p[:], in0=modT[:], scalar1=1.0), reads=[R_mod], writes=[R_mod])
    P.op("dve", lambda e: e.tensor_scalar_mul(out=modh[:], in0=modT[:], scalar1=0.5), reads=[R_mod], writes=[R_mod])
    if "DBG1" in dbg:
        P.dma("sp", ds_misc, DBG1[:, :], modT[:].rearrange("p c t -> p (c t)"), reads=[R_mod])
    P.flush()
    ph0.close()

    stat_i = [0]

    def ln_normalize(src, rsrc, dst, rdst):
        k = stat_i[0] % 8
        stat_i[0] += 1
        s = stats[:, k, :]
        rs = Res("st")
        P.op("dve", lambda e: e.bn_stats(out=s[:, 0:6], in_=src[:, 0:512]), reads=[rsrc], writes=[rs])
        P.op("dve", lambda e: e.bn_stats(out=s[:, 6:12], in_=src[:, 512:1024]), reads=[rsrc], writes=[rs])
        P.op("dve", lambda e: e.bn_aggr(out=s[:, 12:14], in_=s[:, 0:12]), reads=[rs], writes=[rs])
        P.op("act", lambda e: e.activation(out=s[:, 14:15], in_=s[:, 13:14], func=AF.Sqrt, bias=epsc[:, 0:1]),
             reads=[rs, R_ident], writes=[rs])
        P.op("dve", lambda e: e.reciprocal(out=s[:, 14:15], in_=s[:, 14:15]), reads=[rs], writes=[rs])
        P.op("dve", lambda e: e.scalar_tensor_tensor(out=s[:, 15:16], in0=s[:, 12:13], scalar=-1.0, in1=s[:, 14:15],
                                                     op0=ALU.mult, op1=ALU.mult), reads=[rs], writes=[rs])
        P.op("act", lambda e: e.activation(out=dst, in_=src, func=AF.Identity, scale=s[:, 14:15], bias=s[:, 15:16]),
             reads=[rsrc, rs], writes=[rdst])

    class Env:
        pass

    def make_env(rows, merge=False):
        E = Env()
        nr = len(rows)
        lngb = sb("lngb", [128, 2 * nr, D])
        R_lngb = Res("lngb")
        for i, r in enumerate(rows):
            P.dma("sp", ds_misc, lngb[:, i, :], lngb_d[r].partition_broadcast(128), writes=[R_lngb])
            P.dma("sp", ds_misc, lngb[:, nr + i, :], lngb_d[3 + r].partition_broadcast(128), writes=[R_lngb])
        lnslot = {r: i for i, r in enumerate(rows)}
        hx = sb("hx", [128, 4, D])
        zt = sb("zt", [128, 4, D])
        xn_all = sb("xn", [128, 2, D])
        xn = [xn_all[:, i, :] for i in range(2)]
        uTr = sb("uT", [128, 8, 512], F32R)
        Gr = sb("G", [128, NJ, 512], F32R)
        wb_all = sb("wb", [128, 2, 2816], F32R)
        wbs_all = sb("wbs", [128, 6, 1024], F32R)
        wowner = {}
        tmpA_all = sb("tmpA", [128, 3, 512])
        tmpB_all = sb("tmpB", [128, 3, 512])
        R_hx = [Res(f"hx{t}") for t in range(4)]
        R_zt = [Res(f"zt{t}") for t in range(4)]
        R_xn = [Res("xn0"), Res("xn1")]
        R_uT = [Res(f"uT{t}") for t in range(4)]
        R_G = [Res(f"G{j}") for j in range(NJ)]
        tag = "m" if merge else "f"
        ds_hx = [P.dsem(f"{tag}hx{t}") for t in range(4)]
        ds_zt = [P.dsem(f"{tag}zt{t}") for t in range(4)]
        wring = Ring(P, tag + "wb", [wb_all[:, i, :] for i in range(2)])
        wring_s = Ring(P, tag + "wbs", [wbs_all[:, i, :] for i in range(6)])
        tmpA_ring = Ring(P, tag + "tA", [tmpA_all[:, i, :] for i in range(3)])
        tmpB_ring = Ring(P, tag + "tB", [tmpB_all[:, i, :] for i in range(3)])
        xn_i = [0]
        ps_gemm_i = [0]
        ps_tr_i = [0]

        def gemm_ps():
            k = ps_gemm_i[0] % 4
            ps_gemm_i[0] += 1
            return psum[k], R_ps[k]

        def tr_ps():
            k = 4 + ps_tr_i[0] % 3
            ps_tr_i[0] += 1
            return psum[k], R_ps[k]

        class WStream:
            def __init__(self, plan, depth=5):
                self.plan = plan
                self.loaded = []
                self.i = 0
                self.npop = 0
                self.depth = depth
                self.pre()

            def pre(self):
                while len(self.loaded) < self.depth and self.i < len(self.plan):
                    src, ncol = self.plan[self.i]
                    ring = wring_s if ncol <= 1024 else wring
                    slot = ring.slots[ring.i % len(ring.slots)]
                    owner = wowner.get(id(slot[1]))
                    if owner is not None and owner[0] is self and owner[1] >= self.npop:
                        break
                    ap, res, ds = ring.next()
                    wowner[id(res)] = (self, self.i)
                    self.i += 1
                    P.dma("pool", ds, ap[:, 0:ncol], src, writes=[res])
                    self.loaded.append((ap, res))

            def get(self):
                self.pre()
                ap, res = self.loaded.pop(0)
                self.npop += 1
                return ap, res

        def modulate_T(src, rsrc, t, base, sel):
            k = xn_i[0] % 2
            xn_i[0] += 1
            ln_normalize(src, rsrc, xn[k], R_xn[k])
            for half in range(2):
                ps, rps = tr_ps()
                for q in range(4):
                    kc = half * 4 + q
                    P.op("pe", lambda e, ps=ps, q=q, kc=kc, k=k: e.transpose(
                        out=ps[:, q * 128:(q + 1) * 128], in_=xn[k][:, kc * 128:(kc + 1) * 128], identity=ident[:]),
                        reads=[R_xn[k], R_ident], writes=[rps])
                for q in range(4):
                    kc = half * 4 + q
                    ci_shift = base * 8 + kc
                    ci_scale = (base + 1) * 8 + kc
                    dst = uTr[:, kc, t * 128:(t + 1) * 128]
                    if q % 2 == 0:
                        P.op("act", lambda e, ps=ps, q=q, dst=dst, a=ci_scale, b=ci_shift: e.activation(
                            out=dst, in_=ps[:, q * 128:(q + 1) * 128], func=AF.Identity,
                            scale=mod1p[:, a, sel:sel + 1], bias=modT[:, b, sel:sel + 1]),
                            reads=[rps, R_mod], writes=[R_uT[t]])
                    else:
                        P.op("dve", lambda e, ps=ps, q=q, dst=dst, a=ci_scale, b=ci_shift: e.tensor_scalar(
                            out=dst, in0=ps[:, q * 128:(q + 1) * 128], scalar1=mod1p[:, a, sel:sel + 1],
                            scalar2=modT[:, b, sel:sel + 1], op0=ALU.mult, op1=ALU.add),
                            reads=[rps, R_mod], writes=[R_uT[t]])

        def gemm_block(ws, KC, rhs_fn, rhs_res, ntok):
            wap, wres = ws.get()
            ps, rps = gemm_ps()
            for kc in range(KC):
                P.op("pe", lambda e, ps=ps, wap=wap, kc=kc: e.matmul(
                    ps[:, 0:ntok], lhsT=wap[:, kc * 128:(kc + 1) * 128], rhs=rhs_fn(kc),
                    start=(kc == 0), stop=(kc == KC - 1)), reads=[wres] + list(rhs_res), writes=[rps])
            return ps, rps

        def post_ln(buf, rbuf, lnrow):
            i = lnslot[lnrow]
            ln_normalize(buf, rbuf, buf, rbuf)
            P.op("dve", lambda e: e.tensor_tensor(out=buf, in0=buf, in1=lngb[:, i, :], op=ALU.mult),
                 reads=[rbuf, R_lngb], writes=[rbuf])
            P.op("dve", lambda e: e.tensor_tensor(out=buf, in0=buf, in1=lngb[:, nr + i, :], op=ALU.add),
                 reads=[rbuf, R_lngb], writes=[rbuf])

        deferred = []

        def run_deferred():
            while deferred:
                deferred.pop(0)()

        def back_to_tokens(ps, rps, scale_ap, ntile, src, rsrc, dst, rdst, c):
            ntok = ntile * 128
            run_deferred()
            tb, rtb, _ = tmpB_ring.next()
            P.op("act", lambda e: e.activation(out=tb[:, 0:ntok], in_=ps[:, 0:ntok], func=AF.Identity, scale=scale_ap),
                 reads=[rps, R_mod], writes=[rtb])

            def finish():
                pt, rpt = tr_ps()
                for t in range(ntile):
                    P.op("pe", lambda e, t=t: e.transpose(
                        out=pt[:, t * 128:(t + 1) * 128], in_=tb[:, t * 128:(t + 1) * 128], identity=ident[:]),
                        reads=[rtb, R_ident], writes=[rpt])
                P.op("dve", lambda e: e.scalar_tensor_tensor(
                    out=dst[:, 0:ntile, c * 128:(c + 1) * 128], in0=src[:, 0:ntile, c * 128:(c + 1) * 128], scalar=ALPHA,
                    in1=pt[:, 0:ntok].rearrange("p (t d) -> p t d", t=ntile), op0=ALU.mult, op1=ALU.add),
                    reads=[rpt] + rsrc[:ntile], writes=rdst[:ntile])
            deferred.append(finish)

        def ffn_sublayer(ntile, sub, sel, wu_d, wd_d, lnrow, src, rsrc, dst, rdst):
            ntok = ntile * 128
            base = 3 * sub
            for t in range(ntile):
                modulate_T(src[:, t, :], rsrc[t], t, base, sel)
            plan = []
            for j in range(NJ):
                plan.append((wu_d[j], 1024))
                plan.append((wu_d[NJ + j], 1024))
            for c in range(8):
                plan.append((wd_d[c], 2816))
            ws = WStream(plan)
            ures = R_uT[:ntile]
            for j in range(NJ):
                psa, rpa = gemm_block(ws, 8, lambda kc: uTr[:, kc, 0:ntok], ures, ntok)
                psb, rpb = gemm_block(ws, 8, lambda kc: uTr[:, kc, 0:ntok], ures, ntok)
                ta, rta, _ = tmpA_ring.next()
                P.op("act", lambda e, ta=ta, psa=psa: e.activation(out=ta[:, 0:ntok], in_=psa[:, 0:ntok], func=AF.Silu),
                     reads=[rpa], writes=[rta])
                P.op("dve", lambda e, ta=ta, psb=psb, j=j: e.tensor_tensor(
                    out=Gr[:, j, 0:ntok], in0=ta[:, 0:ntok], in1=psb[:, 0:ntok], op=ALU.mult),
                    reads=[rta, rpb], writes=[R_G[j]])
            for c in range(8):
                ps, rps = gemm_block(ws, NJ, lambda kc: Gr[:, kc, 0:ntok], R_G, ntok)
                gi = (base + 2) * 8 + c
                back_to_tokens(ps, rps, modh[:, gi, sel:sel + 1], ntile, src, rsrc, dst, rdst, c)
            run_deferred()
            for t in range(ntile):
                post_ln(dst[:, t, :], rdst[t], lnrow)

        E.__dict__.update(locals())
        return E

    def phase1_group(E, g):
        is_ctx = (g == 8)
        ntile = 2 if is_ctx else 4
        ntok = ntile * 128
        sel = 1 if is_ctx else 0
        tok0 = SEQ if is_ctx else g * 512
        hx, zt, uTr = E.hx, E.zt, E.uTr
        for t in range(ntile):
            src = ctx_d[t * 128:(t + 1) * 128, :] if is_ctx else x_d[g * 512 + t * 128:g * 512 + (t + 1) * 128, :]
            P.dma("sp", E.ds_hx[t], hx[:, t, :], src, writes=[E.R_hx[t]])
        E.ffn_sublayer(ntile, 0, sel, wu1_d, wd1_d, 0, hx, E.R_hx, zt, E.R_zt)
        if not is_ctx:
            for t in range(ntile):
                P.dma("sp", E.ds_zt[t], X1[tok0 + t * 128:tok0 + (t + 1) * 128, :], zt[:, t, :], reads=[E.R_zt[t]])
        for t in range(ntile):
            E.modulate_T(zt[:, t, :], E.R_zt[t], t, 3, sel)
        blocks = list(range(0 if not is_ctx else 4, 24)) + [28]
        ws = E.WStream([(win_d[cb], 1024) for cb in blocks])
        ures = E.R_uT[:ntile]
        for cb in blocks:
            ps, rps = E.gemm_block(ws, 8, lambda kc: uTr[:, kc, 0:ntok], ures, ntok)
            E.run_deferred()
            tb, rtb, dsb = E.tmpB_ring.next()
            if cb < 4:
                P.op("act", lambda e, tb=tb, ps=ps: e.activation(out=tb[:, 0:ntok], in_=ps[:, 0:ntok], func=AF.Copy,
                                                                 scale=0.125), reads=[rps], writes=[rtb])
                P.dma("sp", dsb, QAT[cb * 128:(cb + 1) * 128, tok0:tok0 + ntok], tb[:, 0:ntok], reads=[rtb])
            elif cb < 8 or (12 <= cb < 24):
                P.op("dve", lambda e, tb=tb, ps=ps: e.tensor_copy(out=tb[:, 0:ntok], in_=ps[:, 0:ntok]),
                     reads=[rps], writes=[rtb])
                if cb < 8:
                    dst = KAT[(cb - 4) * 128:(cb - 3) * 128, tok0:tok0 + ntok]
                else:
                    dst = QKVT[(cb - 12) * 128:(cb - 11) * 128, tok0:tok0 + ntok]
                P.dma("sp", dsb, dst, tb[:, 0:ntok], reads=[rtb])
            else:
                P.op("act", lambda e, tb=tb, ps=ps: e.activation(out=tb[:, 0:ntok], in_=ps[:, 0:ntok], func=AF.Copy),
                     reads=[rps], writes=[rtb])

                def finish(tb=tb, rtb=rtb, cb=cb):
                    pt, rpt = E.tr_ps()
                    for t in range(ntile):
                        P.op("pe", lambda e, pt=pt, tb=tb, t=t: e.transpose(
                            out=pt[:, t * 128:(t + 1) * 128], in_=tb[:, t * 128:(t + 1) * 128], identity=ident[:]),
                            reads=[rtb, R_ident], writes=[rpt])
                    ta, rta, dsa = E.tmpA_ring.next()
                    P.op("dve", lambda e, ta=ta, pt=pt: e.tensor_copy(out=ta[:, 0:ntok], in_=pt[:, 0:ntok]),
                         reads=[rpt], writes=[rta])
                    tav = ta[:, 0:ntok].rearrange("p (t c) -> p t c", t=ntile)
                    if cb == 28:
                        dst = AB[tok0:tok0 + ntok, :].rearrange("(t p) c -> p t c", p=128)
                        P.dma("sp", dsa, dst, tav[:, :, 0:16], reads=[rta])
                    else:
                        dst = VA[tok0:tok0 + ntok, (cb - 8) * 128:(cb - 7) * 128].rearrange("(t p) c -> p t c", p=128)
                        P.dma("sp", dsa, dst, tav, reads=[rta])
                E.deferred.append(finish)
        E.run_deferred()

    if 1 in phases:
        ph1 = ExitStack()
        cur[0] = ph1
        E1 = make_env([0])
        for g in groups:
            phase1_group(E1, g)
        P.flush()
        ph1.close()

    if 2 in phases:
        ph2 = ExitStack()
        cur[0] = ph2
        KT = sb("naK", [128, NTOK])
        QT = sb("naQ", [128, SEQ])
        vv = sb("naV", [128, 34, 2, 65])
        tabs = sb("naT", [128, 2, 21, 128])
        msk = sb("naM", [128, 21, 128])
        pT_all = sb("naP", [128, 6, 128])
        sT_all = sb("naS", [128, 4, 128])
        ob_all = sb("naO", [128, 2, 128])
        rec_all = sb("naR", [128, 4])
        R_KT, R_QT, R_vv, R_msk = Res("KT"), Res("QT"), Res("vv"), Res("msk")
        R_tab = [Res("tab0"), Res("tab1")]
        ds_na = P.dsem("na")
        pT_ring = Ring(P, "naP", [pT_all[:, i, :] for i in range(6)], with_dsem=False)
        sT_ring = Ring(P, "naS", [sT_all[:, i, :] for i in range(4)], with_dsem=False)
        ob_ring = Ring(P, "naO", [ob_all[:, i, :] for i in range(2)])
        rec_ring = Ring(P, "naR", [rec_all[:, i:i + 1] for i in range(4)], with_dsem=False)
        P.op("dve", lambda e: e.memset(vv[:, :, :, 64:65], 1.0), writes=[R_vv])
        P.dma("sp", ds_na, msk[:], namask_d[:, :, :], writes=[R_msk])
        cnt_s = [0]
        cnt_o = [0]
        for hp in na_hps:
            P.dma("sp", ds_na, KT[:], KAT[hp * 128:(hp + 1) * 128, :], writes=[R_KT])
            P.dma("sp", ds_na, QT[:], QAT[hp * 128:(hp + 1) * 128, :], writes=[R_QT])
            for h2 in range(2):
                c0 = hp * 128 + h2 * 64
                P.dma("sp", ds_na, vv[:, :, h2, 0:64], VA[:, c0:c0 + 64].rearrange("(t p) c -> p t c", p=128),
                      writes=[R_vv])
                P.dma("sp", ds_na, tabs[:, h2], natab_d[hp * 2 + h2], writes=[R_tab[h2]])
                P.op("dve", lambda e, h2=h2: e.tensor_tensor(out=tabs[:, h2], in0=tabs[:, h2], in1=msk[:], op=ALU.add),
                     reads=[R_tab[h2], R_msk], writes=[R_tab[h2]])
            for b in range(32):
                ob, rob, dsob = ob_ring.next()
                chunks = na_chunks(b) + [(32, None), (33, None)]
                nch = len(chunks)
                pos = []
                for h2 in range(2):
                    kk = 4 + (cnt_o[0] % 4)
                    cnt_o[0] += 1
                    pos.append((psum[kk], R_ps[kk]))

                def pv(pT, rpT, j, ci, h2, nch=nch, pos=pos):
                    po, rpo = pos[h2]
                    P.op("pe", lambda e: e.matmul(po[:, 0:65], lhsT=pT, rhs=vv[:, j, h2, :],
                                                  start=(ci == 0), stop=(ci == nch - 1)),
                         reads=[rpT, R_vv], writes=[rpo])
                pending = []
                for ci, (j, ti) in enumerate(chunks):
                    cur_s = []
                    for h2 in range(2):
                        lo, hi = h2 * 64, (h2 + 1) * 64
                        k = cnt_s[0] % 4
                        cnt_s[0] += 1
                        ps_s, rps_s = psum[k], R_ps[k]
                        P.op("pe", lambda e, ps_s=ps_s, j=j, b=b, lo=lo, hi=hi: e.matmul(
                            ps_s[:, 0:128], lhsT=KT[lo:hi, j * 128:(j + 1) * 128], rhs=QT[lo:hi, b * 128:(b + 1) * 128],
                            start=True, stop=True), reads=[R_KT, R_QT], writes=[rps_s])
                        cur_s.append((ps_s, rps_s))
                    for p in pending:
                        pv(*p)
                    pending = []
                    for h2 in range(2):
                        ps_s, rps_s = cur_s[h2]
                        pT, rpT, _ = pT_ring.next()
                        if ti is not None:
                            sT, rsT, _ = sT_ring.next()
                            P.op("dve", lambda e, sT=sT, ps_s=ps_s, ti=ti, h2=h2: e.tensor_tensor(
                                out=sT, in0=ps_s[:, 0:128], in1=tabs[:, h2, ti, :], op=ALU.add),
                                reads=[rps_s, R_tab[h2]], writes=[rsT])
                            P.op("act", lambda e, pT=pT, sT=sT: e.activation(out=pT, in_=sT, func=AF.Exp),
                                 reads=[rsT], writes=[rpT])
                        else:
                            P.op("act", lambda e, pT=pT, ps_s=ps_s: e.activation(out=pT, in_=ps_s[:, 0:128], func=AF.Exp),
                                 reads=[rps_s], writes=[rpT])
                        pending.append((pT, rpT, j, ci, h2))
                for p in pending:
                    pv(*p)
                for h2 in range(2):
                    lo, hi = h2 * 64, (h2 + 1) * 64
                    po, rpo = pos[h2]
                    rc, rrc, _ = rec_ring.next()
                    P.op("dve", lambda e, rc=rc, po=po: e.reciprocal(out=rc, in_=po[:, 64:65]), reads=[rpo], writes=[rrc])
                    P.op("act", lambda e, rc=rc, po=po, ob=ob, lo=lo, hi=hi: e.activation(
                        out=ob[:, lo:hi], in_=po[:, 0:64], func=AF.Identity, scale=rc), reads=[rpo, rrc], writes=[rob])
                P.dma("sp", dsob, OA[b * 128:(b + 1) * 128, hp * 128:(hp + 1) * 128], ob, reads=[rob])
        P.flush()
        ph2.close()

    if 3 in phases:
        ph3 = ExitStack()
        cur[0] = ph3
        convw = sb("g_convw", [128, 12, 5])
        msk4 = sb("g_msk", [128, 8, 128])
        adt = sb("g_adt", [128, 16])
        nea = sb("g_nea", [128, 8])
        onec = sb("g_onec", [128, 1])
        ones3 = sb("g_ones3", [128, 4, 128])
        R_gc = Res("gconst")
        ds_g = P.dsem("gconst")
        P.dma("sp", ds_g, convw[:], convw_d[:, :, :], writes=[R_gc])
        P.dma("sp", ds_g, msk4[:], gmask_d[:, :, :], writes=[R_gc])
        P.dma("sp", ds_g, adt[:], adt_d[:, :], writes=[R_gc])
        P.op("dve", lambda e: e.memset(onec[:], 1.0), writes=[R_gc])
        P.op("dve", lambda e: e.memset(ones3[:], 1.0), writes=[R_gc])
        P.op("act", lambda e: e.activation(out=nea[:], in_=adt[:, 0:8], func=AF.Exp), reads=[R_gc], writes=[R_gc])
        P.op("dve", lambda e: e.tensor_scalar_mul(out=nea[:], in0=nea[:], scalar1=-1.0), reads=[R_gc], writes=[R_gc])
        gbank_i = [0]

        def gbank():
            k = gbank_i[0] % 8
            gbank_i[0] += 1
            return psum[k], R_ps[k]

        def v3(ap, h=4):
            return ap.rearrange("p (h n) -> p h n", h=h)

        def bc_last(ap2, h=4, n=128):
            return ap2.unsqueeze(2).to_broadcast([128, h, n])

        def bc_mid(ap2, h=4, n=128):
            return ap2.unsqueeze(1).to_broadcast([128, h, n])

        def tile_rows(tau):
            return tau * 128 if tau < 32 else SEQ + (tau - 32) * 128

        R_QKVN = [Res(f"qkvn{t}") for t in range(34)]
        R_GB = [Res(f"gb{t}") for t in range(34)]
        R_OB = [Res(f"ob{t}") for t in range(32)]

        def run_interleaved(gens):
            alive = list(gens)
            while alive:
                for g in list(alive):
                    try:
                        next(g)
                    except StopIteration:
                        alive.remove(g)

        ph3a = ExitStack()
        cur[0] = ph3a
        ropet = sb("g_rope", [128, 32, 128])
        R_rope = Res("rope")
        P.dma("sp", ds_g, ropet[:], rope_d[:, :, :], writes=[R_rope])

        def prep_chain(par):
            cw = sb(f"g_cw{par}", [128, 12, 132])
            acc = sb(f"g_acc{par}", [128, 12, 128])
            tmpc = sb(f"g_tmpc{par}", [128, 12, 128])
            tm = sb(f"g_tm{par}", [128, 1536])
            sq = sb(f"g_sq{par}", [128, 1024])
            rp = sb(f"g_rp{par}", [128, 1024])
            qkn = sb(f"g_qkn{par}", [128, 1024])
            tA = sb(f"g_tA{par}", [128, 512])
            tB = sb(f"g_tB{par}", [128, 512])
            smp = sb(f"g_smp{par}", [128, 16])
            abt = sb(f"g_abt{par}", [128, 16])
            gbt = sb(f"g_gbt{par}", [128, 16])
            R_cw, R_acc, R_tmpc, R_tm, R_sq, R_rp, R_qkn = (Res(n) for n in ("cw", "acc", "tmpc", "tm", "sq", "rp", "qkn"))
            R_tA, R_tB, R_smp, R_abt, R_gbt = (Res(n) for n in ("tA", "tB", "smp", "abt", "gbt"))
            ds_cw, ds_tm, ds_qkn, ds_abt, ds_gbt = (P.dsem(f"{n}{par}") for n in ("g_cw", "g_tm", "g_qkn", "g_abt", "g_gbt"))
            bi = [0]

            def bank():
                k = 4 * par + bi[0] % 4
                bi[0] += 1
                return psum[k], R_ps[k]
            yield
            for tau in range(par, 34, 2):
                t0 = tile_rows(tau)
                seg_lo, seg_hi = (0, SEQ) if tau < 32 else (SEQ, NTOK)
                lo, hi = max(t0 - 2, seg_lo), min(t0 + 130, seg_hi)
                off = lo - (t0 - 2)
                if off > 0:
                    P.op("pool", lambda e: e.memset(cw[:, :, 0:2], 0.0), writes=[R_cw])
                if hi < t0 + 130:
                    P.op("pool", lambda e: e.memset(cw[:, :, 130:132], 0.0), writes=[R_cw])
                P.dma("sp", ds_cw, cw[:, :, off:off + hi - lo], QKVT[:, lo:hi].rearrange("(c p) n -> p c n", p=128),
                      writes=[R_cw])
                P.dma("sp", ds_abt, abt[:], AB[t0:t0 + 128, :], writes=[R_abt])
                yield
                P.op("dve", lambda e: e.tensor_tensor(out=acc[:], in0=cw[:, :, 0:128],
                                                      in1=convw[:, :, 0:1].to_broadcast([128, 12, 128]), op=ALU.mult),
                     reads=[R_cw, R_gc], writes=[R_acc])
                yield
                for k in range(1, 5):
                    P.op("pool", lambda e, k=k: e.tensor_tensor(out=tmpc[:], in0=cw[:, :, k:k + 128],
                                                                in1=convw[:, :, k:k + 1].to_broadcast([128, 12, 128]),
                                                                op=ALU.mult), reads=[R_cw, R_gc], writes=[R_tmpc])
                    yield
                    P.op("dve", lambda e: e.tensor_tensor(out=acc[:], in0=acc[:], in1=tmpc[:], op=ALU.add),
                         reads=[R_acc, R_tmpc], writes=[R_acc])
                    yield
                P.op("act", lambda e: e.activation(out=acc[:], in_=acc[:], func=AF.Silu), reads=[R_acc], writes=[R_acc])
                yield
                P.op("dve", lambda e: e.tensor_tensor(out=gbt[:, 0:8], in0=abt[:, 0:8], in1=adt[:, 8:16], op=ALU.add),
                     reads=[R_abt, R_gc], writes=[R_gbt])
                P.op("act", lambda e: e.activation(out=gbt[:, 0:8], in_=gbt[:, 0:8], func=AF.Exp), reads=[R_gbt], writes=[R_gbt])
                yield
                P.op("act", lambda e: e.activation(out=gbt[:, 0:8], in_=gbt[:, 0:8], func=AF.Ln, bias=onec[:, 0:1]),
                     reads=[R_gbt, R_gc], writes=[R_gbt])
                yield
                P.op("dve", lambda e: e.tensor_tensor(out=gbt[:, 0:8], in0=gbt[:, 0:8], in1=nea[:], op=ALU.mult),
                     reads=[R_gbt, R_gc], writes=[R_gbt])
                P.op("act", lambda e: e.activation(out=gbt[:, 8:16], in_=abt[:, 8:16], func=AF.Sigmoid),
                     reads=[R_abt], writes=[R_gbt])
                P.dma("sp", ds_gbt, GB[t0:t0 + 128, :], gbt[:], reads=[R_gbt], writes=[R_GB[tau]])
                yield
                for c4 in range(3):
                    ps, rps = bank()
                    for q in range(4):
                        P.op("pe", lambda e, ps=ps, q=q, c4=c4: e.transpose(
                            out=ps[:, q * 128:(q + 1) * 128], in_=acc[:, c4 * 4 + q, :], identity=ident[:]),
                            reads=[R_acc, R_ident], writes=[rps])
                    if c4 % 2 == 0:
                        P.op("act", lambda e, ps=ps, c4=c4: e.activation(out=tm[:, c4 * 512:(c4 + 1) * 512], in_=ps[:, :],
                                                                         func=AF.Copy), reads=[rps], writes=[R_tm])
                    else:
                        P.op("dve", lambda e, ps=ps, c4=c4: e.tensor_copy(out=tm[:, c4 * 512:(c4 + 1) * 512], in_=ps[:, :]),
                             reads=[rps], writes=[R_tm])
                    yield
                P.dma("sp", ds_tm, QKVN[t0:t0 + 128, 1024:1536], tm[:, 1024:1536], reads=[R_tm], writes=[R_QKVN[tau]])
                P.op("dve", lambda e: e.tensor_tensor(out=sq[:], in0=tm[:, 0:1024], in1=tm[:, 0:1024], op=ALU.mult),
                     reads=[R_tm], writes=[R_sq])
                yield
                P.op("dve", lambda e: e.reduce_sum(out=smp[:, 0:8], in_=v3(sq[:], 8), axis=mybir.AxisListType.X),
                     reads=[R_sq], writes=[R_smp])
                yield
                P.op("act", lambda e: e.activation(out=smp[:, 8:16], in_=smp[:, 0:8], func=AF.Sqrt, bias=epsc[:, 1:2]),
                     reads=[R_smp, R_ident], writes=[R_smp])
                yield
                P.op("dve", lambda e: e.reciprocal(out=smp[:, 8:16], in_=smp[:, 8:16]), reads=[R_smp], writes=[R_smp])
                yield
                P.op("dve", lambda e: e.tensor_scalar_mul(out=smp[:, 8:12], in0=smp[:, 8:12], scalar1=128.0 ** -0.5),
                     reads=[R_smp], writes=[R_smp])
                yield
                if tau < 32:
                    x5 = tm[:, 0:1024].rearrange("p (i a h f) -> p i a h f", i=8, a=2, h=2)
                    r5 = rp[:].rearrange("p (i a h f) -> p i a h f", i=8, a=2, h=2)
                    cs = ropet[:, tau, :].rearrange("p (s a f) -> p s a f", s=2, a=2)
                    tA4 = tA[:].rearrange("p (i a f) -> p i a f", i=8, a=2)
                    tB4 = tB[:].rearrange("p (i a f) -> p i a f", i=8, a=2)
                    for (xa, xb, half, op) in ((0, 1, 0, ALU.subtract), (1, 0, 1, ALU.add)):
                        for ax in range(2):
                            cosb = cs[:, 0, ax, :].unsqueeze(1).to_broadcast([128, 8, 32])
                            sinb = cs[:, 1, ax, :].unsqueeze(1).to_broadcast([128, 8, 32])
                            P.op("pool", lambda e, xa=xa, ax=ax, cosb=cosb: e.tensor_tensor(
                                out=tA4[:, :, ax, :], in0=x5[:, :, ax, xa, :], in1=cosb, op=ALU.mult),
                                reads=[R_tm, R_rope], writes=[R_tA])
                            P.op("dve", lambda e, xb=xb, ax=ax, sinb=sinb: e.tensor_tensor(
                                out=tB4[:, :, ax, :], in0=x5[:, :, ax, xb, :], in1=sinb, op=ALU.mult),
                                reads=[R_tm, R_rope], writes=[R_tB])
                            yield
                            P.op("dve", lambda e, half=half, op=op, ax=ax: e.tensor_tensor(
                                out=r5[:, :, ax, half, :], in0=tA4[:, :, ax, :], in1=tB4[:, :, ax, :], op=op),
                                reads=[R_tA, R_tB], writes=[R_rp])
                            yield
                    srcqk, rsrc = rp, R_rp
                else:
                    srcqk, rsrc = tm, R_tm
                P.op("dve", lambda e, srcqk=srcqk: e.tensor_tensor(out=v3(qkn[:], 8), in0=v3(srcqk[:, 0:1024], 8),
                                                                   in1=bc_last(smp[:, 8:16], 8), op=ALU.mult),
                     reads=[rsrc, R_smp], writes=[R_qkn])
                P.dma("sp", ds_qkn, QKVN[t0:t0 + 128, 0:1024], qkn[:], reads=[R_qkn], writes=[R_QKVN[tau]])
                yield

        run_interleaved([prep_chain(0)])
        run_interleaved([prep_chain(1)])
        P.flush()
        ph3a.close()

        ph3b = ExitStack()
        cur[0] = ph3b
        names = ["kb", "vb", "kbg", "ktl", "kT", "qT", "kbT", "gbc", "Dd", "tmn", "tmx", "Ee", "Ff", "Em", "Ea", "Fm",
                 "X0", "X1", "XT0", "XT1", "R0", "R1", "Q0", "Q1", "Xf", "XTf", "AT"]
        alias = {"Cm": "gbc", "CmT": "Dd", "W1": "tmn", "W2": "tmx", "uu": "Ee", "wT": "Ff", "vnew": "Em", "o1": "Fm",
                 "obf": "Ea", "prev": "kbT"}

        def scan_chain(d):
            B = {n: sb(f"gs{d}_" + n, [128, 4, 128]) for n in names}
            RB = {n: Res(n) for n in names}
            for k_, v_ in alias.items():
                B[k_] = B[v_]
                RB[k_] = RB[v_]
            qkv_all = sb(f"gs{d}_qkv", [128, 2, 1536])
            gb_all = sb(f"gs{d}_gb", [128, 2, 16])
            qkv_ring = Ring(P, f"gs{d}_qkv", [qkv_all[:, i, :] for i in range(2)])
            gb_ring = Ring(P, f"gs{d}_gb", [gb_all[:, i, :] for i in range(2)])
            sm2 = sb(f"gs{d}_sm2", [128, 24])
            R_sm2 = Res("sm2")
            Sst = sb(f"gs{d}_S", [128, 4, 128])
            R_S = Res("S")
            ds_ob = P.dsem(f"gs{d}_ob")
            ds_prev = P.dsem(f"gs{d}_prev")
            bi = [0]

            def bank():
                k = 4 * d + bi[0] % 4
                bi[0] += 1
                return psum[k], R_ps[k]

            def mm4(lhs_fn, rhs_fn, reads):
                ps, rps = bank()
                for h in range(4):
                    P.op("pe", lambda e, h=h, l=lhs_fn(h), r=rhs_fn(h): e.matmul(
                        ps[:, h * 128:(h + 1) * 128], lhsT=l, rhs=r, start=True, stop=True), reads=reads, writes=[rps])
                return v3(ps[:, :]), rps

            def copy4(ps3, rps, dstn, eng):
                if eng == "act":
                    P.op("act", lambda e: e.activation(out=B[dstn][:], in_=ps3, func=AF.Copy), reads=[rps], writes=[RB[dstn]])
                else:
                    P.op("dve", lambda e: e.tensor_copy(out=B[dstn][:], in_=ps3), reads=[rps], writes=[RB[dstn]])

            def tr4(src, rsrc, dstn, eng):
                ps, rps = bank()
                for h in range(4):
                    P.op("pe", lambda e, h=h, a=src(h): e.transpose(out=ps[:, h * 128:(h + 1) * 128], in_=a, identity=ident[:]),
                         reads=[rsrc, R_ident], writes=[rps])
                copy4(v3(ps[:, :]), rps, dstn, eng)

            def tt(eng, outn, in0, in1, op, reads):
                P.op(eng, lambda e: e.tensor_tensor(out=B[outn][:], in0=in0, in1=in1, op=op), reads=reads, writes=[RB[outn]])

            order = [32, 33] + list(range(32)) if d == 0 else [33, 32] + list(range(31, -1, -1))
            U = msk4[:, 1, :] if d == 0 else msk4[:, 3, :]
            m_s = msk4[:, 0, :] if d == 0 else msk4[:, 2, :]
            m_i = msk4[:, 1, :] if d == 0 else msk4[:, 3, :]
            m_sT = msk4[:, 2, :] if d == 0 else msk4[:, 0, :]
            P.op("dve", lambda e: e.memset(Sst[:], 0.0), writes=[R_S])
            yield
            for tau in order:
                t0 = tile_rows(tau)
                wout = tau < 32
                qkv, rqkv, dsq = qkv_ring.next()
                gbv, rgbv, dsg = gb_ring.next()
                P.dma("sp", dsq, qkv, QKVN[t0:t0 + 128, :], reads=[R_QKVN[tau]], writes=[rqkv])
                P.dma("sp", dsg, gbv, GB[t0:t0 + 128, :], reads=[R_GB[tau]], writes=[rgbv])
                gs = gbv[:, d * 4:(d + 1) * 4]
                bs = gbv[:, 8 + d * 4:8 + (d + 1) * 4]
                qn = v3(qkv[:, 0:512])
                kn = v3(qkv[:, 512:1024])
                vn = v3(qkv[:, 1024:1536])
                ps, rps = bank()
                P.op("pe", lambda e, ps=ps, U=U, gs=gs: e.matmul(ps[:, 0:4], lhsT=U, rhs=gs, start=True, stop=True),
                     reads=[R_gc, rgbv], writes=[rps])
                P.op("pe", lambda e, ps=ps, gs=gs: e.matmul(ps[:, 4:8], lhsT=ones3[:, 0, :], rhs=gs, start=True, stop=True),
                     reads=[R_gc, rgbv], writes=[rps])
                yield
                P.op("dve", lambda e, ps=ps: e.tensor_copy(out=sm2[:, 0:8], in_=ps[:, 0:8]), reads=[rps], writes=[R_sm2])
                yield
                P.op("dve", lambda e: e.tensor_tensor(out=sm2[:, 16:20], in0=sm2[:, 4:8], in1=sm2[:, 0:4], op=ALU.subtract),
                     reads=[R_sm2], writes=[R_sm2])
                yield
                P.op("act", lambda e: e.activation(out=sm2[:, 8:12], in_=sm2[:, 0:4], func=AF.Exp),
                     reads=[R_sm2], writes=[R_sm2])
                P.op("act", lambda e: e.activation(out=sm2[:, 12:16], in_=sm2[:, 4:8], func=AF.Exp), reads=[R_sm2], writes=[R_sm2])
                P.op("act", lambda e: e.activation(out=sm2[:, 16:20], in_=sm2[:, 16:20], func=AF.Exp), reads=[R_sm2], writes=[R_sm2])
                gc, egc, etot, etl = sm2[:, 0:4], sm2[:, 8:12], sm2[:, 12:16], sm2[:, 16:20]
                tt("dve", "kb", kn, bc_last(bs), ALU.mult, [rqkv, rgbv])
                tt("pool", "vb", vn, bc_last(bs), ALU.mult, [rqkv, rgbv])
                tr4(lambda h: kn[:, h, :], rqkv, "kT", "act")
                yield
                tt("pool", "gbc", ones3[:], bc_last(gs), ALU.mult, [R_gc, rgbv])
                tr4(lambda h: B["kb"][:, h, :], RB["kb"], "kbT", "dve")
                yield
                tt("dve", "kbg", B["kb"][:], bc_last(egc), ALU.mult, [RB["kb"], R_sm2])
                tt("pool", "ktl", kn, bc_last(etl), ALU.mult, [rqkv, R_sm2])
                if wout:
                    tr4(lambda h: qn[:, h, :], rqkv, "qT", "act")
                yield
                psD, rpsD = mm4(lambda h: B["gbc"][:, h, :], lambda h: U, [RB["gbc"], R_gc])
                yield
                tt("dve", "Dd", psD, bc_last(gc), ALU.subtract, [rpsD, R_sm2])
                yield
                P.op("dve", lambda e: e.tensor_scalar_min(out=B["tmn"][:], in0=B["Dd"][:], scalar1=0.0),
                     reads=[RB["Dd"]], writes=[RB["tmn"]])
                P.op("pool", lambda e: e.tensor_scalar_max(out=B["tmx"][:], in0=B["Dd"][:], scalar1=0.0),
                     reads=[RB["Dd"]], writes=[RB["tmx"]])
                yield
                P.op("act", lambda e: e.activation(out=B["Ee"][:], in_=B["tmn"][:], func=AF.Exp),
                     reads=[RB["tmn"]], writes=[RB["Ee"]])
                P.op("act", lambda e: e.activation(out=B["Ff"][:], in_=B["tmx"][:], func=AF.Exp, scale=-1.0),
                     reads=[RB["tmx"]], writes=[RB["Ff"]])
                yield
                tt("pool", "Em", B["Ee"][:], bc_mid(m_s), ALU.mult, [RB["Ee"], R_gc])
                tt("dve", "Fm", B["Ff"][:], bc_mid(m_sT), ALU.mult, [RB["Ff"], R_gc])
                psK, rpsK = mm4(lambda h: B["kT"][:, h, :], lambda h: B["kbT"][:, h, :], [RB["kT"], RB["kbT"]])
                psK2, rpsK2 = mm4(lambda h: B["kbT"][:, h, :], lambda h: B["kT"][:, h, :], [RB["kT"], RB["kbT"]])
                yield
                P.op("dve", lambda e, psK=psK: e.scalar_tensor_tensor(out=B["Xf"][:], in0=psK, scalar=-1.0, in1=B["Em"][:],
                                                                      op0=ALU.mult, op1=ALU.mult),
                     reads=[rpsK, RB["Em"]], writes=[RB["Xf"]])
                P.op("dve", lambda e, psK2=psK2: e.scalar_tensor_tensor(out=B["XTf"][:], in0=psK2, scalar=-1.0, in1=B["Fm"][:],
                                                                        op0=ALU.mult, op1=ALU.mult),
                     reads=[rpsK2, RB["Fm"]], writes=[RB["XTf"]])
                yield
                if wout:
                    tt("pool", "Ea", B["Ee"][:], bc_mid(m_i), ALU.mult, [RB["Ee"], R_gc])
                    psA, rpsA = mm4(lambda h: B["kT"][:, h, :], lambda h: B["qT"][:, h, :], [RB["kT"], RB["qT"]])
                    yield
                    tt("dve", "AT", psA, B["Ea"][:], ALU.mult, [rpsA, RB["Ea"]])
                tt("pool", "X0", B["Xf"][:], bc_mid(msk4[:, 4, :]), ALU.mult, [RB["Xf"], R_gc])
                tt("pool", "XT0", B["XTf"][:], bc_mid(msk4[:, 4, :]), ALU.mult, [RB["XTf"], R_gc])
                yield
                tt("dve", "R0", B["X0"][:], bc_mid(ident[:]), ALU.add, [RB["X0"], R_ident])
                tt("dve", "Q0", B["XT0"][:], bc_mid(ident[:]), ALU.add, [RB["XT0"], R_ident])
                yield
                for lvl in range(1, 4):
                    a, b_ = (lvl - 1) % 2, lvl % 2
                    Xa, XTa, Xb, XTb = f"X{a}", f"XT{a}", f"X{b_}", f"XT{b_}"
                    Ra, Rb, Qa, Qb = f"R{a}", f"R{b_}", f"Q{a}", f"Q{b_}"
                    pX, rpX = mm4(lambda h, XTa=XTa: B[XTa][:, h, :], lambda h, Xa=Xa: B[Xa][:, h, :], [RB[Xa], RB[XTa]])
                    pXT, rpXT = mm4(lambda h, Xa=Xa: B[Xa][:, h, :], lambda h, XTa=XTa: B[XTa][:, h, :], [RB[Xa], RB[XTa]])
                    yield
                    copy4(pX, rpX, Xb, "act")
                    copy4(pXT, rpXT, XTb, "dve")
                    yield
                    pR, rpR = mm4(lambda h, XTb=XTb: B[XTb][:, h, :], lambda h, Ra=Ra: B[Ra][:, h, :], [RB[XTb], RB[Ra]])
                    pQ, rpQ = mm4(lambda h, Xb=Xb: B[Xb][:, h, :], lambda h, Qa=Qa: B[Qa][:, h, :], [RB[Xb], RB[Qa]])
                    yield
                    tt("dve", Rb, pR, B[Ra][:], ALU.add, [rpR, RB[Ra]])
                    tt("dve", Qb, pQ, B[Qa][:], ALU.add, [rpQ, RB[Qa]])
                    yield
                curb = 1
                for si in range(3):
                    last = si == 2
                    offm = msk4[:, 5 + si, :]
                    a, b_ = curb, 1 - curb
                    Ra, Rb, Qa, Qb = f"R{a}", f"R{b_}", f"Q{a}", f"Q{b_}"
                    tt("pool", "Cm", B["XTf"][:], bc_mid(offm), ALU.mult, [RB["XTf"], R_gc])
                    if not last:
                        tt("pool", "CmT", B["Xf"][:], bc_mid(offm), ALU.mult, [RB["Xf"], R_gc])
                    yield
                    pW1, rpW1 = mm4(lambda h: B["Cm"][:, h, :], lambda h, Ra=Ra: B[Ra][:, h, :], [RB["Cm"], RB[Ra]])
                    if not last:
                        pW2, rpW2 = mm4(lambda h: B["CmT"][:, h, :], lambda h, Qa=Qa: B[Qa][:, h, :], [RB["CmT"], RB[Qa]])
                    yield
                    copy4(pW1, rpW1, "W1", "act")
                    if not last:
                        copy4(pW2, rpW2, "W2", "dve")
                    yield
                    pY, rpY = mm4(lambda h, Qa=Qa: B[Qa][:, h, :], lambda h: B["W1"][:, h, :], [RB[Qa], RB["W1"]])
                    if not last:
                        pT, rpT = mm4(lambda h, Ra=Ra: B[Ra][:, h, :], lambda h: B["W2"][:, h, :], [RB[Ra], RB["W2"]])
                    yield
                    tt("dve", Rb, pY, B[Ra][:], ALU.add, [rpY, RB[Ra]])
                    if not last:
                        tt("dve", Qb, pT, B[Qa][:], ALU.add, [rpT, RB[Qa]])
                    yield
                    curb = b_
                assert curb == 0
                TT = "R0"
                pU, rpU = mm4(lambda h: B[TT][:, h, :], lambda h: B["vb"][:, h, :], [RB[TT], RB["vb"]])
                pW, rpW = mm4(lambda h: B["kbg"][:, h, :], lambda h: B[TT][:, h, :], [RB[TT], RB["kbg"]])
                yield
                copy4(pU, rpU, "uu", "act")
                copy4(pW, rpW, "wT", "dve")
                yield
                p1, rp1 = mm4(lambda h: B["wT"][:, h, :], lambda h: Sst[:, h, :], [RB["wT"], R_S])
                if wout:
                    p2, rp2 = mm4(lambda h: B["qT"][:, h, :], lambda h: Sst[:, h, :], [RB["qT"], R_S])
                yield
                tt("dve", "vnew", B["uu"][:], p1, ALU.subtract, [RB["uu"], rp1])
                if wout:
                    tt("dve", "o1", p2, bc_last(egc), ALU.mult, [rp2, R_sm2])
                yield
                if wout:
                    p3, rp3 = mm4(lambda h: B["AT"][:, h, :], lambda h: B["vnew"][:, h, :], [RB["AT"], RB["vnew"]])
                p4, rp4 = mm4(lambda h: B["ktl"][:, h, :], lambda h: B["vnew"][:, h, :], [RB["ktl"], RB["vnew"]])
                P.op("pool", lambda e: e.tensor_tensor(out=Sst[:], in0=Sst[:], in1=bc_last(etot), op=ALU.mult),
                     reads=[R_S, R_sm2], writes=[R_S])
                yield
                if wout:
                    tt("dve", "obf", B["o1"][:], p3, ALU.add, [RB["o1"], rp3])
                P.op("dve", lambda e, p4=p4: e.tensor_tensor(out=Sst[:], in0=Sst[:], in1=p4, op=ALU.add),
                     reads=[R_S, rp4], writes=[R_S])
                yield
                if wout:
                    obflat = B["obf"][:].rearrange("p h n -> p (h n)")
                    if R_OB[tau].w is None:
                        P.dma("sp", ds_ob, OB[t0:t0 + 128, :], obflat, reads=[RB["obf"]], writes=[R_OB[tau]])
                    else:
                        P.dma("sp", ds_prev, B["prev"][:].rearrange("p h n -> p (h n)"), OB[t0:t0 + 128, :],
                              reads=[R_OB[tau]], writes=[RB["prev"]])
                        tt("pool", "obf", B["obf"][:], B["prev"][:], ALU.add, [RB["obf"], RB["prev"]])
                        P.dma("sp", ds_ob, OB[t0:t0 + 128, :], obflat, reads=[RB["obf"]], writes=[R_OB[tau]])
                    yield

        run_interleaved([scan_chain(0), scan_chain(1)])
        P.flush()
        ph3b.close()
        ph3.close()

    def phase4_group(E, g):
        tok0 = g * 512
        hx, zt, uTr, Gr = E.hx, E.zt, E.uTr, E.Gr
        for t in range(4):
            P.dma("sp", E.ds_hx[t], hx[:, t, :], X1[tok0 + t * 128:tok0 + (t + 1) * 128, :], writes=[E.R_hx[t]])
        for t in range(4):
            E.modulate_T(hx[:, t, :], E.R_hx[t], t, 3, 0)
        plan = [(win_d[24 + i], 1024) for i in range(4)]
        for c in range(8):
            plan += [(wpa_d[c], 512), (wpb_d[c], 512), (win_d[29 + c], 1024), (win_d[37 + c], 1024)]
        plan += [(wo_d[c], 1024) for c in range(8)]
        ws = E.WStream(plan)
        ures = E.R_uT[:4]
        u_rhs = lambda kc: uTr[:, kc, 0:512]
        for i in range(4):
            ps, rps = E.gemm_block(ws, 8, u_rhs, ures, 512)
            tb, rtb, _ = E.tmpB_ring.next()
            P.op("act", lambda e, tb=tb, ps=ps: e.activation(out=tb[:, :], in_=ps[:, :], func=AF.Copy),
                 reads=[rps], writes=[rtb])
            pt, rpt = E.tr_ps()
            for t in range(4):
                P.op("pe", lambda e, pt=pt, tb=tb, t=t: e.transpose(
                    out=pt[:, t * 128:(t + 1) * 128], in_=tb[:, t * 128:(t + 1) * 128], identity=ident[:]),
                    reads=[rtb, R_ident], writes=[rpt])
            P.op("act", lambda e, pt=pt, i=i: e.activation(
                out=E.zs[:, :, i * 128:(i + 1) * 128], in_=pt[:, :].rearrange("p (t c) -> p t c", t=4), func=AF.Silu),
                reads=[rpt], writes=[E.R_zs])
        for t in range(4):
            oat, roat, dsoa = E.oa_ring.next()
            obt, robt, dsob = E.ob_ring.next()
            r0 = tok0 + t * 128
            P.dma("sp", dsoa, oat, OA[r0:r0 + 128, :], writes=[roat])
            P.dma("sp", dsob, obt, OB[r0:r0 + 128, :], writes=[robt])
            obv = obt.rearrange("p (h d) -> p h d", h=4)
            P.op("dve", lambda e, obt=obt: e.tensor_tensor(out=E.sq[:], in0=obt, in1=obt, op=ALU.mult),
                 reads=[robt], writes=[E.R_sq])
            P.op("dve", lambda e: e.reduce_sum(out=E.ss[:, 0:4], in_=E.sq[:].rearrange("p (h d) -> p h d", h=4),
                                               axis=mybir.AxisListType.X), reads=[E.R_sq], writes=[E.R_ss])
            P.op("act", lambda e: e.activation(out=E.ss[:, 4:8], in_=E.ss[:, 0:4], func=AF.Sqrt, scale=1.0 / 128.0,
                                               bias=epsc[:, 1:2]), reads=[E.R_ss, R_ident], writes=[E.R_ss])
            P.op("dve", lambda e: e.reciprocal(out=E.ss[:, 4:8], in_=E.ss[:, 4:8]), reads=[E.R_ss], writes=[E.R_ss])
            P.op("dve", lambda e, obv=obv: e.tensor_tensor(
                out=obv, in0=obv, in1=E.ss[:, 4:8].unsqueeze(2).to_broadcast([128, 4, 128]), op=ALU.mult),
                reads=[robt, E.R_ss], writes=[robt])
            P.op("dve", lambda e, obv=obv: e.tensor_tensor(
                out=obv, in0=obv, in1=E.normw[:].unsqueeze(1).to_broadcast([128, 4, 128]), op=ALU.mult),
                reads=[robt, E.R_normw], writes=[robt])
            P.op("dve", lambda e, obt=obt, t=t: e.tensor_tensor(out=obt, in0=obt, in1=E.zs[:, t, :], op=ALU.mult),
                 reads=[robt, E.R_zs], writes=[robt])
            for (src, rsrc, base, eng) in ((oat, roat, 0, "act"), (obt, robt, 4, "dve")):
                pt, rpt = E.tr_ps()
                for kc in range(4):
                    P.op("pe", lambda e, pt=pt, src=src, kc=kc: e.transpose(
                        out=pt[:, kc * 128:(kc + 1) * 128], in_=src[:, kc * 128:(kc + 1) * 128], identity=ident[:]),
                        reads=[rsrc, R_ident], writes=[rpt])
                dst = Gr[:, base:base + 4, t * 128:(t + 1) * 128]
                src3 = pt[:, :].rearrange("p (k n) -> p k n", k=4)
                wr = [E.R_G[base + kc] for kc in range(4)]
                if eng == "act":
                    P.op("act", lambda e, dst=dst, src3=src3: e.activation(out=dst, in_=src3, func=AF.Copy),
                         reads=[rpt], writes=wr)
                else:
                    P.op("dve", lambda e, dst=dst, src3=src3: e.tensor_copy(out=dst, in_=src3), reads=[rpt], writes=wr)
        for c in range(8):
            psA, rpA = E.gemm_block(ws, 4, lambda kc: Gr[:, kc, 0:512], E.R_G[0:4], 512)
            psB, rpB = E.gemm_block(ws, 4, lambda kc: Gr[:, 4 + kc, 0:512], E.R_G[4:8], 512)
            psGa, rpGa = E.gemm_block(ws, 8, u_rhs, ures, 512)
            psGb, rpGb = E.gemm_block(ws, 8, u_rhs, ures, 512)
            ta1, rta1, _ = E.tmpA_ring.next()
            ta2, rta2, _ = E.tmpA_ring.next()
            P.op("act", lambda e, ta1=ta1, psGa=psGa: e.activation(out=ta1[:, :], in_=psGa[:, :], func=AF.Sigmoid),
                 reads=[rpGa], writes=[rta1])
            P.op("act", lambda e, ta2=ta2, psGb=psGb: e.activation(out=ta2[:, :], in_=psGb[:, :], func=AF.Sigmoid),
                 reads=[rpGb], writes=[rta2])
            P.op("dve", lambda e, ta1=ta1, psA=psA: e.tensor_tensor(out=ta1[:, :], in0=ta1[:, :], in1=psA[:, :], op=ALU.mult),
                 reads=[rta1, rpA], writes=[rta1])
            P.op("dve", lambda e, ta2=ta2, psB=psB: e.tensor_tensor(out=ta2[:, :], in0=ta2[:, :], in1=psB[:, :], op=ALU.mult),
                 reads=[rta2, rpB], writes=[rta2])
            P.op("dve", lambda e, ta1=ta1, ta2=ta2, c=c: e.tensor_tensor(out=Gr[:, 8 + c, :], in0=ta1[:, :], in1=ta2[:, :],
                                                                          op=ALU.add),
                 reads=[rta1, rta2], writes=[E.R_G[8 + c]])
        for c in range(8):
            ps, rps = E.gemm_block(ws, 8, lambda kc: Gr[:, 8 + kc, 0:512], E.R_G[8:16], 512)
            E.back_to_tokens(ps, rps, modT[:, 5 * 8 + c, 0:1], 4, hx, E.R_hx, zt, E.R_zt, c)
        E.run_deferred()
        for t in range(4):
            E.post_ln(zt[:, t, :], E.R_zt[t], 1)
        E.ffn_sublayer(4, 2, 0, wu2_d, wd2_d, 2, zt, E.R_zt, hx, E.R_hx)
        for t in range(4):
            P.dma("sp", E.ds_hx[t], out_d[tok0 + t * 128:tok0 + (t + 1) * 128, :], hx[:, t, :], reads=[E.R_hx[t]])

    if 4 in phases:
        ph4 = ExitStack()
        cur[0] = ph4
        E4 = make_env([1, 2], merge=True)
        E4.zs = sb("m_zs", [128, 4, 512])
        E4.R_zs = Res("zs")
        oa_all = sb("m_oa", [128, 2, 512])
        ob_all = sb("m_ob", [128, 2, 512])
        E4.oa_ring = Ring(P, "m_oa", [oa_all[:, i, :] for i in range(2)])
        E4.ob_ring = Ring(P, "m_ob", [ob_all[:, i, :] for i in range(2)])
        E4.sq = sb("m_sq", [128, 512])
        E4.R_sq = Res("msq")
        E4.ss = sb("m_ss", [128, 8])
        E4.R_ss = Res("mss")
        E4.normw = sb("m_normw", [128, 128])
        E4.R_normw = Res("normw")
        P.dma("sp", ds_misc, E4.normw[:], normw_d[:, :], writes=[E4.R_normw])
        for g in m_groups:
            phase4_group(E4, g)
        P.flush()
        ph4.close()
    P.flush()
    st.close()
    return nc, P


def host_inputs(inputs, b):
    f = np.float32
    g = lambda k: np.asarray(inputs[k], dtype=f)
    m = {}
    m["x"] = np.ascontiguousarray(g("x")[b])
    m["ctx"] = np.ascontiguousarray(g("ctx")[b])
    cT = np.stack([g("c")[b].reshape(8, 128).T, g("c_ctx").reshape(8, 128).T], axis=-1)
    m["cT"] = np.ascontiguousarray(cT)
    m["w_ada"] = np.ascontiguousarray(g("w_ada")[0])
    m["b_adaT"] = np.ascontiguousarray(g("b_ada")[0].reshape(72, 128).T)
    m["lngb"] = np.ascontiguousarray(np.concatenate([g("ln_g")[0], g("ln_b")[0]], axis=0))
    m["ident"] = np.eye(128, dtype=f)
    m["wu1"] = to_blocks(g("ffn1_w_in")[0])
    m["wd1"] = to_blocks(g("ffn1_w_out")[0])
    w_in = g("w_in")[0]
    pad = np.zeros((D, 112), f)
    w_in_p = np.concatenate([w_in[:, :3584], w_in[:, 3584:3600], pad, w_in[:, 3600:]], axis=1)
    m["win"] = to_blocks(w_in_p)
    m["wpa"] = to_blocks(g("w_pa")[0])
    m["wpb"] = to_blocks(g("w_pb")[0])
    m["wo"] = to_blocks(g("w_o")[0])
    m["wu2"] = to_blocks(g("ffn2_w_in")[0])
    m["wd2"] = to_blocks(g("ffn2_w_out")[0])
    if "tabs" not in _NA_CACHE:
        _NA_CACHE["tabs"] = na_tables()
    ridx, cidx, mask = _NA_CACHE["tabs"]
    rpb = g("na_rpb")[0]
    tab = rpb[:, ridx, cidx]
    m["na_tab"] = np.ascontiguousarray(tab.transpose(0, 2, 1, 3))
    m["na_mask"] = np.ascontiguousarray(mask.transpose(1, 0, 2))
    cw = g("gdn_conv_w")[0]
    m["g_convw"] = np.ascontiguousarray(cw.reshape(5, 12, 128).transpose(2, 1, 0))
    if "gconst" not in _NA_CACHE:
        _NA_CACHE["gconst"] = gdn_consts()
    gmask, rope = _NA_CACHE["gconst"]
    m["g_mask"] = gmask
    m["g_rope"] = rope
    adt = np.concatenate([g("gdn_a_log")[0].reshape(8), g("gdn_dt_bias")[0].reshape(8)])
    m["g_adt"] = np.ascontiguousarray(np.broadcast_to(adt[None, :], (128, 16)))
    m["g_normw"] = np.ascontiguousarray(np.broadcast_to(g("gdn_norm_w")[0][None, :], (128, 128)))
    return m


def gdn_consts():
    r = np.arange(128)[:, None]
    c = np.arange(128)[None, :]
    def same(n):
        return (r // n) == (c // n)
    gmask = np.stack([c > r, c >= r, c < r, c <= r, same(16), same(32) & ~same(16), same(64) & ~same(32),
                      ~same(64)], axis=1).astype(np.float32)
    n_freq = 32
    freqs = (10000.0 ** (-np.arange(n_freq, dtype=np.float32) / n_freq)).astype(np.float32)
    t = np.arange(SEQ)
    pos = np.stack([t // GRID_W, t % GRID_W], axis=-1).astype(np.float32)
    ang = (pos[:, :, None] * freqs).astype(np.float32)
    cs = np.stack([np.cos(ang), np.sin(ang)], axis=1).astype(np.float32)
    rope = cs.reshape(32, 128, 128).transpose(1, 0, 2)
    return np.ascontiguousarray(gmask), np.ascontiguousarray(rope)


def kernel(**inputs):
    nc, _ = build_program()
    shared = host_inputs(inputs, 0)
    in_maps = []
    for b in range(8):
        m = dict(shared)
        if b > 0:
            per = host_inputs_core(inputs, b)
            m.update(per)
        in_maps.append(m)
    res = run_bass_kernel_spmd(nc, in_maps, core_ids=list(range(8)))
    return np.stack([np.asarray(r["out"], dtype=np.float32) for r in res.results], axis=0)


def host_inputs_core(inputs, b):
    f = np.float32
    g = lambda k: np.asarray(inputs[k], dtype=f)
    m = {}
    m["x"] = np.ascontiguousarray(g("x")[b])
    m["ctx"] = np.ascontiguousarray(g("ctx")[b])
    cT = np.stack([g("c")[b].reshape(8, 128).T, g("c_ctx").reshape(8, 128).T], axis=-1)
    m["cT"] = np.ascontiguousarray(cT)
    return m
```

```python
import numpy as np
from contextlib import ExitStack
import concourse.bass as bass
import concourse.mybir as mybir
from concourse.bass_utils import run_bass_kernel_spmd

F32 = mybir.dt.float32
F32R = mybir.dt.float32r
AF = mybir.ActivationFunctionType
ALU = mybir.AluOpType

ENGS = ("pe", "act", "dve", "pool", "sp")


class Res:
    __slots__ = ("name", "w", "r")

    def __init__(self, name=""):
        self.name = name
        self.w = None
        self.r = {}


class DSem:
    __slots__ = ("name", "total", "handle")

    def __init__(self, name):
        self.name = name
        self.total = 0
        self.handle = None


class OpRec:
    __slots__ = ("eng", "fn", "waits", "signal", "sigidx", "dsem", "dval", "cwaits", "retired")

    def __init__(self, eng, fn):
        self.eng = eng
        self.fn = fn
        self.waits = []
        self.cwaits = []
        self.signal = False
        self.sigidx = 0
        self.dsem = None
        self.dval = 0
        self.retired = False


class Prog:
    def __init__(self, nc, stack):
        self.nc = nc
        self.stack = stack
        self.ops = {e: [] for e in ENGS}
        self.dsems = []
        self.nops = 0
        self.esem = {e: stack.enter_context(nc.semaphore("s_" + e)) for e in ENGS}
        self.sigcount = {e: 0 for e in ENGS}
        self.known = {e: {} for e in ENGS}

    def dsem(self, name):
        d = DSem(name)
        self.dsems.append(d)
        return d

    def _deps(self, rec, reads, writes):
        eng = rec.eng
        is_dma = rec.dsem is not None
        deps = []
        for r in reads:
            if r.w is not None:
                deps.append((r.w, True))
        for w in writes:
            if w.w is not None:
                deps.append((w.w, False))
            for o in w.r.values():
                deps.append((o, False))
        for o, raw in deps:
            if o.retired:
                continue
            if o.dsem is not None:
                rec.waits.append((o.dsem, max(o.dsem.total, o.dval)))
                continue
            if o.eng == eng and not is_dma:
                if eng == "pe":
                    continue
                if not raw:
                    continue
            o.signal = True
            rec.cwaits.append(o)
        rkey = ("dma", id(rec.dsem)) if is_dma else eng
        for r in reads:
            r.r[rkey] = rec
        for w in writes:
            w.w = rec
            w.r = {}

    def op(self, eng, fn, reads=(), writes=()):
        rec = OpRec(eng, fn)
        self._deps(rec, reads, writes)
        self.ops[eng].append(rec)
        self.nops += 1
        return rec

    def dma(self, queue, dsem, out, in_, reads=(), writes=(), **kw):
        rec = OpRec(queue, lambda e: e.dma_start(out=out, in_=in_, **kw))
        rec.dsem = dsem
        self._deps(rec, reads, writes)
        dsem.total += 16
        rec.dval = dsem.total
        self.ops[queue].append(rec)
        self.nops += 1
        return rec

    def barrier(self):
        last = {}
        for e in ENGS:
            for rec in reversed(self.ops[e]):
                if rec.dsem is None and rec.fn is not None:
                    rec.signal = True
                    last[e] = rec
                    break
        dw = [(d, d.total) for d in self.dsems if d.total > 0]
        for e in ENGS:
            rec = OpRec(e, None)
            rec.waits = list(dw)
            rec.cwaits = [o for k, o in last.items() if k != e]
            self.ops[e].append(rec)

    def flush(self):
        self.barrier()
        nc = self.nc
        esem = self.esem
        for d in self.dsems:
            if d.handle is None:
                d.handle = self.stack.enter_context(nc.semaphore("d_" + d.name))
        for e in ENGS:
            for rec in self.ops[e]:
                if rec.signal and rec.dsem is None and rec.fn is not None:
                    self.sigcount[e] += 1
                    rec.sigidx = self.sigcount[e]

        def run(e, engobj):
            known = self.known[e]
            for rec in self.ops[e]:
                ws = []
                for (d, v) in rec.waits:
                    ws.append((d.handle, id(d), v))
                for o in rec.cwaits:
                    ws.append((esem[o.eng], o.eng, o.sigidx))
                for (h, key, v) in ws:
                    if known.get(key, 0) >= v:
                        continue
                    known[key] = v
                    engobj.wait_ge(h, v)
                if rec.fn is None:
                    continue
                ins = rec.fn(engobj)
                if rec.dsem is not None:
                    ins.then_inc(rec.dsem.handle, 16)
                elif rec.signal:
                    ins.then_inc(esem[e], 1)

        with nc.Block() as block:
            @block.tensor
            def _(pe):
                run("pe", pe)

            @block.scalar
            def _(act):
                run("act", act)

            @block.vector
            def _(dve):
                run("dve", dve)

            @block.gpsimd
            def _(pool):
                run("pool", pool)

            @block.sync
            def _(sp):
                run("sp", sp)
        for e in ENGS:
            for rec in self.ops[e]:
                rec.retired = True
            self.ops[e] = []


class Ring:
    def __init__(self, P, name, aps, with_dsem=True):
        self.slots = []
        for i, ap in enumerate(aps):
            self.slots.append((ap, Res(f"{name}{i}"), P.dsem(f"{name}{i}") if with_dsem else None))
        self.i = 0

    def next(self):
        s = self.slots[self.i % len(self.slots)]
        self.i += 1
        return s


D = 1024
DFF = 2816
NJ = DFF // 128
SEQ = 4096
CTX = 256
NTOK = SEQ + CTX
GRID_W = 64
LN_EPS = 1e-6
NORM_EPS = 1e-6
ALPHA = 2.0 ** 0.25
NEG = -30000.0
N_WIN_BLK = 45


def to_blocks(W):
    K, N = W.shape
    KC, NCB = K // 128, N // 128
    return np.ascontiguousarray(
        W.reshape(KC, 128, NCB, 128).transpose(2, 1, 0, 3).reshape(NCB, 128, KC * 128))


def na_tables():
    def r0(r):
        return min(max(r - 4, 0), 56)

    def c0(c):
        return min(max(c - 8, 0), 48)
    specs = [(2, j) for j in range(0, 5)] + [(0, j) for j in range(4)] + [(1, j) for j in range(4)] \
        + [(30, j) for j in range(28, 32)] + [(31, j) for j in range(28, 32)]
    ridx = np.zeros((21, 128, 128), np.int64)
    cidx = np.zeros((21, 128, 128), np.int64)
    mask = np.full((21, 128, 128), NEG, np.float32)
    for ti, (b, j) in enumerate(specs):
        for k in range(128):
            krow, kcol = 2 * j + k // 64, k % 64
            for q in range(128):
                qrow, qcol = 2 * b + q // 64, q % 64
                ok = (r0(qrow) <= krow < r0(qrow) + 8) and (c0(qcol) <= kcol < c0(qcol) + 16)
                if ok:
                    ridx[ti, k, q] = krow - qrow + 7
                    cidx[ti, k, q] = kcol - qcol + 15
                    mask[ti, k, q] = 0.0
    return ridx, cidx, mask


def na_chunks(b):
    if b == 0:
        return [(j, 5 + j) for j in range(4)]
    if b == 1:
        return [(j, 9 + j) for j in range(4)]
    if b == 30:
        return [(j, 13 + j - 28) for j in range(28, 32)]
    if b == 31:
        return [(j, 17 + j - 28) for j in range(28, 32)]
    return [(b - 2 + d, d) for d in range(5)]


_NA_CACHE = {}


def build_program(dbg=None, groups=tuple(range(9)), phases=(1, 2, 3, 4), na_hps=(0, 1, 2, 3), m_groups=tuple(range(8))):
    nc = bass.Bass("TRN2", target_bir_lowering=False)
    dbg = dbg or set()
    uid = [0]

    def din(name, shape, dt=F32):
        return nc.dram_tensor(name, list(shape), dt, kind="ExternalInput").ap()

    def dscr(name, shape, dt=F32):
        kind = "ExternalOutput" if name in dbg else "Internal"
        return nc.dram_tensor(name, list(shape), dt, kind=kind).ap()

    x_d = din("x", [SEQ, D])
    ctx_d = din("ctx", [CTX, D])
    cT_d = din("cT", [128, 8, 2])
    wada_d = din("w_ada", [D, 9 * D])
    badaT_d = din("b_adaT", [128, 72])
    lngb_d = din("lngb", [6, D])
    ident_d = din("ident", [128, 128])
    wu1_d = din("wu1", [44, 128, 1024], F32R)
    wd1_d = din("wd1", [8, 128, 2816], F32R)
    win_d = din("win", [N_WIN_BLK, 128, 1024], F32R)
    wpa_d = din("wpa", [8, 128, 512], F32R)
    wpb_d = din("wpb", [8, 128, 512], F32R)
    wo_d = din("wo", [8, 128, 1024], F32R)
    wu2_d = din("wu2", [44, 128, 1024], F32R)
    wd2_d = din("wd2", [8, 128, 2816], F32R)
    natab_d = din("na_tab", [8, 128, 21, 128])
    namask_d = din("na_mask", [128, 21, 128])
    convw_d = din("g_convw", [128, 12, 5])
    gmask_d = din("g_mask", [128, 8, 128])
    rope_d = din("g_rope", [128, 32, 128])
    adt_d = din("g_adt", [128, 16])
    normw_d = din("g_normw", [128, 128])
    out_d = nc.dram_tensor("out", [SEQ, D], F32, kind="ExternalOutput").ap()

    X1 = dscr("X1", [SEQ, D])
    QAT = dscr("QAT", [512, SEQ])
    KAT = dscr("KAT", [512, NTOK])
    VA = dscr("VA", [NTOK, 512])
    QKVT = dscr("QKVT", [1536, NTOK])
    AB = dscr("AB", [NTOK, 16])
    OA = dscr("OA", [SEQ, 512])
    OB = dscr("OB", [SEQ, 512])
    QKVN = dscr("QKVN", [NTOK, 1536])
    GB = dscr("GB", [NTOK, 16])
    DBG1 = dscr("DBG1", [128, 144])

    st = ExitStack()
    P = Prog(nc, st)
    cur = [st]

    def sb(name, shape, dt=F32):
        uid[0] += 1
        return cur[0].enter_context(nc.sbuf_tensor(f"sb{uid[0]}_{name}", list(shape), dt))

    ident = sb("ident", [128, 128])
    modT = sb("modT", [128, 72, 2])
    mod1p = sb("mod1p", [128, 72, 2])
    modh = sb("modh", [128, 72, 2])
    scT = sb("scT", [128, 8, 2])
    badaT = sb("badaT", [128, 72])
    stats = sb("stats", [128, 8, 16])
    epsc = sb("epsc", [128, 2])
    R_ident, R_mod = Res("ident"), Res("mod")
    P.op("dve", lambda e: e.memset(epsc[:, 0:1], LN_EPS), writes=[R_ident])
    P.op("dve", lambda e: e.memset(epsc[:, 1:2], NORM_EPS), writes=[R_ident])
    ds_misc = P.dsem("misc")

    psum = [st.enter_context(nc.psum_tensor(f"ps{i}", [128, 512], F32)) for i in range(8)]
    R_ps = [Res(f"ps{i}") for i in range(8)]

    ph0 = ExitStack()
    cur[0] = ph0
    wa_all = sb("wa_all", [128, 2, 6144])
    P.dma("sp", ds_misc, ident[:], ident_d[:, :], writes=[R_ident])
    P.dma("sp", ds_misc, scT[:], cT_d[:, :, :], writes=[R_mod])
    P.dma("sp", ds_misc, badaT[:], badaT_d[:, :], writes=[R_mod])
    P.op("act", lambda e: e.activation(out=scT[:], in_=scT[:], func=AF.Silu), reads=[R_mod], writes=[R_mod])
    wa_bufs = [wa_all[:, i, :] for i in range(2)]
    wa_ring = Ring(P, "wa", wa_bufs)
    for pn in range(12):
        ap, res, ds = wa_ring.next()
        apv = ap.rearrange("p (k n) -> p k n", k=8)
        P.dma("sp", ds, apv, wada_d[:, pn * 768:(pn + 1) * 768].rearrange("(k p) n -> p k n", p=128), writes=[res])
        for cc in range(6):
            ch = pn * 6 + cc
            for kc in range(8):
                P.op("pe", lambda e, apv=apv, cc=cc, ch=ch, kc=kc: e.matmul(
                    psum[7][:, ch * 2:ch * 2 + 2], lhsT=apv[:, kc, cc * 128:(cc + 1) * 128], rhs=scT[:, kc, :],
                    start=(kc == 0), stop=(kc == 7)), reads=[res, R_mod], writes=[R_ps[7]])
    ps7v = psum[7][:, 0:144].rearrange("p (c t) -> p c t", t=2)
    for t in range(2):
        P.op("dve", lambda e, t=t: e.tensor_tensor(out=modT[:, :, t], in0=ps7v[:, :, t], in1=badaT[:], op=ALU.add),
             reads=[R_ps[7], R_mod], writes=[R_mod])
    P.op("dve", lambda e: e.tensor_scalar_add(out=mod1p[:], in0=modT[:], scalar1=1.0), reads=[R_mod], writes=[R_mod])
    P.op("dve", lambda e: e.tensor_scalar_mul(out=modh[:], in0=modT[:], scalar1=0.5), reads=[R_mod], writes=[R_mod])
    if "DBG1" in dbg:
        P.dma("sp", ds_misc, DBG1[:, :], modT[:].rearrange("p c t -> p (c t)"), reads=[R_mod])
    P.flush()
    ph0.close()

    stat_i = [0]

    def ln_normalize(src, rsrc, dst, rdst):
        k = stat_i[0] % 8
        stat_i[0] += 1
        s = stats[:, k, :]
        rs = Res("st")
        P.op("dve", lambda e: e.bn_stats(out=s[:, 0:6], in_=src[:, 0:512]), reads=[rsrc], writes=[rs])
        P.op("dve", lambda e: e.bn_stats(out=s[:, 6:12], in_=src[:, 512:1024]), reads=[rsrc], writes=[rs])
        P.op("dve", lambda e: e.bn_aggr(out=s[:, 12:14], in_=s[:, 0:12]), reads=[rs], writes=[rs])
        P.op("act", lambda e: e.activation(out=s[:, 14:15], in_=s[:, 13:14], func=AF.Sqrt, bias=epsc[:, 0:1]),
             reads=[rs, R_ident], writes=[rs])
        P.op("dve", lambda e: e.reciprocal(out=s[:, 14:15], in_=s[:, 14:15]), reads=[rs], writes=[rs])
        P.op("dve", lambda e: e.scalar_tensor_tensor(out=s[:, 15:16], in0=s[:, 12:13], scalar=-1.0, in1=s[:, 14:15],
                                                     op0=ALU.mult, op1=ALU.mult), reads=[rs], writes=[rs])
        P.op("act", lambda e: e.activation(out=dst, in_=src, func=AF.Identity, scale=s[:, 14:15], bias=s[:, 15:16]),
             reads=[rsrc, rs], writes=[rdst])

    class Env:
        pass

    def make_env(rows, merge=False):
        E = Env()
        nr = len(rows)
        lngb = sb("lngb", [128, 2 * nr, D])
        R_lngb = Res("lngb")
        for i, r in enumerate(rows):
            P.dma("sp", ds_misc, lngb[:, i, :], lngb_d[r].partition_broadcast(128), writes=[R_lngb])
            P.dma("sp", ds_misc, lngb[:, nr + i, :], lngb_d[3 + r].partition_broadcast(128), writes=[R_lngb])
        lnslot = {r: i for i, r in enumerate(rows)}
        hx = sb("hx", [128, 4, D])
        zt = sb("zt", [128, 4, D])
        xn_all = sb("xn", [128, 2, D])
        xn = [xn_all[:, i, :] for i in range(2)]
        uTr = sb("uT", [128, 8, 512], F32R)
        Gr = sb("G", [128, NJ, 512], F32R)
        wb_all = sb("wb", [128, 2, 2816], F32R)
        wbs_all = sb("wbs", [128, 6, 1024], F32R)
        wowner = {}
        tmpA_all = sb("tmpA", [128, 3, 512])
        tmpB_all = sb("tmpB", [128, 3, 512])
        R_hx = [Res(f"hx{t}") for t in range(4)]
        R_zt = [Res(f"zt{t}") for t in range(4)]
        R_xn = [Res("xn0"), Res("xn1")]
        R_uT = [Res(f"uT{t}") for t in range(4)]
        R_G = [Res(f"G{j}") for j in range(NJ)]
        tag = "m" if merge else "f"
        ds_hx = [P.dsem(f"{tag}hx{t}") for t in range(4)]
        ds_zt = [P.dsem(f"{tag}zt{t}") for t in range(4)]
        wring = Ring(P, tag + "wb", [wb_all[:, i, :] for i in range(2)])
        wring_s = Ring(P, tag + "wbs", [wbs_all[:, i, :] for i in range(6)])
        tmpA_ring = Ring(P, tag + "tA", [tmpA_all[:, i, :] for i in range(3)])
        tmpB_ring = Ring(P, tag + "tB", [tmpB_all[:, i, :] for i in range(3)])
        xn_i = [0]
        ps_gemm_i = [0]
        ps_tr_i = [0]

        def gemm_ps():
            k = ps_gemm_i[0] % 4
            ps_gemm_i[0] += 1
            return psum[k], R_ps[k]

        def tr_ps():
            k = 4 + ps_tr_i[0] % 3
            ps_tr_i[0] += 1
            return psum[k], R_ps[k]

        class WStream:
            def __init__(self, plan, depth=5):
                self.plan = plan
                self.loaded = []
                self.i = 0
                self.npop = 0
                self.depth = depth
                self.pre()

            def pre(self):
                while len(self.loaded) < self.depth and self.i < len(self.plan):
                    src, ncol = self.plan[self.i]
                    ring = wring_s if ncol <= 1024 else wring
                    slot = ring.slots[ring.i % len(ring.slots)]
                    owner = wowner.get(id(slot[1]))
                    if owner is not None and owner[0] is self and owner[1] >= self.npop:
                        break
                    ap, res, ds = ring.next()
                    wowner[id(res)] = (self, self.i)
                    self.i += 1
                    P.dma("pool", ds, ap[:, 0:ncol], src, writes=[res])
                    self.loaded.append((ap, res))

            def get(self):
                self.pre()
                ap, res = self.loaded.pop(0)
                self.npop += 1
                return ap, res

        def mod_ln(src, rsrc):
            k = xn_i[0] % 2
            xn_i[0] += 1
            ln_normalize(src, rsrc, xn[k], R_xn[k])
            return k

        def mod_tr(k, t, base, sel):
            for half in range(2):
                ps, rps = tr_ps()
                for q in range(4):
                    kc = half * 4 + q
                    P.op("pe", lambda e, ps=ps, q=q, kc=kc, k=k: e.transpose(
                        out=ps[:, q * 128:(q + 1) * 128], in_=xn[k][:, kc * 128:(kc + 1) * 128], identity=ident[:]),
                        reads=[R_xn[k], R_ident], writes=[rps])
                for q in range(4):
                    kc = half * 4 + q
                    ci_shift = base * 8 + kc
                    ci_scale = (base + 1) * 8 + kc
                    dst = uTr[:, kc, t * 128:(t + 1) * 128]
                    if q % 2 == 0:
                        P.op("act", lambda e, ps=ps, q=q, dst=dst, a=ci_scale, b=ci_shift: e.activation(
                            out=dst, in_=ps[:, q * 128:(q + 1) * 128], func=AF.Identity,
                            scale=mod1p[:, a, sel:sel + 1], bias=modT[:, b, sel:sel + 1]),
                            reads=[rps, R_mod], writes=[R_uT[t]])
                    else:
                        P.op("dve", lambda e, ps=ps, q=q, dst=dst, a=ci_scale, b=ci_shift: e.tensor_scalar(
                            out=dst, in0=ps[:, q * 128:(q + 1) * 128], scalar1=mod1p[:, a, sel:sel + 1],
                            scalar2=modT[:, b, sel:sel + 1], op0=ALU.mult, op1=ALU.add),
                            reads=[rps, R_mod], writes=[R_uT[t]])

        def modulate_all(buf, rbuf, ntile, base, sel):
            ks = {0: mod_ln(buf[:, 0, :], rbuf[0])}
            for t in range(ntile):
                if t + 1 < ntile:
                    ks[t + 1] = mod_ln(buf[:, t + 1, :], rbuf[t + 1])
                mod_tr(ks[t], t, base, sel)

        def gemm_block(ws, KC, rhs_fn, rhs_res, ntok):
            wap, wres = ws.get()
            ps, rps = gemm_ps()
            for kc in range(KC):
                P.op("pe", lambda e, ps=ps, wap=wap, kc=kc: e.matmul(
                    ps[:, 0:ntok], lhsT=wap[:, kc * 128:(kc + 1) * 128], rhs=rhs_fn(kc),
                    start=(kc == 0), stop=(kc == KC - 1)), reads=[wres] + list(rhs_res), writes=[rps])
            return ps, rps

        def post_ln(buf, rbuf, lnrow):
            i = lnslot[lnrow]
            ln_normalize(buf, rbuf, buf, rbuf)
            P.op("dve", lambda e: e.tensor_tensor(out=buf, in0=buf, in1=lngb[:, i, :], op=ALU.mult),
                 reads=[rbuf, R_lngb], writes=[rbuf])
            P.op("dve", lambda e: e.tensor_tensor(out=buf, in0=buf, in1=lngb[:, nr + i, :], op=ALU.add),
                 reads=[rbuf, R_lngb], writes=[rbuf])

        deferred = []

        def run_deferred():
            while deferred:
                deferred.pop(0)()

        def back_to_tokens(ps, rps, scale_ap, ntile, src, rsrc, dst, rdst, c):
            ntok = ntile * 128
            run_deferred()
            tb, rtb, _ = tmpB_ring.next()
            P.op("act", lambda e: e.activation(out=tb[:, 0:ntok], in_=ps[:, 0:ntok], func=AF.Identity, scale=scale_ap),
                 reads=[rps, R_mod], writes=[rtb])

            def finish():
                pt, rpt = tr_ps()
                for t in range(ntile):
                    P.op("pe", lambda e, t=t: e.transpose(
                        out=pt[:, t * 128:(t + 1) * 128], in_=tb[:, t * 128:(t + 1) * 128], identity=ident[:]),
                        reads=[rtb, R_ident], writes=[rpt])
                P.op("dve", lambda e: e.scalar_tensor_tensor(
                    out=dst[:, 0:ntile, c * 128:(c + 1) * 128], in0=src[:, 0:ntile, c * 128:(c + 1) * 128], scalar=ALPHA,
                    in1=pt[:, 0:ntok].rearrange("p (t d) -> p t d", t=ntile), op0=ALU.mult, op1=ALU.add),
                    reads=[rpt] + rsrc[:ntile], writes=rdst[:ntile])
            deferred.append(finish)

        def ffn_sublayer(ntile, sub, sel, wu_d, wd_d, lnrow, src, rsrc, dst, rdst):
            ntok = ntile * 128
            base = 3 * sub
            modulate_all(src, rsrc, ntile, base, sel)
            plan = []
            for j in range(NJ):
                plan.append((wu_d[j], 1024))
                plan.append((wu_d[NJ + j], 1024))
            for c in range(8):
                plan.append((wd_d[c], 2816))
            ws = WStream(plan)
            ures = R_uT[:ntile]
            for j in range(NJ):
                psa, rpa = gemm_block(ws, 8, lambda kc: uTr[:, kc, 0:ntok], ures, ntok)
                psb, rpb = gemm_block(ws, 8, lambda kc: uTr[:, kc, 0:ntok], ures, ntok)
                ta, rta, _ = tmpA_ring.next()
                P.op("act", lambda e, ta=ta, psa=psa: e.activation(out=ta[:, 0:ntok], in_=psa[:, 0:ntok], func=AF.Silu),
                     reads=[rpa], writes=[rta])
                P.op("dve", lambda e, ta=ta, psb=psb, j=j: e.tensor_tensor(
                    out=Gr[:, j, 0:ntok], in0=ta[:, 0:ntok], in1=psb[:, 0:ntok], op=ALU.mult),
                    reads=[rta, rpb], writes=[R_G[j]])
            for c in range(8):
                ps, rps = gemm_block(ws, NJ, lambda kc: Gr[:, kc, 0:ntok], R_G, ntok)
                gi = (base + 2) * 8 + c
                back_to_tokens(ps, rps, modh[:, gi, sel:sel + 1], ntile, src, rsrc, dst, rdst, c)
            run_deferred()
            for t in range(ntile):
                post_ln(dst[:, t, :], rdst[t], lnrow)

        E.__dict__.update(locals())
        return E

    def phase1_group(E, g):
        is_ctx = (g == 8)
        ntile = 2 if is_ctx else 4
        ntok = ntile * 128
        sel = 1 if is_ctx else 0
        tok0 = SEQ if is_ctx else g * 512
        hx, zt, uTr = E.hx, E.zt, E.uTr

        def load_group(gg):
            for t in range(2 if gg == 8 else 4):
                src = ctx_d[t * 128:(t + 1) * 128, :] if gg == 8 else x_d[gg * 512 + t * 128:gg * 512 + (t + 1) * 128, :]
                P.dma("sp", E.ds_hx[t], hx[:, t, :], src, writes=[E.R_hx[t]])
        if g == groups[0]:
            load_group(g)
        E.ffn_sublayer(ntile, 0, sel, wu1_d, wd1_d, 0, hx, E.R_hx, zt, E.R_zt)
        gi_ = list(groups).index(g)
        if gi_ + 1 < len(groups):
            load_group(groups[gi_ + 1])
        if not is_ctx:
            for t in range(ntile):
                P.dma("sp", E.ds_zt[t], X1[tok0 + t * 128:tok0 + (t + 1) * 128, :], zt[:, t, :], reads=[E.R_zt[t]])
        E.modulate_all(zt, E.R_zt, ntile, 3, sel)
        blocks = list(range(0 if not is_ctx else 4, 24)) + [28]
        ws = E.WStream([(win_d[cb], 1024) for cb in blocks])
        ures = E.R_uT[:ntile]
        for cb in blocks:
            ps, rps = E.gemm_block(ws, 8, lambda kc: uTr[:, kc, 0:ntok], ures, ntok)
            E.run_deferred()
            tb, rtb, dsb = E.tmpB_ring.next()
            if cb < 4:
                P.op("act", lambda e, tb=tb, ps=ps: e.activation(out=tb[:, 0:ntok], in_=ps[:, 0:ntok], func=AF.Copy,
                                                                 scale=0.125), reads=[rps], writes=[rtb])
                P.dma("sp", dsb, QAT[cb * 128:(cb + 1) * 128, tok0:tok0 + ntok], tb[:, 0:ntok], reads=[rtb])
            elif cb < 8 or (12 <= cb < 24):
                P.op("dve", lambda e, tb=tb, ps=ps: e.tensor_copy(out=tb[:, 0:ntok], in_=ps[:, 0:ntok]),
                     reads=[rps], writes=[rtb])
                if cb < 8:
                    dst = KAT[(cb - 4) * 128:(cb - 3) * 128, tok0:tok0 + ntok]
                else:
                    dst = QKVT[(cb - 12) * 128:(cb - 11) * 128, tok0:tok0 + ntok]
                P.dma("sp", dsb, dst, tb[:, 0:ntok], reads=[rtb])
            else:
                P.op("act", lambda e, tb=tb, ps=ps: e.activation(out=tb[:, 0:ntok], in_=ps[:, 0:ntok], func=AF.Copy),
                     reads=[rps], writes=[rtb])

                def finish(tb=tb, rtb=rtb, cb=cb):
                    pt, rpt = E.tr_ps()
                    for t in range(ntile):
                        P.op("pe", lambda e, pt=pt, tb=tb, t=t: e.transpose(
                            out=pt[:, t * 128:(t + 1) * 128], in_=tb[:, t * 128:(t + 1) * 128], identity=ident[:]),
                            reads=[rtb, R_ident], writes=[rpt])
                    ta, rta, dsa = E.tmpA_ring.next()
                    P.op("dve", lambda e, ta=ta, pt=pt: e.tensor_copy(out=ta[:, 0:ntok], in_=pt[:, 0:ntok]),
                         reads=[rpt], writes=[rta])
                    tav = ta[:, 0:ntok].rearrange("p (t c) -> p t c", t=ntile)
                    if cb == 28:
                        dst = AB[tok0:tok0 + ntok, :].rearrange("(t p) c -> p t c", p=128)
                        P.dma("sp", dsa, dst, tav[:, :, 0:16], reads=[rta])
                    else:
                        dst = VA[tok0:tok0 + ntok, (cb - 8) * 128:(cb - 7) * 128].rearrange("(t p) c -> p t c", p=128)
                        P.dma("sp", dsa, dst, tav, reads=[rta])
                E.deferred.append(finish)
        E.run_deferred()

    if 1 in phases:
        ph1 = ExitStack()
        cur[0] = ph1
        E1 = make_env([0])
        for g in groups:
            phase1_group(E1, g)
        P.flush()
        ph1.close()

    if 2 in phases:
        ph2 = ExitStack()
        cur[0] = ph2
        KT = sb("naK", [128, NTOK])
        QT = sb("naQ", [128, SEQ])
        vv = sb("naV", [128, 34, 2, 65])
        tabs = sb("naT", [128, 2, 21, 128])
        msk = sb("naM", [128, 21, 128])
        pT_all = sb("naP", [128, 6, 128])
        sT_all = sb("naS", [128, 4, 128])
        ob_all = sb("naO", [128, 2, 128])
        rec_all = sb("naR", [128, 4])
        R_KT, R_QT, R_vv, R_msk = Res("KT"), Res("QT"), Res("vv"), Res("msk")
        R_tab = [Res("tab0"), Res("tab1")]
        ds_na = P.dsem("na")
        pT_ring = Ring(P, "naP", [pT_all[:, i, :] for i in range(6)], with_dsem=False)
        sT_ring = Ring(P, "naS", [sT_all[:, i, :] for i in range(4)], with_dsem=False)
        ob_ring = Ring(P, "naO", [ob_all[:, i, :] for i in range(2)])
        rec_ring = Ring(P, "naR", [rec_all[:, i:i + 1] for i in range(4)], with_dsem=False)
        P.op("dve", lambda e: e.memset(vv[:, :, :, 64:65], 1.0), writes=[R_vv])
        P.dma("sp", ds_na, msk[:], namask_d[:, :, :], writes=[R_msk])
        cnt_s = [0]
        cnt_o = [0]
        for hp in na_hps:
            P.dma("sp", ds_na, KT[:], KAT[hp * 128:(hp + 1) * 128, :], writes=[R_KT])
            P.dma("sp", ds_na, QT[:], QAT[hp * 128:(hp + 1) * 128, :], writes=[R_QT])
            for h2 in range(2):
                c0 = hp * 128 + h2 * 64
                P.dma("sp", ds_na, vv[:, :, h2, 0:64], VA[:, c0:c0 + 64].rearrange("(t p) c -> p t c", p=128),
                      writes=[R_vv])
                P.dma("sp", ds_na, tabs[:, h2], natab_d[hp * 2 + h2], writes=[R_tab[h2]])
                P.op("dve", lambda e, h2=h2: e.tensor_tensor(out=tabs[:, h2], in0=tabs[:, h2], in1=msk[:], op=ALU.add),
                     reads=[R_tab[h2], R_msk], writes=[R_tab[h2]])
            for b in range(32):
                ob, rob, dsob = ob_ring.next()
                chunks = na_chunks(b) + [(32, None), (33, None)]
                nch = len(chunks)
                pos = []
                for h2 in range(2):
                    kk = 4 + (cnt_o[0] % 4)
                    cnt_o[0] += 1
                    pos.append((psum[kk], R_ps[kk]))

                def pv(pT, rpT, j, ci, h2, nch=nch, pos=pos):
                    po, rpo = pos[h2]
                    P.op("pe", lambda e: e.matmul(po[:, 0:65], lhsT=pT, rhs=vv[:, j, h2, :],
                                                  start=(ci == 0), stop=(ci == nch - 1)),
                         reads=[rpT, R_vv], writes=[rpo])
                pending = []
                for ci, (j, ti) in enumerate(chunks):
                    cur_s = []
                    for h2 in range(2):
                        lo, hi = h2 * 64, (h2 + 1) * 64
                        k = cnt_s[0] % 4
                        cnt_s[0] += 1
                        ps_s, rps_s = psum[k], R_ps[k]
                        P.op("pe", lambda e, ps_s=ps_s, j=j, b=b, lo=lo, hi=hi: e.matmul(
                            ps_s[:, 0:128], lhsT=KT[lo:hi, j * 128:(j + 1) * 128], rhs=QT[lo:hi, b * 128:(b + 1) * 128],
                            start=True, stop=True), reads=[R_KT, R_QT], writes=[rps_s])
                        cur_s.append((ps_s, rps_s))
                    for p in pending:
                        pv(*p)
                    pending = []
                    for h2 in range(2):
                        ps_s, rps_s = cur_s[h2]
                        pT, rpT, _ = pT_ring.next()
                        if ti is not None:
                            sT, rsT, _ = sT_ring.next()
                            P.op("dve", lambda e, sT=sT, ps_s=ps_s, ti=ti, h2=h2: e.tensor_tensor(
                                out=sT, in0=ps_s[:, 0:128], in1=tabs[:, h2, ti, :], op=ALU.add),
                                reads=[rps_s, R_tab[h2]], writes=[rsT])
                            P.op("act", lambda e, pT=pT, sT=sT: e.activation(out=pT, in_=sT, func=AF.Exp),
                                 reads=[rsT], writes=[rpT])
                        else:
                            P.op("act", lambda e, pT=pT, ps_s=ps_s: e.activation(out=pT, in_=ps_s[:, 0:128], func=AF.Exp),
                                 reads=[rps_s], writes=[rpT])
                        pending.append((pT, rpT, j, ci, h2))
                for p in pending:
                    pv(*p)
                for h2 in range(2):
                    lo, hi = h2 * 64, (h2 + 1) * 64
                    po, rpo = pos[h2]
                    rc, rrc, _ = rec_ring.next()
                    P.op("dve", lambda e, rc=rc, po=po: e.reciprocal(out=rc, in_=po[:, 64:65]), reads=[rpo], writes=[rrc])
                    P.op("act", lambda e, rc=rc, po=po, ob=ob, lo=lo, hi=hi: e.activation(
                        out=ob[:, lo:hi], in_=po[:, 0:64], func=AF.Identity, scale=rc), reads=[rpo, rrc], writes=[rob])
                P.dma("sp", dsob, OA[b * 128:(b + 1) * 128, hp * 128:(hp + 1) * 128], ob, reads=[rob])
        P.flush()
        ph2.close()

    if 3 in phases:
        ph3 = ExitStack()
        cur[0] = ph3
        convw = sb("g_convw", [128, 12, 5])
        msk4 = sb("g_msk", [128, 8, 128])
        adt = sb("g_adt", [128, 16])
        nea = sb("g_nea", [128, 8])
        onec = sb("g_onec", [128, 1])
        ones3 = sb("g_ones3", [128, 4, 128])
        R_gc = Res("gconst")
        ds_g = P.dsem("gconst")
        P.dma("sp", ds_g, convw[:], convw_d[:, :, :], writes=[R_gc])
        P.dma("sp", ds_g, msk4[:], gmask_d[:, :, :], writes=[R_gc])
        P.dma("sp", ds_g, adt[:], adt_d[:, :], writes=[R_gc])
        P.op("dve", lambda e: e.memset(onec[:], 1.0), writes=[R_gc])
        P.op("dve", lambda e: e.memset(ones3[:], 1.0), writes=[R_gc])
        P.op("act", lambda e: e.activation(out=nea[:], in_=adt[:, 0:8], func=AF.Exp), reads=[R_gc], writes=[R_gc])
        P.op("dve", lambda e: e.tensor_scalar_mul(out=nea[:], in0=nea[:], scalar1=-1.0), reads=[R_gc], writes=[R_gc])
        gbank_i = [0]

        def gbank():
            k = gbank_i[0] % 8
            gbank_i[0] += 1
            return psum[k], R_ps[k]

        def v3(ap, h=4):
            return ap.rearrange("p (h n) -> p h n", h=h)

        def bc_last(ap2, h=4, n=128):
            return ap2.unsqueeze(2).to_broadcast([128, h, n])

        def bc_mid(ap2, h=4, n=128):
            return ap2.unsqueeze(1).to_broadcast([128, h, n])

        def tile_rows(tau):
            return tau * 128 if tau < 32 else SEQ + (tau - 32) * 128

        R_QKVN = [Res(f"qkvn{t}") for t in range(34)]
        R_GB = [Res(f"gb{t}") for t in range(34)]
        R_OB = [Res(f"ob{t}") for t in range(32)]

        def run_interleaved(gens):
            alive = list(gens)
            while alive:
                for g in list(alive):
                    try:
                        next(g)
                    except StopIteration:
                        alive.remove(g)

        ph3a = ExitStack()
        cur[0] = ph3a
        ropet = sb("g_rope", [128, 32, 128])
        R_rope = Res("rope")
        P.dma("sp", ds_g, ropet[:], rope_d[:, :, :], writes=[R_rope])

        def prep_chain(par):
            cw = sb(f"g_cw{par}", [128, 12, 132])
            acc = sb(f"g_acc{par}", [128, 12, 128])
            tmpc = sb(f"g_tmpc{par}", [128, 12, 128])
            tm = sb(f"g_tm{par}", [128, 1536])
            sq = sb(f"g_sq{par}", [128, 1024])
            rp = sb(f"g_rp{par}", [128, 1024])
            qkn = sb(f"g_qkn{par}", [128, 1024])
            tA = sb(f"g_tA{par}", [128, 512])
            tB = sb(f"g_tB{par}", [128, 512])
            smp = sb(f"g_smp{par}", [128, 16])
            abt = sb(f"g_abt{par}", [128, 16])
            gbt = sb(f"g_gbt{par}", [128, 16])
            R_cw, R_acc, R_tmpc, R_tm, R_sq, R_rp, R_qkn = (Res(n) for n in ("cw", "acc", "tmpc", "tm", "sq", "rp", "qkn"))
            R_tA, R_tB, R_smp, R_abt, R_gbt = (Res(n) for n in ("tA", "tB", "smp", "abt", "gbt"))
            ds_cw, ds_tm, ds_qkn, ds_abt, ds_gbt = (P.dsem(f"{n}{par}") for n in ("g_cw", "g_tm", "g_qkn", "g_abt", "g_gbt"))
            bi = [0]

            def bank():
                k = 4 * par + bi[0] % 4
                bi[0] += 1
                return psum[k], R_ps[k]
            yield
            for tau in range(par, 34, 2):
                t0 = tile_rows(tau)
                seg_lo, seg_hi = (0, SEQ) if tau < 32 else (SEQ, NTOK)
                lo, hi = max(t0 - 2, seg_lo), min(t0 + 130, seg_hi)
                off = lo - (t0 - 2)
                if off > 0:
                    P.op("pool", lambda e: e.memset(cw[:, :, 0:2], 0.0), writes=[R_cw])
                if hi < t0 + 130:
                    P.op("pool", lambda e: e.memset(cw[:, :, 130:132], 0.0), writes=[R_cw])
                P.dma("sp", ds_cw, cw[:, :, off:off + hi - lo], QKVT[:, lo:hi].rearrange("(c p) n -> p c n", p=128),
                      writes=[R_cw])
                P.dma("sp", ds_abt, abt[:], AB[t0:t0 + 128, :], writes=[R_abt])
                yield
                P.op("dve", lambda e: e.tensor_tensor(out=acc[:], in0=cw[:, :, 0:128],
                                                      in1=convw[:, :, 0:1].to_broadcast([128, 12, 128]), op=ALU.mult),
                     reads=[R_cw, R_gc], writes=[R_acc])
                yield
                for k in range(1, 5):
                    P.op("pool", lambda e, k=k: e.tensor_tensor(out=tmpc[:], in0=cw[:, :, k:k + 128],
                                                                in1=convw[:, :, k:k + 1].to_broadcast([128, 12, 128]),
                                                                op=ALU.mult), reads=[R_cw, R_gc], writes=[R_tmpc])
                    yield
                    P.op("dve", lambda e: e.tensor_tensor(out=acc[:], in0=acc[:], in1=tmpc[:], op=ALU.add),
                         reads=[R_acc, R_tmpc], writes=[R_acc])
                    yield
                P.op("act", lambda e: e.activation(out=acc[:], in_=acc[:], func=AF.Silu), reads=[R_acc], writes=[R_acc])
                yield
                P.op("dve", lambda e: e.tensor_tensor(out=gbt[:, 0:8], in0=abt[:, 0:8], in1=adt[:, 8:16], op=ALU.add),
                     reads=[R_abt, R_gc], writes=[R_gbt])
                P.op("act", lambda e: e.activation(out=gbt[:, 0:8], in_=gbt[:, 0:8], func=AF.Exp), reads=[R_gbt], writes=[R_gbt])
                yield
                P.op("act", lambda e: e.activation(out=gbt[:, 0:8], in_=gbt[:, 0:8], func=AF.Ln, bias=onec[:, 0:1]),
                     reads=[R_gbt, R_gc], writes=[R_gbt])
                yield
                P.op("dve", lambda e: e.tensor_tensor(out=gbt[:, 0:8], in0=gbt[:, 0:8], in1=nea[:], op=ALU.mult),
                     reads=[R_gbt, R_gc], writes=[R_gbt])
                P.op("act", lambda e: e.activation(out=gbt[:, 8:16], in_=abt[:, 8:16], func=AF.Sigmoid),
                     reads=[R_abt], writes=[R_gbt])
                P.dma("sp", ds_gbt, GB[t0:t0 + 128, :], gbt[:], reads=[R_gbt], writes=[R_GB[tau]])
                yield
                for c4 in range(3):
                    ps, rps = bank()
                    for q in range(4):
                        P.op("pe", lambda e, ps=ps, q=q, c4=c4: e.transpose(
                            out=ps[:, q * 128:(q + 1) * 128], in_=acc[:, c4 * 4 + q, :], identity=ident[:]),
                            reads=[R_acc, R_ident], writes=[rps])
                    if c4 % 2 == 0:
                        P.op("act", lambda e, ps=ps, c4=c4: e.activation(out=tm[:, c4 * 512:(c4 + 1) * 512], in_=ps[:, :],
                                                                         func=AF.Copy), reads=[rps], writes=[R_tm])
                    else:
                        P.op("dve", lambda e, ps=ps, c4=c4: e.tensor_copy(out=tm[:, c4 * 512:(c4 + 1) * 512], in_=ps[:, :]),
                             reads=[rps], writes=[R_tm])
                    yield
                P.dma("sp", ds_tm, QKVN[t0:t0 + 128, 1024:1536], tm[:, 1024:1536], reads=[R_tm], writes=[R_QKVN[tau]])
                P.op("dve", lambda e: e.tensor_tensor(out=sq[:], in0=tm[:, 0:1024], in1=tm[:, 0:1024], op=ALU.mult),
                     reads=[R_tm], writes=[R_sq])
                yield
                P.op("dve", lambda e: e.reduce_sum(out=smp[:, 0:8], in_=v3(sq[:], 8), axis=mybir.AxisListType.X),
                     reads=[R_sq], writes=[R_smp])
                yield
                P.op("act", lambda e: e.activation(out=smp[:, 8:16], in_=smp[:, 0:8], func=AF.Sqrt, bias=epsc[:, 1:2]),
                     reads=[R_smp, R_ident], writes=[R_smp])
                yield
                P.op("dve", lambda e: e.reciprocal(out=smp[:, 8:16], in_=smp[:, 8:16]), reads=[R_smp], writes=[R_smp])
                yield
                P.op("dve", lambda e: e.tensor_scalar_mul(out=smp[:, 8:12], in0=smp[:, 8:12], scalar1=128.0 ** -0.5),
                     reads=[R_smp], writes=[R_smp])
                yield
                if tau < 32:
                    x5 = tm[:, 0:1024].rearrange("p (i a h f) -> p i a h f", i=8, a=2, h=2)
                    r5 = rp[:].rearrange("p (i a h f) -> p i a h f", i=8, a=2, h=2)
                    cs = ropet[:, tau, :].rearrange("p (s a f) -> p s a f", s=2, a=2)
                    tA4 = tA[:].rearrange("p (i a f) -> p i a f", i=8, a=2)
                    tB4 = tB[:].rearrange("p (i a f) -> p i a f", i=8, a=2)
                    for (xa, xb, half, op) in ((0, 1, 0, ALU.subtract), (1, 0, 1, ALU.add)):
                        for ax in range(2):
                            cosb = cs[:, 0, ax, :].unsqueeze(1).to_broadcast([128, 8, 32])
                            sinb = cs[:, 1, ax, :].unsqueeze(1).to_broadcast([128, 8, 32])
                            P.op("pool", lambda e, xa=xa, ax=ax, cosb=cosb: e.tensor_tensor(
                                out=tA4[:, :, ax, :], in0=x5[:, :, ax, xa, :], in1=cosb, op=ALU.mult),
                                reads=[R_tm, R_rope], writes=[R_tA])
                            P.op("dve", lambda e, xb=xb, ax=ax, sinb=sinb: e.tensor_tensor(
                                out=tB4[:, :, ax, :], in0=x5[:, :, ax, xb, :], in1=sinb, op=ALU.mult),
                                reads=[R_tm, R_rope], writes=[R_tB])
                            yield
                            P.op("dve", lambda e, half=half, op=op, ax=ax: e.tensor_tensor(
                                out=r5[:, :, ax, half, :], in0=tA4[:, :, ax, :], in1=tB4[:, :, ax, :], op=op),
                                reads=[R_tA, R_tB], writes=[R_rp])
                            yield
                    srcqk, rsrc = rp, R_rp
                else:
                    srcqk, rsrc = tm, R_tm
                P.op("dve", lambda e, srcqk=srcqk: e.tensor_tensor(out=v3(qkn[:], 8), in0=v3(srcqk[:, 0:1024], 8),
                                                                   in1=bc_last(smp[:, 8:16], 8), op=ALU.mult),
                     reads=[rsrc, R_smp], writes=[R_qkn])
                P.dma("sp", ds_qkn, QKVN[t0:t0 + 128, 0:1024], qkn[:], reads=[R_qkn], writes=[R_QKVN[tau]])
                yield

        run_interleaved([prep_chain(0)])
        run_interleaved([prep_chain(1)])
        P.flush()
        ph3a.close()

        ph3b = ExitStack()
        cur[0] = ph3b
        names = ["kb", "vb", "kbg", "ktl", "kT", "qT", "kbT", "gbc", "Dd", "tmn", "tmx", "Ee", "Ff", "Em", "Ea", "Fm",
                 "X0", "X1", "XT0", "XT1", "R0", "R1", "Q0", "Q1", "Xf", "XTf", "AT"]
        alias = {"Cm": "gbc", "CmT": "Dd", "W1": "tmn", "W2": "tmx", "uu": "Ee", "wT": "Ff", "vnew": "Em", "o1": "Fm",
                 "obf": "Ea", "prev": "kbT"}

        def scan_chain(d):
            B = {n: sb(f"gs{d}_" + n, [128, 4, 128]) for n in names}
            RB = {n: Res(n) for n in names}
            for k_, v_ in alias.items():
                B[k_] = B[v_]
                RB[k_] = RB[v_]
            qkv_all = sb(f"gs{d}_qkv", [128, 2, 1536])
            gb_all = sb(f"gs{d}_gb", [128, 2, 16])
            qkv_ring = Ring(P, f"gs{d}_qkv", [qkv_all[:, i, :] for i in range(2)])
            gb_ring = Ring(P, f"gs{d}_gb", [gb_all[:, i, :] for i in range(2)])
            sm2 = sb(f"gs{d}_sm2", [128, 24])
            R_sm2 = Res("sm2")
            Sst = sb(f"gs{d}_S", [128, 4, 128])
            R_S = Res("S")
            ds_ob = P.dsem(f"gs{d}_ob")
            ds_prev = P.dsem(f"gs{d}_prev")
            bi = [0]

            def bank():
                k = 4 * d + bi[0] % 4
                bi[0] += 1
                return psum[k], R_ps[k]

            def mm4(lhs_fn, rhs_fn, reads):
                ps, rps = bank()
                for h in range(4):
                    P.op("pe", lambda e, h=h, l=lhs_fn(h), r=rhs_fn(h): e.matmul(
                        ps[:, h * 128:(h + 1) * 128], lhsT=l, rhs=r, start=True, stop=True), reads=reads, writes=[rps])
                return v3(ps[:, :]), rps

            def copy4(ps3, rps, dstn, eng):
                if eng == "act":
                    P.op("act", lambda e: e.activation(out=B[dstn][:], in_=ps3, func=AF.Copy), reads=[rps], writes=[RB[dstn]])
                else:
                    P.op("dve", lambda e: e.tensor_copy(out=B[dstn][:], in_=ps3), reads=[rps], writes=[RB[dstn]])

            def tr4(src, rsrc, dstn, eng):
                ps, rps = bank()
                for h in range(4):
                    P.op("pe", lambda e, h=h, a=src(h): e.transpose(out=ps[:, h * 128:(h + 1) * 128], in_=a, identity=ident[:]),
                         reads=[rsrc, R_ident], writes=[rps])
                copy4(v3(ps[:, :]), rps, dstn, eng)

            def tt(eng, outn, in0, in1, op, reads):
                P.op(eng, lambda e: e.tensor_tensor(out=B[outn][:], in0=in0, in1=in1, op=op), reads=reads, writes=[RB[outn]])

            order = [32, 33] + list(range(32)) if d == 0 else [33, 32] + list(range(31, -1, -1))
            U = msk4[:, 1, :] if d == 0 else msk4[:, 3, :]
            m_s = msk4[:, 0, :] if d == 0 else msk4[:, 2, :]
            m_i = msk4[:, 1, :] if d == 0 else msk4[:, 3, :]
            m_sT = msk4[:, 2, :] if d == 0 else msk4[:, 0, :]
            P.op("dve", lambda e: e.memset(Sst[:], 0.0), writes=[R_S])
            yield
            for tau in order:
                t0 = tile_rows(tau)
                wout = tau < 32
                qkv, rqkv, dsq = qkv_ring.next()
                gbv, rgbv, dsg = gb_ring.next()
                P.dma("sp", dsq, qkv, QKVN[t0:t0 + 128, :], reads=[R_QKVN[tau]], writes=[rqkv])
                P.dma("sp", dsg, gbv, GB[t0:t0 + 128, :], reads=[R_GB[tau]], writes=[rgbv])
                gs = gbv[:, d * 4:(d + 1) * 4]
                bs = gbv[:, 8 + d * 4:8 + (d + 1) * 4]
                qn = v3(qkv[:, 0:512])
                kn = v3(qkv[:, 512:1024])
                vn = v3(qkv[:, 1024:1536])
                ps, rps = bank()
                P.op("pe", lambda e, ps=ps, U=U, gs=gs: e.matmul(ps[:, 0:4], lhsT=U, rhs=gs, start=True, stop=True),
                     reads=[R_gc, rgbv], writes=[rps])
                P.op("pe", lambda e, ps=ps, gs=gs: e.matmul(ps[:, 4:8], lhsT=ones3[:, 0, :], rhs=gs, start=True, stop=True),
                     reads=[R_gc, rgbv], writes=[rps])
                yield
                P.op("dve", lambda e, ps=ps: e.tensor_copy(out=sm2[:, 0:8], in_=ps[:, 0:8]), reads=[rps], writes=[R_sm2])
                yield
                P.op("dve", lambda e: e.tensor_tensor(out=sm2[:, 16:20], in0=sm2[:, 4:8], in1=sm2[:, 0:4], op=ALU.subtract),
                     reads=[R_sm2], writes=[R_sm2])
                yield
                P.op("act", lambda e: e.activation(out=sm2[:, 8:12], in_=sm2[:, 0:4], func=AF.Exp),
                     reads=[R_sm2], writes=[R_sm2])
                P.op("act", lambda e: e.activation(out=sm2[:, 12:16], in_=sm2[:, 4:8], func=AF.Exp), reads=[R_sm2], writes=[R_sm2])
                P.op("act", lambda e: e.activation(out=sm2[:, 16:20], in_=sm2[:, 16:20], func=AF.Exp), reads=[R_sm2], writes=[R_sm2])
                gc, egc, etot, etl = sm2[:, 0:4], sm2[:, 8:12], sm2[:, 12:16], sm2[:, 16:20]
                tt("dve", "kb", kn, bc_last(bs), ALU.mult, [rqkv, rgbv])
                tt("pool", "vb", vn, bc_last(bs), ALU.mult, [rqkv, rgbv])
                tr4(lambda h: kn[:, h, :], rqkv, "kT", "act")
                yield
                tt("pool", "gbc", ones3[:], bc_last(gs), ALU.mult, [R_gc, rgbv])
                tr4(lambda h: B["kb"][:, h, :], RB["kb"], "kbT", "dve")
                yield
                tt("dve", "kbg", B["kb"][:], bc_last(egc), ALU.mult, [RB["kb"], R_sm2])
                tt("pool", "ktl", kn, bc_last(etl), ALU.mult, [rqkv, R_sm2])
                if wout:
                    tr4(lambda h: qn[:, h, :], rqkv, "qT", "act")
                yield
                psD, rpsD = mm4(lambda h: B["gbc"][:, h, :], lambda h: U, [RB["gbc"], R_gc])
                yield
                tt("dve", "Dd", psD, bc_last(gc), ALU.subtract, [rpsD, R_sm2])
                yield
                P.op("dve", lambda e: e.tensor_scalar_min(out=B["tmn"][:], in0=B["Dd"][:], scalar1=0.0),
                     reads=[RB["Dd"]], writes=[RB["tmn"]])
                P.op("pool", lambda e: e.tensor_scalar_max(out=B["tmx"][:], in0=B["Dd"][:], scalar1=0.0),
                     reads=[RB["Dd"]], writes=[RB["tmx"]])
                yield
                P.op("act", lambda e: e.activation(out=B["Ee"][:], in_=B["tmn"][:], func=AF.Exp),
                     reads=[RB["tmn"]], writes=[RB["Ee"]])
                P.op("act", lambda e: e.activation(out=B["Ff"][:], in_=B["tmx"][:], func=AF.Exp, scale=-1.0),
                     reads=[RB["tmx"]], writes=[RB["Ff"]])
                yield
                tt("pool", "Em", B["Ee"][:], bc_mid(m_s), ALU.mult, [RB["Ee"], R_gc])
                tt("dve", "Fm", B["Ff"][:], bc_mid(m_sT), ALU.mult, [RB["Ff"], R_gc])
                psK, rpsK = mm4(lambda h: B["kT"][:, h, :], lambda h: B["kbT"][:, h, :], [RB["kT"], RB["kbT"]])
                psK2, rpsK2 = mm4(lambda h: B["kbT"][:, h, :], lambda h: B["kT"][:, h, :], [RB["kT"], RB["kbT"]])
                yield
                P.op("dve", lambda e, psK=psK: e.scalar_tensor_tensor(out=B["Xf"][:], in0=psK, scalar=-1.0, in1=B["Em"][:],
                                                                      op0=ALU.mult, op1=ALU.mult),
                     reads=[rpsK, RB["Em"]], writes=[RB["Xf"]])
                P.op("dve", lambda e, psK2=psK2: e.scalar_tensor_tensor(out=B["XTf"][:], in0=psK2, scalar=-1.0, in1=B["Fm"][:],
                                                                        op0=ALU.mult, op1=ALU.mult),
                     reads=[rpsK2, RB["Fm"]], writes=[RB["XTf"]])
                yield
                if wout:
                    tt("pool", "Ea", B["Ee"][:], bc_mid(m_i), ALU.mult, [RB["Ee"], R_gc])
                    psA, rpsA = mm4(lambda h: B["kT"][:, h, :], lambda h: B["qT"][:, h, :], [RB["kT"], RB["qT"]])
                    yield
                    tt("dve", "AT", psA, B["Ea"][:], ALU.mult, [rpsA, RB["Ea"]])
                tt("pool", "X0", B["Xf"][:], bc_mid(msk4[:, 4, :]), ALU.mult, [RB["Xf"], R_gc])
                tt("pool", "XT0", B["XTf"][:], bc_mid(msk4[:, 4, :]), ALU.mult, [RB["XTf"], R_gc])
                yield
                tt("dve", "R0", B["X0"][:], bc_mid(ident[:]), ALU.add, [RB["X0"], R_ident])
                tt("dve", "Q0", B["XT0"][:], bc_mid(ident[:]), ALU.add, [RB["XT0"], R_ident])
                yield
                for lvl in range(1, 4):
                    a, b_ = (lvl - 1) % 2, lvl % 2
                    Xa, XTa, Xb, XTb = f"X{a}", f"XT{a}", f"X{b_}", f"XT{b_}"
                    Ra, Rb, Qa, Qb = f"R{a}", f"R{b_}", f"Q{a}", f"Q{b_}"
                    pX, rpX = mm4(lambda h, XTa=XTa: B[XTa][:, h, :], lambda h, Xa=Xa: B[Xa][:, h, :], [RB[Xa], RB[XTa]])
                    pXT, rpXT = mm4(lambda h, Xa=Xa: B[Xa][:, h, :], lambda h, XTa=XTa: B[XTa][:, h, :], [RB[Xa], RB[XTa]])
                    yield
                    copy4(pX, rpX, Xb, "act")
                    copy4(pXT, rpXT, XTb, "dve")
                    yield
                    pR, rpR = mm4(lambda h, XTb=XTb: B[XTb][:, h, :], lambda h, Ra=Ra: B[Ra][:, h, :], [RB[XTb], RB[Ra]])
                    pQ, rpQ = mm4(lambda h, Xb=Xb: B[Xb][:, h, :], lambda h, Qa=Qa: B[Qa][:, h, :], [RB[Xb], RB[Qa]])
                    yield
                    tt("dve", Rb, pR, B[Ra][:], ALU.add, [rpR, RB[Ra]])
                    tt("dve", Qb, pQ, B[Qa][:], ALU.add, [rpQ, RB[Qa]])
                    yield
                curb = 1
                for si in range(3):
                    last = si == 2
                    offm = msk4[:, 5 + si, :]
                    a, b_ = curb, 1 - curb
                    Ra, Rb, Qa, Qb = f"R{a}", f"R{b_}", f"Q{a}", f"Q{b_}"
                    tt("pool", "Cm", B["XTf"][:], bc_mid(offm), ALU.mult, [RB["XTf"], R_gc])
                    if not last:
                        tt("pool", "CmT", B["Xf"][:], bc_mid(offm), ALU.mult, [RB["Xf"], R_gc])
                    yield
                    pW1, rpW1 = mm4(lambda h: B["Cm"][:, h, :], lambda h, Ra=Ra: B[Ra][:, h, :], [RB["Cm"], RB[Ra]])
                    if not last:
                        pW2, rpW2 = mm4(lambda h: B["CmT"][:, h, :], lambda h, Qa=Qa: B[Qa][:, h, :], [RB["CmT"], RB[Qa]])
                    yield
                    copy4(pW1, rpW1, "W1", "act")
                    if not last:
                        copy4(pW2, rpW2, "W2", "dve")
                    yield
                    pY, rpY = mm4(lambda h, Qa=Qa: B[Qa][:, h, :], lambda h: B["W1"][:, h, :], [RB[Qa], RB["W1"]])
                    if not last:
                        pT, rpT = mm4(lambda h, Ra=Ra: B[Ra][:, h, :], lambda h: B["W2"][:, h, :], [RB[Ra], RB["W2"]])
                    yield
                    tt("dve", Rb, pY, B[Ra][:], ALU.add, [rpY, RB[Ra]])
                    if not last:
                        tt("dve", Qb, pT, B[Qa][:], ALU.add, [rpT, RB[Qa]])
                    yield
                    curb = b_
                assert curb == 0
                TT = "R0"
                pU, rpU = mm4(lambda h: B[TT][:, h, :], lambda h: B["vb"][:, h, :], [RB[TT], RB["vb"]])
                pW, rpW = mm4(lambda h: B["kbg"][:, h, :], lambda h: B[TT][:, h, :], [RB[TT], RB["kbg"]])
                yield
                copy4(pU, rpU, "uu", "act")
                copy4(pW, rpW, "wT", "dve")
                yield
                p1, rp1 = mm4(lambda h: B["wT"][:, h, :], lambda h: Sst[:, h, :], [RB["wT"], R_S])
                if wout:
                    p2, rp2 = mm4(lambda h: B["qT"][:, h, :], lambda h: Sst[:, h, :], [RB["qT"], R_S])
                yield
                tt("dve", "vnew", B["uu"][:], p1, ALU.subtract, [RB["uu"], rp1])
                if wout:
                    tt("dve", "o1", p2, bc_last(egc), ALU.mult, [rp2, R_sm2])
                yield
                if wout:
                    p3, rp3 = mm4(lambda h: B["AT"][:, h, :], lambda h: B["vnew"][:, h, :], [RB["AT"], RB["vnew"]])
                p4, rp4 = mm4(lambda h: B["ktl"][:, h, :], lambda h: B["vnew"][:, h, :], [RB["ktl"], RB["vnew"]])
                P.op("pool", lambda e: e.tensor_tensor(out=Sst[:], in0=Sst[:], in1=bc_last(etot), op=ALU.mult),
                     reads=[R_S, R_sm2], writes=[R_S])
                yield
                if wout:
                    tt("dve", "obf", B["o1"][:], p3, ALU.add, [RB["o1"], rp3])
                P.op("dve", lambda e, p4=p4: e.tensor_tensor(out=Sst[:], in0=Sst[:], in1=p4, op=ALU.add),
                     reads=[R_S, rp4], writes=[R_S])
                yield
                if wout:
                    obflat = B["obf"][:].rearrange("p h n -> p (h n)")
                    if R_OB[tau].w is None:
                        P.dma("sp", ds_ob, OB[t0:t0 + 128, :], obflat, reads=[RB["obf"]], writes=[R_OB[tau]])
                    else:
                        P.dma("sp", ds_prev, B["prev"][:].rearrange("p h n -> p (h n)"), OB[t0:t0 + 128, :],
                              reads=[R_OB[tau]], writes=[RB["prev"]])
                        tt("pool", "obf", B["obf"][:], B["prev"][:], ALU.add, [RB["obf"], RB["prev"]])
                        P.dma("sp", ds_ob, OB[t0:t0 + 128, :], obflat, reads=[RB["obf"]], writes=[R_OB[tau]])
                    yield

        run_interleaved([scan_chain(0), scan_chain(1)])
        P.flush()
        ph3b.close()
        ph3.close()

    def phase4_group(E, g):
        tok0 = g * 512
        hx, zt, uTr, Gr = E.hx, E.zt, E.uTr, E.Gr
        for t in range(4):
            P.dma("sp", E.ds_hx[t], hx[:, t, :], X1[tok0 + t * 128:tok0 + (t + 1) * 128, :], writes=[E.R_hx[t]])
        E.modulate_all(hx, E.R_hx, 4, 3, 0)
        plan = [(win_d[24 + i], 1024) for i in range(4)]
        for c in range(8):
            plan += [(wpa_d[c], 512), (wpb_d[c], 512), (win_d[29 + c], 1024), (win_d[37 + c], 1024)]
        plan += [(wo_d[c], 1024) for c in range(8)]
        ws = E.WStream(plan)
        ures = E.R_uT[:4]
        u_rhs = lambda kc: uTr[:, kc, 0:512]
        for i in range(4):
            ps, rps = E.gemm_block(ws, 8, u_rhs, ures, 512)
            tb, rtb, _ = E.tmpB_ring.next()
            P.op("act", lambda e, tb=tb, ps=ps: e.activation(out=tb[:, :], in_=ps[:, :], func=AF.Copy),
                 reads=[rps], writes=[rtb])
            pt, rpt = E.tr_ps()
            for t in range(4):
                P.op("pe", lambda e, pt=pt, tb=tb, t=t: e.transpose(
                    out=pt[:, t * 128:(t + 1) * 128], in_=tb[:, t * 128:(t + 1) * 128], identity=ident[:]),
                    reads=[rtb, R_ident], writes=[rpt])
            P.op("act", lambda e, pt=pt, i=i: e.activation(
                out=E.zs[:, :, i * 128:(i + 1) * 128], in_=pt[:, :].rearrange("p (t c) -> p t c", t=4), func=AF.Silu),
                reads=[rpt], writes=[E.R_zs])
        for t in range(4):
            oat, roat, dsoa = E.oa_ring.next()
            obt, robt, dsob = E.ob_ring.next()
            r0 = tok0 + t * 128
            P.dma("sp", dsoa, oat, OA[r0:r0 + 128, :], writes=[roat])
            P.dma("sp", dsob, obt, OB[r0:r0 + 128, :], writes=[robt])
            obv = obt.rearrange("p (h d) -> p h d", h=4)
            P.op("dve", lambda e, obt=obt: e.tensor_tensor(out=E.sq[:], in0=obt, in1=obt, op=ALU.mult),
                 reads=[robt], writes=[E.R_sq])
            P.op("dve", lambda e: e.reduce_sum(out=E.ss[:, 0:4], in_=E.sq[:].rearrange("p (h d) -> p h d", h=4),
                                               axis=mybir.AxisListType.X), reads=[E.R_sq], writes=[E.R_ss])
            P.op("act", lambda e: e.activation(out=E.ss[:, 4:8], in_=E.ss[:, 0:4], func=AF.Sqrt, scale=1.0 / 128.0,
                                               bias=epsc[:, 1:2]), reads=[E.R_ss, R_ident], writes=[E.R_ss])
            P.op("dve", lambda e: e.reciprocal(out=E.ss[:, 4:8], in_=E.ss[:, 4:8]), reads=[E.R_ss], writes=[E.R_ss])
            P.op("dve", lambda e, obv=obv: e.tensor_tensor(
                out=obv, in0=obv, in1=E.ss[:, 4:8].unsqueeze(2).to_broadcast([128, 4, 128]), op=ALU.mult),
                reads=[robt, E.R_ss], writes=[robt])
            P.op("dve", lambda e, obv=obv: e.tensor_tensor(
                out=obv, in0=obv, in1=E.normw[:].unsqueeze(1).to_broadcast([128, 4, 128]), op=ALU.mult),
                reads=[robt, E.R_normw], writes=[robt])
            P.op("dve", lambda e, obt=obt, t=t: e.tensor_tensor(out=obt, in0=obt, in1=E.zs[:, t, :], op=ALU.mult),
                 reads=[robt, E.R_zs], writes=[robt])
            for (src, rsrc, base, eng) in ((oat, roat, 0, "act"), (obt, robt, 4, "dve")):
                pt, rpt = E.tr_ps()
                for kc in range(4):
                    P.op("pe", lambda e, pt=pt, src=src, kc=kc: e.transpose(
                        out=pt[:, kc * 128:(kc + 1) * 128], in_=src[:, kc * 128:(kc + 1) * 128], identity=ident[:]),
                        reads=[rsrc, R_ident], writes=[rpt])
                dst = Gr[:, base:base + 4, t * 128:(t + 1) * 128]
                src3 = pt[:, :].rearrange("p (k n) -> p k n", k=4)
                wr = [E.R_G[base + kc] for kc in range(4)]
                if eng == "act":
                    P.op("act", lambda e, dst=dst, src3=src3: e.activation(out=dst, in_=src3, func=AF.Copy),
                         reads=[rpt], writes=wr)
                else:
                    P.op("dve", lambda e, dst=dst, src3=src3: e.tensor_copy(out=dst, in_=src3), reads=[rpt], writes=wr)
        for c in range(8):
            psA, rpA = E.gemm_block(ws, 4, lambda kc: Gr[:, kc, 0:512], E.R_G[0:4], 512)
            psB, rpB = E.gemm_block(ws, 4, lambda kc: Gr[:, 4 + kc, 0:512], E.R_G[4:8], 512)
            psGa, rpGa = E.gemm_block(ws, 8, u_rhs, ures, 512)
            psGb, rpGb = E.gemm_block(ws, 8, u_rhs, ures, 512)
            ta1, rta1, _ = E.tmpA_ring.next()
            ta2, rta2, _ = E.tmpA_ring.next()
            P.op("act", lambda e, ta1=ta1, psGa=psGa: e.activation(out=ta1[:, :], in_=psGa[:, :], func=AF.Sigmoid),
                 reads=[rpGa], writes=[rta1])
            P.op("act", lambda e, ta2=ta2, psGb=psGb: e.activation(out=ta2[:, :], in_=psGb[:, :], func=AF.Sigmoid),
                 reads=[rpGb], writes=[rta2])
            P.op("dve", lambda e, ta1=ta1, psA=psA: e.tensor_tensor(out=ta1[:, :], in0=ta1[:, :], in1=psA[:, :], op=ALU.mult),
                 reads=[rta1, rpA], writes=[rta1])
            P.op("dve", lambda e, ta2=ta2, psB=psB: e.tensor_tensor(out=ta2[:, :], in0=ta2[:, :], in1=psB[:, :], op=ALU.mult),
                 reads=[rta2, rpB], writes=[rta2])
            P.op("dve", lambda e, ta1=ta1, ta2=ta2, c=c: e.tensor_tensor(out=Gr[:, 8 + c, :], in0=ta1[:, :], in1=ta2[:, :],
                                                                          op=ALU.add),
                 reads=[rta1, rta2], writes=[E.R_G[8 + c]])
        for c in range(8):
            ps, rps = E.gemm_block(ws, 8, lambda kc: Gr[:, 8 + kc, 0:512], E.R_G[8:16], 512)
            E.back_to_tokens(ps, rps, modT[:, 5 * 8 + c, 0:1], 4, hx, E.R_hx, zt, E.R_zt, c)
        E.run_deferred()
        for t in range(4):
            E.post_ln(zt[:, t, :], E.R_zt[t], 1)
        E.ffn_sublayer(4, 2, 0, wu2_d, wd2_d, 2, zt, E.R_zt, hx, E.R_hx)
        for t in range(4):
            P.dma("sp", E.ds_hx[t], out_d[tok0 + t * 128:tok0 + (t + 1) * 128, :], hx[:, t, :], reads=[E.R_hx[t]])

    if 4 in phases:
        ph4 = ExitStack()
        cur[0] = ph4
        E4 = make_env([1, 2], merge=True)
        E4.zs = sb("m_zs", [128, 4, 512])
        E4.R_zs = Res("zs")
        oa_all = sb("m_oa", [128, 2, 512])
        ob_all = sb("m_ob", [128, 2, 512])
        E4.oa_ring = Ring(P, "m_oa", [oa_all[:, i, :] for i in range(2)])
        E4.ob_ring = Ring(P, "m_ob", [ob_all[:, i, :] for i in range(2)])
        E4.sq = sb("m_sq", [128, 512])
        E4.R_sq = Res("msq")
        E4.ss = sb("m_ss", [128, 8])
        E4.R_ss = Res("mss")
        E4.normw = sb("m_normw", [128, 128])
        E4.R_normw = Res("normw")
        P.dma("sp", ds_misc, E4.normw[:], normw_d[:, :], writes=[E4.R_normw])
        for g in m_groups:
            phase4_group(E4, g)
        P.flush()
        ph4.close()
    P.flush()
    st.close()
    return nc, P


def host_inputs(inputs, b):
    f = np.float32
    g = lambda k: np.asarray(inputs[k], dtype=f)
    m = {}
    m["x"] = np.ascontiguousarray(g("x")[b])
    m["ctx"] = np.ascontiguousarray(g("ctx")[b])
    cT = np.stack([g("c")[b].reshape(8, 128).T, g("c_ctx").reshape(8, 128).T], axis=-1)
    m["cT"] = np.ascontiguousarray(cT)
    m["w_ada"] = np.ascontiguousarray(g("w_ada")[0])
    m["b_adaT"] = np.ascontiguousarray(g("b_ada")[0].reshape(72, 128).T)
    m["lngb"] = np.ascontiguousarray(np.concatenate([g("ln_g")[0], g("ln_b")[0]], axis=0))
    m["ident"] = np.eye(128, dtype=f)
    m["wu1"] = to_blocks(g("ffn1_w_in")[0])
    m["wd1"] = to_blocks(g("ffn1_w_out")[0])
    w_in = g("w_in")[0]
    pad = np.zeros((D, 112), f)
    w_in_p = np.concatenate([w_in[:, :3584], w_in[:, 3584:3600], pad, w_in[:, 3600:]], axis=1)
    m["win"] = to_blocks(w_in_p)
    m["wpa"] = to_blocks(g("w_pa")[0])
    m["wpb"] = to_blocks(g("w_pb")[0])
    m["wo"] = to_blocks(g("w_o")[0])
    m["wu2"] = to_blocks(g("ffn2_w_in")[0])
    m["wd2"] = to_blocks(g("ffn2_w_out")[0])
    if "tabs" not in _NA_CACHE:
        _NA_CACHE["tabs"] = na_tables()
    ridx, cidx, mask = _NA_CACHE["tabs"]
    rpb = g("na_rpb")[0]
    tab = rpb[:, ridx, cidx]
    m["na_tab"] = np.ascontiguousarray(tab.transpose(0, 2, 1, 3))
    m["na_mask"] = np.ascontiguousarray(mask.transpose(1, 0, 2))
    cw = g("gdn_conv_w")[0]
    m["g_convw"] = np.ascontiguousarray(cw.reshape(5, 12, 128).transpose(2, 1, 0))
    if "gconst" not in _NA_CACHE:
        _NA_CACHE["gconst"] = gdn_consts()
    gmask, rope = _NA_CACHE["gconst"]
    m["g_mask"] = gmask
    m["g_rope"] = rope
    adt = np.concatenate([g("gdn_a_log")[0].reshape(8), g("gdn_dt_bias")[0].reshape(8)])
    m["g_adt"] = np.ascontiguousarray(np.broadcast_to(adt[None, :], (128, 16)))
    m["g_normw"] = np.ascontiguousarray(np.broadcast_to(g("gdn_norm_w")[0][None, :], (128, 128)))
    return m


def gdn_consts():
    r = np.arange(128)[:, None]
    c = np.arange(128)[None, :]
    def same(n):
        return (r // n) == (c // n)
    gmask = np.stack([c > r, c >= r, c < r, c <= r, same(16), same(32) & ~same(16), same(64) & ~same(32),
                      ~same(64)], axis=1).astype(np.float32)
    n_freq = 32
    freqs = (10000.0 ** (-np.arange(n_freq, dtype=np.float32) / n_freq)).astype(np.float32)
    t = np.arange(SEQ)
    pos = np.stack([t // GRID_W, t % GRID_W], axis=-1).astype(np.float32)
    ang = (pos[:, :, None] * freqs).astype(np.float32)
    cs = np.stack([np.cos(ang), np.sin(ang)], axis=1).astype(np.float32)
    rope = cs.reshape(32, 128, 128).transpose(1, 0, 2)
    return np.ascontiguousarray(gmask), np.ascontiguousarray(rope)


def kernel(**inputs):
    nc, _ = build_program()
    shared = host_inputs(inputs, 0)
    in_maps = []
    for b in range(8):
        m = dict(shared)
        if b > 0:
            per = host_inputs_core(inputs, b)
            m.update(per)
        in_maps.append(m)
    res = run_bass_kernel_spmd(nc, in_maps, core_ids=list(range(8)))
    return np.stack([np.asarray(r["out"], dtype=np.float32) for r in res.results], axis=0)


def host_inputs_core(inputs, b):
    f = np.float32
    g = lambda k: np.asarray(inputs[k], dtype=f)
    m = {}
    m["x"] = np.ascontiguousarray(g("x")[b])
    m["ctx"] = np.ascontiguousarray(g("ctx")[b])
    cT = np.stack([g("c")[b].reshape(8, 128).T, g("c_ctx").reshape(8, 128).T], axis=-1)
    m["cT"] = np.ascontiguousarray(cT)
    return m
```
